# Optimizing a Trainium2 kernel written in Bass

```python
import math
import jax
import jax.numpy as jnp
from jax import lax
import numpy as np

D_MODEL = 1024
BATCH = 16
SEQ = 2048
DEPTH = 4

GRID_W = 64
CTX_LEN = 256
N_MIXERS = 4
N_MOD = 9
BLOCK_Q = 128
ROPE_BASE = 10000.0
EPS = 1e-6
NEG_INF = -1e30
D_FF = 2816

GQA_HEADS = 16
GQA_KV_HEADS = 4
GQA_HEAD_DIM = 64
GQA_WINDOW = 128

DIFF_HEADS = 8
DIFF_HEAD_DIM = 64

HGRN_HEADS = 8
HGRN_KEY_DIM = 128
HGRN_VAL_DIM = D_MODEL // HGRN_HEADS
HGRN_CHUNK = 64

MLA_HEADS = 16
MLA_Q_LORA = 256
MLA_KV_LORA = 256
MLA_NOPE = 64
MLA_ROPE = 32
MLA_V_DIM = 64

kernel_name = 'hybrid_interleaved_diffusion_block'


def _rmsnorm(x, w):
    xf = x.astype(jnp.float32)
    y = xf * lax.rsqrt(jnp.mean(xf * xf, axis=-1, keepdims=True) + EPS)
    return (y * w.astype(jnp.float32)).astype(x.dtype)


def _pre(h, gain, shift, scale):
    return _rmsnorm(h, gain) * (1.0 + scale) + shift


def _swiglu(h, w_gate, w_up, w_down):
    return (jax.nn.silu(h @ w_gate) * (h @ w_up)) @ w_down


def _axial_rope_tables(n_tokens, rot_dim):
    rows = n_tokens // GRID_W
    row = jnp.repeat(jnp.arange(rows, dtype=jnp.float32), GRID_W)
    col = jnp.tile(jnp.arange(GRID_W, dtype=jnp.float32), rows)
    axis_dim = rot_dim // 2
    inv_freq = ROPE_BASE ** (-jnp.arange(0, axis_dim, 2, dtype=jnp.float32) / axis_dim)
    ang_r = row[:, None] * inv_freq[None, :]
    ang_c = col[:, None] * inv_freq[None, :]
    ang = jnp.concatenate([ang_r, ang_r, ang_c, ang_c], axis=-1)
    return jnp.cos(ang), jnp.sin(ang)


def _apply_rope(x, cos, sin):
    a = x.shape[-1] // 2
    f = a // 2
    rot = jnp.concatenate([-x[..., f:a], x[..., :f], -x[..., a + f:], x[..., a:a + f]], axis=-1)
    out = x.astype(jnp.float32) * cos[None, :, None, :] + rot.astype(jnp.float32) * sin[None, :, None, :]
    return out.astype(x.dtype)


def _to_blocks(a):
    b, s = a.shape[:2]
    return jnp.moveaxis(a.reshape((b, s // BLOCK_Q, BLOCK_Q) + a.shape[2:]), 1, 0)


def _from_blocks(o):
    nb, b, bq = o.shape[:3]
    return jnp.moveaxis(o, 0, 1).reshape((b, nb * bq) + o.shape[3:])


def _gqa_window_mixer(hx, hc, w_in, w_out, sinks, cos, sin, with_ctx_out):
    B, S, _ = hx.shape
    H, KV, HD = GQA_HEADS, GQA_KV_HEADS, GQA_HEAD_DIM
    G = H // KV
    QD, KD = H * HD, KV * HD
    scale = HD ** -0.5
    sink = sinks.astype(jnp.float32).reshape(1, KV, G, 1, 1)

    def proj_q(h):
        return (h @ w_in[:, :QD]).reshape(h.shape[0], h.shape[1], H, HD)

    def proj_kv(h):
        p = h @ w_in[:, QD:]
        return (p[..., :KD].reshape(h.shape[0], h.shape[1], KV, HD),
                p[..., KD:].reshape(h.shape[0], h.shape[1], KV, HD))

    def sink_softmax(s):
        m = jnp.maximum(jnp.max(s, axis=-1, keepdims=True), sink)
        e = jnp.exp(s - m)
        return e / (jnp.sum(e, axis=-1, keepdims=True) + jnp.exp(sink - m))

    kx, vx = proj_kv(hx)
    kc, vc = proj_kv(hc)
    kx = _apply_rope(kx, cos, sin)
    qx = _apply_rope(proj_q(hx), cos, sin).reshape(B, S, KV, G, HD)

    span = BLOCK_Q + 2 * GQA_WINDOW
    pad = ((0, 0), (GQA_WINDOW, GQA_WINDOW), (0, 0), (0, 0))
    kxp, vxp = jnp.pad(kx, pad), jnp.pad(vx, pad)
    k_off = jnp.arange(span) - GQA_WINDOW
    band = jnp.abs(jnp.arange(BLOCK_Q)[:, None] - k_off[None, :]) <= GQA_WINDOW

    def block(args):
        start, qb = args
        kb = lax.dynamic_slice_in_dim(kxp, start, span, axis=1)
        vb = lax.dynamic_slice_in_dim(vxp, start, span, axis=1)
        k_abs = start + k_off
        mask = band & ((k_abs >= 0) & (k_abs < S))[None, :]
        s_loc = jnp.where(mask, jnp.einsum('bqkgd,bskd->bkgqs', qb, kb).astype(jnp.float32) * scale, NEG_INF)
        s_ctx = jnp.einsum('bqkgd,bckd->bkgqc', qb, kc).astype(jnp.float32) * scale
        p = sink_softmax(jnp.concatenate([s_loc, s_ctx], axis=-1)).astype(vb.dtype)
        return (jnp.einsum('bkgqs,bskd->bqkgd', p[..., :span], vb)
                + jnp.einsum('bkgqc,bckd->bqkgd', p[..., span:], vc))

    starts = jnp.arange(S // BLOCK_Q, dtype=jnp.int32) * BLOCK_Q
    ox = _from_blocks(lax.map(block, (starts, _to_blocks(qx))))
    ox = ox.reshape(B, S, QD) @ w_out
    oc = None
    if with_ctx_out:
        bc, nc = hc.shape[:2]
        qc = proj_q(hc).reshape(bc, nc, KV, G, HD)
        s = jnp.einsum('bqkgd,bckd->bkgqc', qc, kc).astype(jnp.float32) * scale
        p = sink_softmax(s).astype(vc.dtype)
        oc = jnp.einsum('bkgqc,bckd->bqkgd', p, vc).reshape(bc, nc, QD) @ w_out
    return ox, oc


def _diff_attn_mixer(hx, hc, w_in, w_out, lam_params, subln_w, lambda_init, cos, sin, with_ctx_out):
    B, S, _ = hx.shape
    H, HD = DIFF_HEADS, DIFF_HEAD_DIM
    QK = H * 2 * HD
    scale = HD ** -0.5
    lp = lam_params.astype(jnp.float32)
    lam = jnp.exp(jnp.sum(lp[0] * lp[1])) - jnp.exp(jnp.sum(lp[2] * lp[3])) + lambda_init

    def proj_q(h):
        return (h @ w_in[:, :QK]).reshape(h.shape[0], h.shape[1], H, 2, HD)

    def proj_kv(h):
        p = h @ w_in[:, QK:]
        return (p[..., :QK].reshape(h.shape[0], h.shape[1], H, 2, HD),
                p[..., QK:].reshape(h.shape[0], h.shape[1], H, 2 * HD))

    def rope(t):
        return _apply_rope(t.reshape(B, S, H * 2, HD), cos, sin).reshape(B, S, H, 2, HD)

    def diff_probs(s):
        p = jax.nn.softmax(s * scale, axis=-1)
        return p[:, :, 0] - lam * p[:, :, 1]

    def head_out(o):
        o = _rmsnorm(o, subln_w) * (1.0 - lambda_init)
        return o.reshape(o.shape[0], o.shape[1], H * 2 * HD) @ w_out

    kx, vx = proj_kv(hx)
    kc, vc = proj_kv(hc)
    kx = rope(kx)
    qx = rope(proj_q(hx))

    def block(qb):
        s_lat = jnp.einsum('bqhjd,bkhjd->bhjqk', qb, kx)
        s_ctx = jnp.einsum('bqhjd,bchjd->bhjqc', qb, kc)
        p = diff_probs(jnp.concatenate([s_lat, s_ctx], axis=-1).astype(jnp.float32)).astype(vx.dtype)
        return (jnp.einsum('bhqk,bkhe->bqhe', p[..., :S], vx)
                + jnp.einsum('bhqc,bche->bqhe', p[..., S:], vc))

    ox = head_out(_from_blocks(lax.map(block, _to_blocks(qx))))
    oc = None
    if with_ctx_out:
        qc = proj_q(hc)
        p = diff_probs(jnp.einsum('bqhjd,bchjd->bhjqc', qc, kc).astype(jnp.float32)).astype(vc.dtype)
        oc = head_out(jnp.einsum('bhqc,bche->bqhe', p, vc))
    return ox, oc


def _hgrn2_chunk_scan(q, k, v, logf, s0):
    B, T, H, DK = q.shape
    DV = v.shape[-1]
    C = HGRN_CHUNK
    N = T // C

    def chunks(a):
        return a.reshape(B, N, C, H, a.shape[-1]).transpose(1, 0, 3, 2, 4)

    q, k, v, logf = chunks(q), chunks(k), chunks(v), chunks(logf)
    g = jnp.cumsum(logf, axis=3)
    g_mid = g[:, :, :, C // 2:C // 2 + 1, :]
    g_last = g[:, :, :, C - 1:, :]
    a = jnp.einsum('nbhtd,nbhsd->nbhts', q * jnp.exp(g - g_mid), k * jnp.exp(g_mid - g))
    lower = jnp.arange(C)[:, None] >= jnp.arange(C)[None, :]
    a = jnp.where(lower, a, 0.0)
    o_intra = jnp.einsum('nbhts,nbhse->nbhte', a, v)

    def step(state, inp):
        q_in, k_out, v_c, decay = inp
        o = jnp.einsum('bhtd,bhde->bhte', q_in, state)
        state = state * decay[..., None] + jnp.einsum('bhsd,bhse->bhde', k_out, v_c)
        return state, o

    s_fin, o_inter = lax.scan(step, s0, (q * jnp.exp(g), k * jnp.exp(g_last - g), v, jnp.exp(g_last[:, :, :, 0, :])))
    o = (o_intra + o_inter).transpose(1, 0, 3, 2, 4).reshape(B, T, H, DV)
    return o, s_fin


def _hgrn2_mixer(hx, hc, w_in, w_out, norm_w, lb, with_ctx_out):
    H, DK, DV = HGRN_HEADS, HGRN_KEY_DIM, HGRN_VAL_DIM
    KD, VD = H * DK, H * DV

    def proj(h):
        b, t = h.shape[:2]
        p = (h @ w_in).astype(jnp.float32)
        q = jax.nn.silu(p[..., :KD]).reshape(b, t, H, DK)

        def forget(z):
            f = lb + (1.0 - lb) * jax.nn.sigmoid(z)
            return jnp.log(f).reshape(b, t, H, DK), (1.0 - f).reshape(b, t, H, DK)

        fwd = forget(p[..., KD:2 * KD])
        bwd = forget(p[..., 2 * KD:3 * KD])
        v = p[..., 3 * KD:3 * KD + VD].reshape(b, t, H, DV)
        gate = p[..., 3 * KD + VD:]
        return q, fwd, bwd, v, gate

    def flip(a):
        return jnp.flip(a, axis=1)

    def readout(o, gate, dtype):
        b, t = o.shape[:2]
        y = _rmsnorm(o, norm_w).reshape(b, t, VD) * jax.nn.silu(gate)
        return y.astype(dtype) @ w_out

    qc, (lfc, kfc), (lbc, kbc), vc, gc = proj(hc)
    qx, (lfx, kfx), (lbx, kbx), vx, gx = proj(hx)
    zero = jnp.zeros((hx.shape[0], H, DK, DV), jnp.float32)
    oc_f, sc_f = _hgrn2_chunk_scan(qc, kfc, vc, lfc, zero)
    oc_b, sc_b = _hgrn2_chunk_scan(flip(qc), flip(kbc), flip(vc), flip(lbc), zero)
    ox_f, _ = _hgrn2_chunk_scan(qx, kfx, vx, lfx, sc_f)
    ox_b, _ = _hgrn2_chunk_scan(flip(qx), flip(kbx), flip(vx), flip(lbx), sc_b)
    ox = readout(ox_f + flip(ox_b), gx, hx.dtype)
    oc = readout(oc_f + flip(oc_b), gc, hc.dtype) if with_ctx_out else None
    return ox, oc


def _mla_mixer(hx, hc, w_down, q_norm_w, kv_norm_w, w_uq, w_ukv, w_out, cos, sin, with_ctx_out):
    B, S, _ = hx.shape
    H, NOPE, R, VD = MLA_HEADS, MLA_NOPE, MLA_ROPE, MLA_V_DIM
    scale = (NOPE + R) ** -0.5

    def proj_q(h, rotate):
        b, t = h.shape[:2]
        cq = _rmsnorm(h @ w_down[:, :MLA_Q_LORA], q_norm_w)
        q = (cq @ w_uq).reshape(b, t, H, NOPE + R)
        qn, qr = q[..., :NOPE], q[..., NOPE:]
        if rotate:
            qr = _apply_rope(qr, cos, sin)
        return qn, qr

    def proj_kv(h, rotate):
        b, t = h.shape[:2]
        p = h @ w_down[:, MLA_Q_LORA:]
        ckv = _rmsnorm(p[..., :MLA_KV_LORA], kv_norm_w)
        kr = p[..., MLA_KV_LORA:]
        if rotate:
            kr = _apply_rope(kr[:, :, None, :], cos, sin)[:, :, 0, :]
        kv = (ckv @ w_ukv).reshape(b, t, H, NOPE + VD)
        return kv[..., :NOPE], kr, kv[..., NOPE:]

    def logits(qn, qr, kn, kr):
        s = jnp.einsum('bqhd,bkhd->bhqk', qn, kn) + jnp.einsum('bqhr,bkr->bhqk', qr, kr)
        return s.astype(jnp.float32) * scale

    knx, krx, vx = proj_kv(hx, True)
    knc, krc, vc = proj_kv(hc, False)
    qnx, qrx = proj_q(hx, True)

    def block(args):
        qn, qr = args
        s = jnp.concatenate([logits(qn, qr, knx, krx), logits(qn, qr, knc, krc)], axis=-1)
        p = jax.nn.softmax(s, axis=-1).astype(vx.dtype)
        return (jnp.einsum('bhqk,bkhe->bqhe', p[..., :S], vx)
                + jnp.einsum('bhqc,bche->bqhe', p[..., S:], vc))

    ox = _from_blocks(lax.map(block, (_to_blocks(qnx), _to_blocks(qrx))))
    ox = ox.reshape(B, S, H * VD) @ w_out
    oc = None
    if with_ctx_out:
        qnc, qrc = proj_q(hc, False)
        p = jax.nn.softmax(logits(qnc, qrc, knc, krc), axis=-1).astype(vc.dtype)
        oc = jnp.einsum('bhqc,bche->bqhe', p, vc)
        oc = oc.reshape(oc.shape[0], oc.shape[1], H * VD) @ w_out
    return ox, oc


def _layers_of(kind):
    return len(range(kind, DEPTH, N_MIXERS))


def setup_inputs(seed: int = 0) -> dict:
    key = jax.random.key(seed)
    keys = iter(jax.random.split(key, 40))

    def nrm(shape, std):
        return jax.random.normal(next(keys), shape, jnp.float32) * std

    D = D_MODEL
    n_gqa, n_diff, n_hgrn, n_mla = (_layers_of(k) for k in range(N_MIXERS))
    gqa_in = (GQA_HEADS + 2 * GQA_KV_HEADS) * GQA_HEAD_DIM
    gqa_o = GQA_HEADS * GQA_HEAD_DIM
    diff_o = DIFF_HEADS * 2 * DIFF_HEAD_DIM
    hk, hv = HGRN_HEADS * HGRN_KEY_DIM, HGRN_HEADS * HGRN_VAL_DIM
    mla_down = MLA_Q_LORA + MLA_KV_LORA + MLA_ROPE
    mla_o = MLA_HEADS * MLA_V_DIM
    return {
        'x': nrm((BATCH, SEQ, D), 1.0),
        'c': nrm((BATCH, D), 1.0),
        'ctx': nrm((BATCH, CTX_LEN, D), 1.0),
        'c_ctx': nrm((D,), 1.0),
        'ada_w': nrm((DEPTH, D, N_MOD * D), 0.5 * D ** -0.5),
        'ada_b': nrm((DEPTH, N_MOD * D), 0.02),
        'norm_w': 1.0 + nrm((DEPTH, 3, D), 0.05),
        'final_norm_w': 1.0 + nrm((D,), 0.05),
        'ffn_w_gate': nrm((DEPTH, 2, D, D_FF), D ** -0.5),
        'ffn_w_up': nrm((DEPTH, 2, D, D_FF), D ** -0.5),
        'ffn_w_down': nrm((DEPTH, 2, D_FF, D), D_FF ** -0.5),
        'gqa_w_in': nrm((n_gqa, D, gqa_in), D ** -0.5),
        'gqa_w_out': nrm((n_gqa, gqa_o, D), gqa_o ** -0.5),
        'gqa_sinks': nrm((n_gqa, GQA_HEADS), 0.5),
        'diff_w_in': nrm((n_diff, D, 3 * diff_o), D ** -0.5),
        'diff_w_out': nrm((n_diff, diff_o, D), diff_o ** -0.5),
        'diff_lambda': nrm((n_diff, 4, DIFF_HEAD_DIM), 0.1),
        'diff_subln_w': 1.0 + nrm((n_diff, 2 * DIFF_HEAD_DIM), 0.05),
        'hgrn_w_in': nrm((n_hgrn, D, 3 * hk + 2 * hv), D ** -0.5),
        'hgrn_w_out': nrm((n_hgrn, hv, D), hv ** -0.5),
        'hgrn_norm_w': 1.0 + nrm((n_hgrn, HGRN_VAL_DIM), 0.05),
        'hgrn_lower_bounds': nrm((DEPTH, hk), 0.5),
        'mla_w_down': nrm((n_mla, D, mla_down), D ** -0.5),
        'mla_q_norm_w': 1.0 + nrm((n_mla, MLA_Q_LORA), 0.05),
        'mla_kv_norm_w': 1.0 + nrm((n_mla, MLA_KV_LORA), 0.05),
        'mla_w_uq': nrm((n_mla, MLA_Q_LORA, MLA_HEADS * (MLA_NOPE + MLA_ROPE)), MLA_Q_LORA ** -0.5),
        'mla_w_ukv': nrm((n_mla, MLA_KV_LORA, MLA_HEADS * (MLA_NOPE + MLA_V_DIM)), MLA_KV_LORA ** -0.5),
        'mla_w_out': nrm((n_mla, mla_o, D), mla_o ** -0.5),
    }


def reference(x, c, ctx, c_ctx, ada_w, ada_b, norm_w, final_norm_w, ffn_w_gate, ffn_w_up, ffn_w_down,
              gqa_w_in, gqa_w_out, gqa_sinks, diff_w_in, diff_w_out, diff_lambda, diff_subln_w,
              hgrn_w_in, hgrn_w_out, hgrn_norm_w, hgrn_lower_bounds,
              mla_w_down, mla_q_norm_w, mla_kv_norm_w, mla_w_uq, mla_w_ukv, mla_w_out):
    B, S, D = x.shape
    cos_h, sin_h = _axial_rope_tables(S, GQA_HEAD_DIM)
    cos_m, sin_m = _axial_rope_tables(S, MLA_ROPE)
    lb_soft = jax.nn.softmax(hgrn_lower_bounds.astype(jnp.float32), axis=0)
    lb_all = jnp.cumsum(lb_soft, axis=0) - lb_soft[0:1]
    silu_c = jax.nn.silu(c)
    silu_cc = jax.nn.silu(c_ctx)
    hx, hc = x, ctx
    for i in range(DEPTH):
        kind, j = i % N_MIXERS, i // N_MIXERS
        with_ctx_out = i < DEPTH - 1
        mx = (silu_c @ ada_w[i] + ada_b[i]).reshape(B, N_MOD, 1, D)
        mc = (silu_cc @ ada_w[i] + ada_b[i]).reshape(N_MOD, D)
        fx = _swiglu(_pre(hx, norm_w[i, 0], mx[:, 0], mx[:, 1]), ffn_w_gate[i, 0], ffn_w_up[i, 0], ffn_w_down[i, 0])
        fc = _swiglu(_pre(hc, norm_w[i, 0], mc[0], mc[1]), ffn_w_gate[i, 0], ffn_w_up[i, 0], ffn_w_down[i, 0])
        hx = hx + 0.5 * mx[:, 2] * fx
        hc = hc + 0.5 * mc[2] * fc
        ax = _pre(hx, norm_w[i, 1], mx[:, 3], mx[:, 4])
        ac = _pre(hc, norm_w[i, 1], mc[3], mc[4])
        if kind == 0:
            ox, oc = _gqa_window_mixer(ax, ac, gqa_w_in[j], gqa_w_out[j], gqa_sinks[j], cos_h, sin_h, with_ctx_out)
        elif kind == 1:
            lambda_init = 0.8 - 0.6 * math.exp(-0.3 * i)
            ox, oc = _diff_attn_mixer(ax, ac, diff_w_in[j], diff_w_out[j], diff_lambda[j], diff_subln_w[j],
                                      lambda_init, cos_h, sin_h, with_ctx_out)
        elif kind == 2:
            ox, oc = _hgrn2_mixer(ax, ac, hgrn_w_in[j], hgrn_w_out[j], hgrn_norm_w[j], lb_all[i], with_ctx_out)
        else:
            ox, oc = _mla_mixer(ax, ac, mla_w_down[j], mla_q_norm_w[j], mla_kv_norm_w[j], mla_w_uq[j],
                                mla_w_ukv[j], mla_w_out[j], cos_m, sin_m, with_ctx_out)
        hx = hx + mx[:, 5] * ox
        fx = _swiglu(_pre(hx, norm_w[i, 2], mx[:, 6], mx[:, 7]), ffn_w_gate[i, 1], ffn_w_up[i, 1], ffn_w_down[i, 1])
        hx = hx + 0.5 * mx[:, 8] * fx
        if with_ctx_out:
            hc = hc + mc[5] * oc
            fc = _swiglu(_pre(hc, norm_w[i, 2], mc[6], mc[7]), ffn_w_gate[i, 1], ffn_w_up[i, 1], ffn_w_down[i, 1])
            hc = hc + 0.5 * mc[8] * fc
    return _rmsnorm(hx, final_norm_w)
```

```python
import math
import contextlib
import numpy as np
import concourse.bass as bass
import concourse.mybir as mybir
from concourse.bass_utils import run_bass_kernel_spmd

F32 = mybir.dt.float32
BF16 = mybir.dt.bfloat16
AF = mybir.ActivationFunctionType
ALU = mybir.AluOpType
AX = mybir.AxisListType

ENGS = ("pe", "act", "dve", "pool", "sp")

D = 1024
KC = 8
S = 2048
CT = 256
T = CT + S
DFF = 2816
NL = 4
EPS = 1e-6
NCORES = 8
TT = [(0, 256), (256, 512), (768, 512), (1280, 512), (1792, 512)]


class Ev:
    __slots__ = ("eng", "need_inc", "semval", "dma_sem", "dma_val", "idx")

    def __init__(self, eng):
        self.eng = eng
        self.idx = -1
        self.need_inc = False
        self.semval = None
        self.dma_sem = None
        self.dma_val = None


class Rec:
    __slots__ = ("fn", "deps", "ev", "is_dma")

    def __init__(self, fn, deps, ev, is_dma=False):
        self.fn = fn
        self.deps = deps
        self.ev = ev
        self.is_dma = is_dma


class Prog:
    def __init__(self, nc, n_dma_sems=40):
        self.nc = nc
        self.streams = {e: [] for e in ENGS}
        self.last_w = {}
        self.readers = {}
        self.last_ev = {e: None for e in ENGS}
        self.n_dma_sems = n_dma_sems
        self.dma_cnt = 0
        self.dma_last = [None] * n_dma_sems
        self.dma_vals = [0] * n_dma_sems
        self.n_ops = 0

    def _collect(self, eng, R, W, is_dma):
        best = {}

        def add(e):
            if e is None:
                return
            k = ("d", e.dma_sem) if e.dma_sem is not None else ("e", e.eng)
            o = best.get(k)
            if o is None or (e.dma_val if e.dma_sem is not None else e.idx) > (o.dma_val if o.dma_sem is not None else o.idx):
                best[k] = e

        for k in R:
            add(self.last_w.get(k))
        for k in W:
            w = self.last_w.get(k)
            if w is not None and (w.eng != eng or w.dma_sem is not None or is_dma):
                add(w)
            for r in self.readers.get(k, ()):
                if r.eng != eng or r.dma_sem is not None or is_dma:
                    add(r)
        return list(best.values())

    def _update(self, ev, R, W):
        for k in R:
            self.readers.setdefault(k, []).append(ev)
        for k in W:
            self.last_w[k] = ev
            self.readers[k] = []

    def op(self, eng, fn, R=(), W=()):
        ev = Ev(eng)
        deps = self._collect(eng, R, W, False)
        for d in deps:
            if d.dma_sem is None:
                d.need_inc = True
        ev.idx = len(self.streams[eng])
        self.streams[eng].append(Rec(fn, deps, ev))
        self._update(ev, R, W)
        self.last_ev[eng] = ev
        self.n_ops += 1
        return ev

    def dma(self, eng, out, in_, R=(), W=(), **kw):
        ev = Ev(eng)
        slot = self.dma_cnt % self.n_dma_sems
        self.dma_cnt += 1
        deps = self._collect(eng, R, W, True)
        prev = self.dma_last[slot]
        if prev is not None:
            deps.append(prev)
        for d in deps:
            if d.dma_sem is None:
                d.need_inc = True
        self.dma_vals[slot] += 16
        ev.dma_sem = slot
        ev.dma_val = self.dma_vals[slot]
        self.dma_last[slot] = ev
        ev.idx = len(self.streams[eng])
        self.streams[eng].append(
            Rec(lambda e, o=out, i=in_, k=kw: e.dma_start(out=o, in_=i, **k), deps, ev, True))
        self._update(ev, R, W)
        self.n_ops += 1
        return ev

    def barrier(self, wait_dma=False):
        lasts = [self.last_ev[e] for e in ENGS if self.last_ev[e] is not None]
        for l in lasts:
            l.need_inc = True
        dmas = [d for d in self.dma_last if d is not None] if wait_dma else []
        for e in ENGS:
            deps = [l for l in lasts if l.eng != e] + dmas
            if deps:
                self.streams[e].append(Rec(None, deps, None))
        if wait_dma:
            self.last_w = {}
            self.readers = {}
        else:
            self.last_w = {k: v for k, v in self.last_w.items() if v.dma_sem is not None}
            rd = {k: [r for r in v if r.dma_sem is not None] for k, v in self.readers.items()}
            self.readers = {k: v for k, v in rd.items() if v}

    def wait_all_dma(self, eng="sp"):
        deps = [d for d in self.dma_last if d is not None]
        self.streams[eng].append(Rec(None, deps, None))

    def emit(self):
        nc = self.nc
        with contextlib.ExitStack() as st:
            sems = {e: st.enter_context(nc.semaphore("s_" + e)) for e in ENGS}
            dsems = [st.enter_context(nc.semaphore("d%d" % i)) for i in range(self.n_dma_sems)]
            for e in ENGS:
                c = 0
                for r in self.streams[e]:
                    if r.ev is not None and r.ev.dma_sem is None and r.ev.need_inc:
                        c += 1
                        r.ev.semval = c
            block = st.enter_context(nc.Block())

            def run(engname, engobj):
                waited = {}
                for r in self.streams[engname]:
                    for d in r.deps:
                        if d.dma_sem is not None:
                            key, sem, val = ("d", d.dma_sem), dsems[d.dma_sem], d.dma_val
                        else:
                            key, sem, val = ("e", d.eng), sems[d.eng], d.semval
                        if waited.get(key, 0) >= val:
                            continue
                        waited[key] = val
                        engobj.wait_ge(sem, val)
                    if r.fn is None:
                        continue
                    ins = r.fn(engobj)
                    if r.is_dma:
                        ins.then_inc(dsems[r.ev.dma_sem], 16)
                    elif r.ev.need_inc:
                        ins.then_inc(sems[engname], 1)

            @block.tensor
            def _(eng):
                run("pe", eng)

            @block.scalar
            def _(eng):
                run("act", eng)

            @block.vector
            def _(eng):
                run("dve", eng)

            @block.gpsimd
            def _(eng):
                run("pool", eng)

            @block.sync
            def _(eng):
                run("sp", eng)


class Mem:
    def __init__(self, nc, total_bytes):
        self.n = total_bytes // 4
        self.t = nc.alloc_sbuf_tensor("sbuf_all", [128, self.n], F32)
        self.off = 0
        self.hi = 0

    def alloc(self, shape, dt):
        n = 1
        for s in shape:
            n *= s
        nb = n * (4 if dt == F32 else 2)
        nf = ((nb + 31) // 32) * 8
        assert self.off + nf <= self.n, ("SBUF overflow", self.off * 4, nf * 4, self.n * 4)
        a = self.t[:, self.off:self.off + nf]
        self.off += nf
        self.hi = max(self.hi, self.off)
        if dt != F32:
            a = a.bitcast(dt)
        a = a[:, 0:n]
        if len(shape) == 2:
            a = a.rearrange("p (a b) -> p a b", a=shape[0])
        elif len(shape) == 3:
            a = a.rearrange("p (a b c) -> p a b c", a=shape[0], b=shape[1])
        return a

    def mark(self):
        return self.off

    def reset(self, m):
        self.off = m


class Rot:
    def __init__(self, mem, name, n, shape, dt):
        self.bufs = [mem.alloc(shape, dt) for _ in range(n)]
        self.name = name
        self.i = 0

    def next(self):
        j = self.i % len(self.bufs)
        self.i += 1
        return self.bufs[j], (self.name, j)


class Builder:
    def __init__(self, debug=None, layers=(0, 1, 2, 3), n_seq=2, stop_after=None):
        self.debug = debug or []
        self.layers = list(layers)
        self.n_seq = n_seq
        self.stop_after = stop_after
        self.cut = 99
        nc = self.nc = bass.Bass("TRN2", target_bir_lowering=False)
        self.P = Prog(nc)
        self.dram = {}
        self.dbg_out = {}

    def din(self, name, shape, dt=F32):
        t = self.nc.dram_tensor(name, list(shape), dt, kind="ExternalInput").ap()
        self.dram[name] = t
        return t

    def dout(self, name, shape, dt=F32):
        t = self.nc.dram_tensor(name, list(shape), dt, kind="ExternalOutput").ap()
        self.dram[name] = t
        return t

    def ps_s(self):
        i = self._ps_s % 4
        self._ps_s += 1
        return self.psb[i], ("ps", i)

    def ps_s2(self):
        if self._ps_s % 2 == 1:
            self._ps_s += 1
        i = self._ps_s % 4
        self._ps_s += 2
        return self.psd[i // 2], (("ps", i), ("ps", i + 1))

    def ps_a(self):
        i = 4 + self._ps_a % 4
        self._ps_a += 1
        return self.psb[i], ("ps", i)

    def wload(self, view, shape):
        a, b = shape
        assert a * b <= 2048, (a, b)
        s = self._wslot % len(self.wring)
        self._wslot += 1
        dst = self.wring[s][:, 0:a * b].rearrange("p (a b) -> p a b", a=a)
        key = ("w", s)
        self.P.dma("pool", dst, view, W=[key])
        return dst, key

    def mmk(self, out, pairs, R, pk):
        n = len(pairs)
        for i, (lt, rh) in enumerate(pairs):
            self.P.op("pe", lambda e, out=out, lt=lt, rh=rh, i=i, n=n: e.matmul(out, lt, rh, start=(i == 0), stop=(i == n - 1)),
                      R=R, W=[pk])

    def dump(self, name, ap, n, R):
        if name not in self.debug:
            return
        o = self.dout("dbg_" + name, [128, n])
        self.P.dma("sp", o, ap, R=R)

    def build(self):
        nc, P = self.nc, self.P
        hT = self.din("hT", [2, D, T])
        cT = self.din("cT", [128, KC, 3])
        ada_w = self.din("ada_w", [NL, D, 9 * D])
        ada_bT = self.din("ada_bT", [128, NL, 72])
        norm_wT = self.din("norm_wT", [128, NL * 3, KC])
        fnorm_wT = self.din("fnorm_wT", [128, KC])
        ffn_wg = self.din("ffn_wg_l", [NL, 2, 11, 128, KC * 256])
        ffn_wu = self.din("ffn_wu_l", [NL, 2, 11, 128, KC * 256])
        ffn_wd = self.din("ffn_w_down", [NL, 2, DFF, D])
        outT = self.dout("outT", [2, D, S])
        self.declare_mixer_inputs()

        mem = self.mem = Mem(nc, 206 * 1024)
        self.psd = [nc.alloc_psum_tensor("psd%d" % i, [128, 1024], F32) for i in range(2)]
        self.psb = [self.psd[i // 2][:, (i % 2) * 512:(i % 2 + 1) * 512] for i in range(4)]
        self.psb += [nc.alloc_psum_tensor("ps%d" % i, [128, 512], F32) for i in range(4, 8)]
        self._ps_s = 0
        self._ps_a = 0
        self._wslot = 0
        hx = self.hx = mem.alloc([KC, T], F32)
        self.wring = [mem.alloc([2048], BF16) for _ in range(8)]
        self.mod = mem.alloc([NL, 72, 3], F32)
        self.Amod = mem.alloc([NL * 3, KC, 3], F32)
        self.Gmod = mem.alloc([NL * 3, KC, 3], F32)
        self.normw = mem.alloc([NL * 3, KC], F32)
        self.fnormw = mem.alloc([KC], F32)
        self.ones_bf = mem.alloc([128], BF16)
        self.ostage = Rot(mem, "ostage", 2, [512], F32)
        self.alloc_mixer_consts()
        self.arena0 = mem.mark()

        P.op("dve", lambda e: e.memset(self.ones_bf, 1.0), W=["ones"])
        P.dma("sp", self.normw, norm_wT, W=["normw"])
        P.dma("sp", self.fnormw, fnorm_wT, W=["fnormw"])
        self.load_mixer_consts()

        self.hT = hT
        for tt in range(5):
            self.load_hx_tile(0, tt)
        self.prologue_mod(cT, ada_w, ada_bT)

        for s in range(self.n_seq):
            for l in self.layers:
                last = (l == NL - 1)
                self.ffn(s, l, 0, [0, 1, 2, 3, 4], ffn_wg, ffn_wu, ffn_wd)
                if self.stop_after == ("ffn0", l):
                    break
                self.mixer(s, l)
                if self.stop_after == ("mixer", l):
                    break
                self.ffn(s, l, 1, [1, 2, 3, 4] if last else [0, 1, 2, 3, 4], ffn_wg, ffn_wu, ffn_wd)
            if "hx" in self.debug and s == 0:
                o = self.dout("dbg_hx", [128, KC, T])
                P.dma("sp", o, hx, R=[("hx", tt) for tt in range(5)])
            self.final_out(s, outT, s + 1 if s + 1 < self.n_seq else None)
        P.wait_all_dma("sp")
        P.emit()
        return nc

    def prologue_mod(self, cT, ada_w, ada_bT):
        P, mem = self.P, self.mem
        m0 = mem.mark()
        sc = mem.alloc([KC, 3], F32)
        bT = mem.alloc([NL, 72], F32)
        stage = Rot(mem, "adastage", 3, [KC, 512], BF16)
        scb = mem.alloc([KC, 3], BF16)
        P.dma("sp", sc, cT, W=["sc"])
        P.dma("sp", bT, ada_bT, W=["bT"])
        P.op("act", lambda e: e.activation(scb, sc, AF.Silu), R=["sc"], W=["sc"])
        for l in self.layers:
            ps, pk = self.ps_a()
            psv = ps[:, 0:216].rearrange("p (a b) -> p a b", b=3)
            for j2 in range(18):
                st, sk = stage.next()
                P.dma("pool", st, ada_w[l, :, j2 * 512:(j2 + 1) * 512].rearrange("(k p) n -> p k n", p=128), W=[sk])
                for c in range(4):
                    for kc in range(8):
                        P.op("pe", lambda e, st=st, c=c, kc=kc, j2=j2, psv=psv: e.matmul(
                            psv[:, j2 * 4 + c, :], st[:, kc, c * 128:(c + 1) * 128], scb[:, kc, :],
                            start=(kc == 0), stop=(kc == 7)), R=[sk, "sc"], W=[pk])
            P.op("dve", lambda e, l=l, psv=psv: e.tensor_tensor(
                self.mod[:, l], psv, bT[:, l].unsqueeze(2).broadcast_to([128, 72, 3]), ALU.add),
                R=[pk, "bT"], W=["mod"])
            for n in range(3):
                ln = l * 3 + n
                P.op("dve", lambda e, l=l, n=n, ln=ln: e.scalar_tensor_tensor(
                    self.Amod[:, ln], self.mod[:, l, (3 * n + 1) * 8:(3 * n + 2) * 8, :], 1.0,
                    self.normw[:, ln].unsqueeze(2).broadcast_to([128, KC, 3]), ALU.add, ALU.mult),
                    R=["mod", "normw"], W=["Amod"])
                P.op("dve", lambda e, l=l, n=n, ln=ln: e.tensor_scalar(
                    self.Gmod[:, ln], self.mod[:, l, (3 * n + 2) * 8:(3 * n + 3) * 8, :],
                    (1.0 if n == 1 else 0.5), None, ALU.mult), R=["mod"], W=["Gmod"])
        self.prologue_mixer()
        P.barrier(wait_dma=True)
        mem.reset(m0)

    def shift_ap(self, l, n, c, col):
        return self.mod[:, l, (3 * n) * 8 + c, col:col + 1]

    def prenorm(self, s, l, n, tiles, h, rots):
        P = self.P
        sqr, sdr, tmpr = rots
        hx = self.hx
        ln = l * 3 + n
        for tt in tiles:
            t0, N = TT[tt]
            col = 2 if tt == 0 else s
            sq, sqk = sqr.next()
            P.op("act", lambda e, sq=sq, t0=t0, N=N: e.activation(sq[:, :, 0:N], hx[:, :, t0:t0 + N], AF.Square),
                 R=[("hx", tt)], W=[sqk])
            ps, pk = self.ps_s()
            for kc in range(KC):
                P.op("pe", lambda e, ps=ps, sq=sq, kc=kc, N=N: e.matmul(
                    ps[:, 0:N], self.ones_bf, sq[:, kc, 0:N], start=(kc == 0), stop=(kc == KC - 1)),
                    R=[sqk, "ones"], W=[pk])
            sd, sdk = sdr.next()
            P.op("act", lambda e, sd=sd, ps=ps, N=N: e.activation(sd[:, 0:N], ps[:, 0:N], AF.Ln, bias=self.eps_ap, scale=1.0 / D),
                 R=[pk, "eps"], W=[sdk])
            P.op("act", lambda e, sd=sd, N=N: e.activation(sd[:, 0:N], sd[:, 0:N], AF.Exp, scale=-0.5), R=[sdk], W=[sdk])
            for c in range(KC):
                tmp, tk = tmpr.next()
                P.op("dve", lambda e, tmp=tmp, c=c, t0=t0, N=N, sd=sd: e.tensor_tensor(
                    tmp[:, 0:N], hx[:, c, t0:t0 + N], sd[:, 0:N], ALU.mult), R=[("hx", tt), sdk], W=[tk])
                P.op("act", lambda e, tmp=tmp, c=c, t0=t0, N=N, col=col: e.activation(
                    h[:, c, t0:t0 + N], tmp[:, 0:N], AF.Identity,
                    bias=self.shift_ap(l, n, c, col), scale=self.Amod[:, ln, c, col:col + 1]),
                    R=[tk], W=[("h", tt)])

    def ffn(self, s, l, which, tiles, ffn_wg, ffn_wu, ffn_wd):
        P, mem = self.P, self.mem
        hx = self.hx
        n = 0 if which == 0 else 2
        ln = l * 3 + n
        P.barrier()
        m0 = mem.mark()
        h = mem.alloc([KC, T], BF16)
        rots = (Rot(mem, "sq", 2, [KC, 512], BF16), Rot(mem, "sd", 2, [512], F32), Rot(mem, "ntmp", 3, [512], F32))
        sgr = Rot(mem, "sg", 3, [512], BF16)
        actr = Rot(mem, "act", 3, [2, 512], BF16)
        groups = [(f0, 2) for f0 in range(0, 22, 2)]

        def down(tt, act, actk, wd, wdk, nf):
            t0, N = TT[tt]
            col = 2 if tt == 0 else s
            for dc in range(KC):
                pd, pdk = self.ps_a()
                self.mmk(pd[:, 0:N], [(wd[:, fc, dc * 128:(dc + 1) * 128], act[:, fc, 0:N]) for fc in range(nf)], [wdk, actk], pdk)
                P.op("dve", lambda e, pd=pd, dc=dc: e.scalar_tensor_tensor(
                    hx[:, dc, t0:t0 + N], pd[:, 0:N], self.Gmod[:, ln, dc, col:col + 1], hx[:, dc, t0:t0 + N],
                    ALU.mult, ALU.add), R=[pdk, ("hx", tt)], W=[("hx", tt)])

        pend_d = None
        for gi, (f0, nf) in enumerate(groups):
            wg, wgk = self.wload(ffn_wg[l, which, gi].rearrange("p (k n) -> p k n", k=KC), (KC, 256))
            wu, wuk = self.wload(ffn_wu[l, which, gi].rearrange("p (k n) -> p k n", k=KC), (KC, 256))
            wd, wdk = self.wload(ffn_wd[l, which, f0 * 128:(f0 + nf) * 128, :].rearrange("(k p) n -> p k n", p=128), (nf, D))
            for tt in tiles:
                t0, N = TT[tt]
                col = 2 if tt == 0 else s
                if gi == 0:
                    self.prenorm(s, l, n, [tt], h, rots)
                act, actk = actr.next()
                for fc in range(nf):
                    pg, pgk = self.ps_s()
                    for kc in range(KC):
                        P.op("pe", lambda e, pg=pg, wg=wg, fc=fc, kc=kc, t0=t0, N=N: e.matmul(
                            pg[:, 0:N], wg[:, kc, fc * 128:(fc + 1) * 128], h[:, kc, t0:t0 + N],
                            start=(kc == 0), stop=(kc == KC - 1)), R=[wgk, ("h", tt)], W=[pgk])
                    pu, puk = self.ps_s()
                    for kc in range(KC):
                        P.op("pe", lambda e, pu=pu, wu=wu, fc=fc, kc=kc, t0=t0, N=N: e.matmul(
                            pu[:, 0:N], wu[:, kc, fc * 128:(fc + 1) * 128], h[:, kc, t0:t0 + N],
                            start=(kc == 0), stop=(kc == KC - 1)), R=[wuk, ("h", tt)], W=[puk])
                    sg, sgk = sgr.next()
                    P.op("act", lambda e, sg=sg, pg=pg, N=N: e.activation(sg[:, 0:N], pg[:, 0:N], AF.Silu), R=[pgk], W=[sgk])
                    P.op("dve", lambda e, act=act, fc=fc, pu=pu, sg=sg, N=N: e.tensor_tensor(
                        act[:, fc, 0:N], pu[:, 0:N], sg[:, 0:N], ALU.mult), R=[puk, sgk], W=[actk])
                if pend_d is not None:
                    down(*pend_d)
                pend_d = (tt, act, actk, wd, wdk, nf)
        down(*pend_d)
        P.barrier()
        mem.reset(m0)

    def load_hx_tile(self, s, tt):
        t0, N = TT[tt]
        self.P.dma("sp", self.hx[:, :, t0:t0 + N], self.hT[s, :, t0:t0 + N].rearrange("(c p) t -> p c t", p=128), W=[("hx", tt)])

    def final_out(self, s, outT, nxt=None):
        P, mem = self.P, self.mem
        hx = self.hx
        P.barrier()
        if nxt is not None:
            self.load_hx_tile(nxt, 0)
        m0 = mem.mark()
        sqr = Rot(mem, "sq", 2, [KC, 512], BF16)
        sdr = Rot(mem, "sd", 2, [512], F32)
        for tt in [1, 2, 3, 4]:
            t0, N = TT[tt]
            sq, sqk = sqr.next()
            P.op("act", lambda e, sq=sq, t0=t0, N=N: e.activation(sq[:, :, 0:N], hx[:, :, t0:t0 + N], AF.Square),
                 R=[("hx", tt)], W=[sqk])
            ps, pk = self.ps_s()
            for kc in range(KC):
                P.op("pe", lambda e, ps=ps, sq=sq, kc=kc, N=N: e.matmul(
                    ps[:, 0:N], self.ones_bf, sq[:, kc, 0:N], start=(kc == 0), stop=(kc == KC - 1)),
                    R=[sqk, "ones"], W=[pk])
            sd, sdk = sdr.next()
            P.op("act", lambda e, sd=sd, ps=ps, N=N: e.activation(sd[:, 0:N], ps[:, 0:N], AF.Ln, bias=self.eps_ap, scale=1.0 / D),
                 R=[pk, "eps"], W=[sdk])
            P.op("act", lambda e, sd=sd, N=N: e.activation(sd[:, 0:N], sd[:, 0:N], AF.Exp, scale=-0.5), R=[sdk], W=[sdk])
            for c in range(KC):
                o, ok = self.ostage.next()
                P.op("dve", lambda e, o=o, c=c, t0=t0, N=N, sd=sd: e.scalar_tensor_tensor(
                    o[:, 0:N], hx[:, c, t0:t0 + N], self.fnormw[:, c:c + 1], sd[:, 0:N], ALU.mult, ALU.mult),
                    R=[("hx", tt), sdk, "fnormw"], W=[ok])
                P.dma("sp", outT[s, c * 128:(c + 1) * 128, t0 - CT:t0 - CT + N], o[:, 0:N], R=[ok])
            if nxt is not None:
                self.load_hx_tile(nxt, tt)
        P.barrier()
        mem.reset(m0)

    def declare_mixer_inputs(self):
        self.din("rope64", [128, 2, S])
        self.din("rope32", [128, 2, S])
        self.din("ident", [128, 128])
        self.din("gmask", [128, 2, 512])
        self.din("gqa_pA", [4, 2, D, 256])
        self.din("gqa_pB", [4, D, 256])
        self.din("gqa_pV", [4, D, 64])
        self.din("gqa_w_out", [D, D])
        self.din("gqa_sinks_b", [128, 16])
        self.din("diff_pQ", [8, D, 256])
        self.din("diff_pK", [8, D, 256])
        self.din("diff_pV", [8, D, 128])
        self.din("diff_w_out", [D, D])
        self.din("diff_lam_b", [128, 4, 64])
        self.din("diff_subln_b", [128, 1])
        self.din("mla_pD", [3, D, 256])
        self.din("mla_pUQ", [16, 256, 256])
        self.din("mla_pUKV", [16, 256, 128])
        self.din("mla_w_out", [D, D])
        self.din("mla_nw", [128, 2, 2])
        self.din("hgrn_pA", [8, D, 256])
        self.din("hgrn_pB", [8, D, 256])
        self.din("hgrn_pV", [8, D, 128])
        self.din("hgrn_w_out", [D, D])
        self.din("hgrn_lbT", [128, 4, 8])
        self.din("hgrn_nw", [128, 1])
        self.din("rmask", [128, 512])
        self.din("trimask", [128, 2, 64])

    def alloc_mixer_consts(self):
        mem = self.mem
        self.eps_ap = mem.alloc([1], F32)
        self.rope = mem.alloc([2, S], BF16)
        self.ident = mem.alloc([128], BF16)
        self.gmask = mem.alloc([2, 512], BF16)
        self.sinkexp = mem.alloc([16], F32)
        self.neglam = mem.alloc([1], F32)
        self.sublnw = mem.alloc([1], F32)
        self.mla_nw = mem.alloc([2, 2], F32)
        self.lb = mem.alloc([8], F32)
        self.oml = mem.alloc([8], F32)
        self.noml = mem.alloc([8], F32)
        self.one_ap = mem.alloc([1], F32)
        self.hnw = mem.alloc([1], F32)
        self.rmask = mem.alloc([512], F32)
        self.trimask = mem.alloc([2, 64], BF16)

    def load_mixer_consts(self):
        P = self.P
        P.op("dve", lambda e: e.memset(self.eps_ap, EPS), W=["eps"])
        P.op("dve", lambda e: e.memset(self.one_ap, 1.0), W=["one"])
        P.dma("pool", self.ident, self.dram["ident"], W=["ident"])
        P.dma("pool", self.gmask, self.dram["gmask"], W=["gmask"])
        P.dma("sp", self.sinkexp, self.dram["gqa_sinks_b"], W=["sinkexp"])
        P.dma("sp", self.mla_nw, self.dram["mla_nw"], W=["mla_nw"])
        P.dma("sp", self.hnw, self.dram["hgrn_nw"], W=["hnw"])
        P.dma("sp", self.rmask, self.dram["rmask"], W=["rmask"])
        P.dma("pool", self.trimask, self.dram["trimask"], W=["trimask"])
        P.op("act", lambda e: e.activation(self.sinkexp, self.sinkexp, AF.Exp), R=["sinkexp"], W=["sinkexp"])

    def prologue_mixer(self):
        P, mem = self.P, self.mem
        d = self.dram
        lam_init = 0.8 - 0.6 * math.exp(-0.3 * 1)
        lp = mem.alloc([4, 64], F32)
        pr = mem.alloc([2, 64], F32)
        ss = mem.alloc([2], F32)
        P.dma("sp", lp, d["diff_lam_b"], W=["lp"])
        P.dma("sp", self.sublnw, d["diff_subln_b"], W=["sublnw"])
        P.op("dve", lambda e: e.tensor_tensor(pr[:, 0, :], lp[:, 0, :], lp[:, 1, :], ALU.mult), R=["lp"], W=["pr"])
        P.op("dve", lambda e: e.tensor_tensor(pr[:, 1, :], lp[:, 2, :], lp[:, 3, :], ALU.mult), R=["lp"], W=["pr"])
        P.op("dve", lambda e: e.reduce_sum(ss[:, 0:1], pr[:, 0, :], axis=AX.X), R=["pr"], W=["ss"])
        P.op("dve", lambda e: e.reduce_sum(ss[:, 1:2], pr[:, 1, :], axis=AX.X), R=["pr"], W=["ss"])
        P.op("act", lambda e: e.activation(ss, ss, AF.Exp), R=["ss"], W=["ss"])
        P.op("dve", lambda e: e.scalar_tensor_tensor(self.neglam, ss[:, 1:2], -lam_init, ss[:, 0:1], ALU.add, ALU.subtract),
             R=["ss"], W=["neglam"])
        P.op("dve", lambda e: e.tensor_scalar(self.sublnw, self.sublnw, 1.0 - lam_init, None, ALU.mult), R=["sublnw"], W=["sublnw"])
        le = mem.alloc([4, 8], F32)
        den = mem.alloc([8], F32)
        P.dma("sp", le, d["hgrn_lbT"], W=["le"])
        P.op("act", lambda e: e.activation(le, le, AF.Exp), R=["le"], W=["le"])
        P.op("dve", lambda e: e.tensor_tensor(self.lb, le[:, 1, :], le[:, 2, :], ALU.add), R=["le"], W=["lb"])
        P.op("dve", lambda e: e.tensor_tensor(den, le[:, 0, :], le[:, 3, :], ALU.add), R=["le"], W=["den"])
        P.op("dve", lambda e: e.tensor_tensor(den, den, self.lb, ALU.add), R=["den", "lb"], W=["den"])
        P.op("dve", lambda e: e.reciprocal(den, den), R=["den"], W=["den"])
        P.op("dve", lambda e: e.tensor_tensor(self.lb, self.lb, den, ALU.mult), R=["lb", "den"], W=["lb"])
        P.op("dve", lambda e: e.tensor_scalar(self.oml, self.lb, -1.0, 1.0, ALU.mult, ALU.add), R=["lb"], W=["oml"])
        P.op("dve", lambda e: e.tensor_scalar(self.noml, self.lb, 1.0, -1.0, ALU.mult, ALU.add), R=["lb"], W=["noml"])

    def mm(self, out, lt, rh, start, stop, R, W):
        self.P.op("pe", lambda e: e.matmul(out, lt, rh, start=start, stop=stop), R=R, W=W)

    def mixer_begin(self, s, l, tiles):
        P, mem = self.P, self.mem
        P.barrier()
        m0 = mem.mark()
        h = mem.alloc([KC, T], BF16)
        rots = (Rot(mem, "sq", 1, [KC, 512], BF16), Rot(mem, "sd", 2, [512], F32), Rot(mem, "ntmp", 3, [512], F32))
        self.prenorm(s, l, 1, tiles, h, rots)
        P.barrier()
        mem.reset(m0)
        h = mem.alloc([KC, T], BF16)
        return m0, h

    def mixer_end(self, m0):
        self.P.barrier()
        self.mem.reset(m0)

    def rope_evac(self, psA, pkA, psB, pkB, dst, N, tokx, W, rows=slice(0, 128)):
        P = self.P
        t1, t2 = self.t1, self.t2
        cos = self.rope[rows, 0, tokx:tokx + N]
        sin = self.rope[rows, 1, tokx:tokx + N]
        P.op("dve", lambda e: e.tensor_tensor(t1[rows, 0:N], psA[rows, 0:N], cos, ALU.mult), R=[pkA, "rope"], W=["t1"])
        P.op("dve", lambda e: e.tensor_tensor(t2[rows, 0:N], psB[rows, 0:N], sin, ALU.mult), R=[pkB, "rope"], W=["t2"])
        P.op("dve", lambda e: e.tensor_tensor(dst, t1[rows, 0:N], t2[rows, 0:N], ALU.add), R=["t1", "t2"], W=W)

    def outproj_tile(self, s, l, tt, wO, kO, oT, ok, nk, tiles_key="hx"):
        P = self.P
        hx = self.hx
        t0, N = TT[tt]
        col = 2 if tt == 0 else s
        ln = l * 3 + 1
        for dc in range(KC):
            pd, pdk = self.ps_s()
            self.mmk(pd[:, 0:N], [(wO[:, kc, dc * 128:(dc + 1) * 128], oT[:, kc, 0:N]) for kc in range(nk)], [kO, ok], pdk)
            P.op("dve", lambda e, pd=pd, dc=dc: e.scalar_tensor_tensor(
                hx[:, dc, t0:t0 + N], pd[:, 0:N], self.Gmod[:, ln, dc, col:col + 1], hx[:, dc, t0:t0 + N],
                ALU.mult, ALU.add), R=[pdk, ("hx", tt)], W=[("hx", tt)])

    @staticmethod
    def tile_of_block(tb):
        return 0 if tb < 2 else 1 + (tb - 2) // 4

    def mixer(self, s, l):
        kind = l % 4
        if kind == 0:
            self.gqa(s, l)
        elif kind == 1:
            self.diffattn(s, l)
        elif kind == 2:
            self.hgrn(s, l)
        else:
            self.mla(s, l)

    def gqa(self, s, l):
        P, mem = self.P, self.mem
        tiles = [0, 1, 2, 3, 4]
        m0, h = self.mixer_begin(s, l, tiles)
        d = self.dram
        P.dma("pool", self.rope, d["rope64"], W=["rope"])
        kTs = [mem.alloc([T], BF16), mem.alloc([T], BF16)]
        vaug = mem.alloc([18, 2, 128], BF16)
        qr = Rot(mem, "qT", 2, [2, 512], BF16)
        ptr = Rot(mem, "PT", 2, [5, 512], BF16)
        self.t1 = mem.alloc([512], F32)
        self.t2 = mem.alloc([512], F32)
        otr = Rot(mem, "oT", 2, [2, 512], BF16)
        dtr = Rot(mem, "dt", 2, [128], F32)
        P.op("dve", lambda e: e.memset(vaug, 1.0), W=["vaug"])
        P.op("dve", lambda e: e.memset(kTs[0], 0.0), W=[("kT", tt) for tt in tiles])
        P.op("dve", lambda e: e.memset(kTs[1], 0.0), W=[("kT", tt) for tt in tiles])
        for j in range(4):
            wA1, k1 = self.wload(d["gqa_pA"][j, 0].rearrange("(k p) n -> p k n", p=128), (KC, 256))
            wA2, k2 = self.wload(d["gqa_pA"][j, 1].rearrange("(k p) n -> p k n", p=128), (KC, 256))
            wB, kB = self.wload(d["gqa_pB"][j].rearrange("(k p) n -> p k n", p=128), (KC, 256))
            wV, kV = self.wload(d["gqa_pV"][j].rearrange("(k p) n -> p k n", p=128), (KC, 64))
            wO, kO = self.wload(d["gqa_w_out"][j * 256:(j + 1) * 256, :].rearrange("(k p) n -> p k n", p=128), (2, D))
            for tt in tiles:
                t0, N = TT[tt]
                psA, pkA = self.ps_s()
                self.mmk(psA[:, 0:N], [(wB[:, kc, 0:128], h[:, kc, t0:t0 + N]) for kc in range(KC)], [kB, ("h", tt)], pkA)
                rws = (slice(0, 64), slice(64, 128))
                if tt == 0:
                    for i in range(2):
                        P.op("act", lambda e, psA=psA, t0=t0, N=N, i=i: e.activation(kTs[i][rws[i], t0:t0 + N], psA[rws[i], 0:N], AF.Copy),
                             R=[pkA], W=[("kT", tt)])
                else:
                    psB, pkB = self.ps_s()
                    self.mmk(psB[:, 0:N], [(wB[:, kc, 128:256], h[:, kc, t0:t0 + N]) for kc in range(KC)], [kB, ("h", tt)], pkB)
                    for i in range(2):
                        self.rope_evac(psA, pkA, psB, pkB, kTs[i][rws[i], t0:t0 + N], N, t0 - CT, [("kT", tt)], rows=rws[i])
            if self.cut <= 1:
                break
            for g4 in range(0, 18, 4):
                nb = min(4, 18 - g4)
                ps, pk = self.ps_s()
                for bi in range(nb):
                    tb = g4 + bi
                    self.mmk(ps[:, bi * 64:(bi + 1) * 64], [(h[:, kc, tb * 128:(tb + 1) * 128], wV[:, kc, :]) for kc in range(KC)],
                             [kV, ("h", self.tile_of_block(tb))], pk)
                src = ps[:, 0:nb * 64].rearrange("p (a b) -> p a b", b=64)
                P.op("act", lambda e, src=src, g4=g4, nb=nb: e.activation(vaug[:, g4:g4 + nb, 0, 0:64], src, AF.Copy), R=[pk], W=["vaug"])
                P.op("dve", lambda e, src=src, g4=g4, nb=nb: e.tensor_copy(vaug[:, g4:g4 + nb, 1, 64:128], src), R=[pk], W=["vaug"])
            if self.cut <= 2:
                break
            pend_o = None

            def make_q(tt, wA1=wA1, wA2=wA2, k1=k1, k2=k2):
                t0, N = TT[tt]
                qT, qk = qr.next()
                for qc in range(2):
                    psA, pkA = self.ps_s()
                    self.mmk(psA[:, 0:N], [(wA1[:, kc, qc * 128:(qc + 1) * 128], h[:, kc, t0:t0 + N]) for kc in range(KC)], [k1, ("h", tt)], pkA)
                    if tt == 0:
                        P.op("act", lambda e, psA=psA, qc=qc, N=N, qT=qT: e.activation(qT[:, qc, 0:N], psA[:, 0:N], AF.Copy), R=[pkA], W=[qk])
                    else:
                        psB, pkB = self.ps_s()
                        self.mmk(psB[:, 0:N], [(wA2[:, kc, qc * 128:(qc + 1) * 128], h[:, kc, t0:t0 + N]) for kc in range(KC)], [k2, ("h", tt)], pkB)
                        self.rope_evac(psA, pkA, psB, pkB, qT[:, qc, 0:N], N, t0 - CT, [qk])
                return qT, qk

            qnext = make_q(tiles[0])
            for ti, tt in enumerate(tiles):
                t0, N = TT[tt]
                nqb = N // 128
                qT, qk = qnext
                if ti + 1 < len(tiles):
                    qnext = make_q(tiles[ti + 1])
                oT, ok = otr.next()

                def issue_scores(qi, tt=tt, t0=t0, qT=qT, qk=qk):
                    tbq = (t0 + qi * 128) // 128
                    if tt == 0:
                        kbs = [(0, None), (1, None)]
                    else:
                        kbs = []
                        if tbq > 2:
                            kbs.append((tbq - 1, 0))
                        kbs.append((tbq, None))
                        if tbq < 17:
                            kbs.append((tbq + 1, 1))
                        kbs += [(0, None), (1, None)]
                    PT, ptk = ptr.next()
                    for kbi, (kb, m) in enumerate(kbs):
                        ps, pk = self.ps_s()
                        first = True
                        if m is not None:
                            self.mm(ps[:, 0:512], self.ident, self.gmask[:, m, :], True, False, [], [pk])
                            first = False
                        for hh in range(4):
                            qc = hh // 2
                            self.mm(ps[:, hh * 128:(hh + 1) * 128], kTs[hh % 2][:, kb * 128:(kb + 1) * 128],
                                    qT[:, qc, qi * 128:(qi + 1) * 128], first, hh == 3,
                                    [("kT", self.tile_of_block(kb)), qk], [pk])
                            first = False
                        P.op("act", lambda e, PT=PT, kbi=kbi, ps=ps: e.activation(PT[:, kbi, :], ps[:, 0:512], AF.Exp, scale=0.125),
                             R=[pk], W=[ptk])
                    return PT, ptk, kbs

                def consume(qi, PT, ptk, kbs, oT=oT, ok=ok, j=j):
                    po_, pok = self.ps_a()
                    for hh in range(4):
                        var = hh % 2
                        self.mmk(po_[:, hh * 128:(hh + 1) * 128],
                                 [(vaug[:, kb, var, :], PT[:, kbi, hh * 128:(hh + 1) * 128]) for kbi, (kb, m) in enumerate(kbs)],
                                 [ptk, "vaug"], pok)
                    for hh in range(4):
                        head = 4 * j + hh
                        qc = hh // 2
                        orow, drow = (slice(0, 64), slice(64, 128)) if hh % 2 == 0 else (slice(64, 128), slice(0, 64))
                        cols = slice(hh * 128, (hh + 1) * 128)
                        dt, dk = dtr.next()
                        P.op("act", lambda e, dt=dt, orow=orow, drow=drow, cols=cols, head=head, po_=po_: e.activation(
                            dt[orow, :], po_[drow, cols], AF.Identity, bias=self.sinkexp[drow, head:head + 1]), R=[pok], W=[dk])
                        P.op("dve", lambda e, dt=dt, orow=orow: e.reciprocal(dt[orow, :], dt[orow, :]), R=[dk], W=[dk])
                        P.op("dve", lambda e, dt=dt, orow=orow, cols=cols, qc=qc, qi=qi, oT=oT, po_=po_: e.tensor_tensor(
                            oT[orow, qc, qi * 128:(qi + 1) * 128], po_[orow, cols], dt[orow, :], ALU.mult), R=[pok, dk], W=[ok])

                nxt = issue_scores(0)
                for qi in range(nqb):
                    cur = nxt
                    if qi + 1 < nqb:
                        nxt = issue_scores(qi + 1)
                    consume(qi, *cur)
                if pend_o is not None:
                    self.outproj_tile(*pend_o)
                pend_o = (s, l, tt, wO, kO, oT, ok, 2)
            self.outproj_tile(*pend_o)
        self.mixer_end(m0)

    def diffattn(self, s, l):
        P, mem = self.P, self.mem
        d = self.dram
        tiles = [0, 1, 2, 3, 4]
        m0, h = self.mixer_begin(s, l, tiles)
        P.dma("pool", self.rope, d["rope64"], W=["rope"])
        kTs = [mem.alloc([T], BF16), mem.alloc([T], BF16)]
        V = mem.alloc([18, 128], BF16)
        qr = Rot(mem, "qT", 2, [512], BF16)
        ptr = Rot(mem, "PT", 3, [1024], BF16)
        self.t1 = mem.alloc([512], F32)
        self.t2 = mem.alloc([512], F32)
        rr = Rot(mem, "rj", 2, [512], F32)
        tr = Rot(mem, "tj", 2, [512], F32)
        combr = Rot(mem, "comb", 2, [512], F32)
        sqb = mem.alloc([512], BF16)
        sd = mem.alloc([512], F32)
        otr = Rot(mem, "oT", 2, [1, 512], BF16)

        def finish(tt, comb, ck, wO, kO):
            t0, N = TT[tt]
            P.op("act", lambda e: e.activation(sqb[:, 0:N], comb[:, 0:N], AF.Square), R=[ck], W=["sqb"])
            ps, pk = self.ps_s()
            self.mm(ps[:, 0:N], self.ones_bf, sqb[:, 0:N], True, True, ["sqb"], [pk])
            P.op("act", lambda e: e.activation(sd[:, 0:N], ps[:, 0:N], AF.Ln, bias=self.eps_ap, scale=1.0 / 128), R=[pk], W=["sd"])
            P.op("act", lambda e: e.activation(sd[:, 0:N], sd[:, 0:N], AF.Exp, scale=-0.5), R=["sd"], W=["sd"])
            P.op("dve", lambda e: e.tensor_tensor(comb[:, 0:N], comb[:, 0:N], sd[:, 0:N], ALU.mult), R=[ck, "sd"], W=[ck])
            oT, ok = otr.next()
            P.op("act", lambda e: e.activation(oT[:, 0, 0:N], comb[:, 0:N], AF.Identity, scale=self.sublnw[:, 0:1]), R=[ck], W=[ok])
            self.outproj_tile(s, l, tt, wO, kO, oT, ok, 1)

        P.op("dve", lambda e: e.memset(kTs[0], 0.0), W=[("kT", tt) for tt in tiles])
        P.op("dve", lambda e: e.memset(kTs[1], 0.0), W=[("kT", tt) for tt in tiles])
        rws = (slice(0, 64), slice(64, 128))
        for hd in range(8):
            wQ, kQ = self.wload(d["diff_pQ"][hd].rearrange("(k p) n -> p k n", p=128), (KC, 256))
            wK, kK = self.wload(d["diff_pK"][hd].rearrange("(k p) n -> p k n", p=128), (KC, 256))
            wV, kV = self.wload(d["diff_pV"][hd].rearrange("(k p) n -> p k n", p=128), (KC, 128))
            wO, kO = self.wload(d["diff_w_out"][hd * 128:(hd + 1) * 128, :].rearrange("(k p) n -> p k n", p=128), (1, D))
            for tt in tiles:
                t0, N = TT[tt]
                psA, pkA = self.ps_s()
                self.mmk(psA[:, 0:N], [(wK[:, kc, 0:128], h[:, kc, t0:t0 + N]) for kc in range(KC)], [kK, ("h", tt)], pkA)
                if tt == 0:
                    for i in range(2):
                        P.op("act", lambda e, psA=psA, t0=t0, N=N, i=i: e.activation(kTs[i][rws[i], t0:t0 + N], psA[rws[i], 0:N], AF.Copy),
                             R=[pkA], W=[("kT", tt)])
                else:
                    psB, pkB = self.ps_s()
                    self.mmk(psB[:, 0:N], [(wK[:, kc, 128:256], h[:, kc, t0:t0 + N]) for kc in range(KC)], [kK, ("h", tt)], pkB)
                    for i in range(2):
                        self.rope_evac(psA, pkA, psB, pkB, kTs[i][rws[i], t0:t0 + N], N, t0 - CT, [("kT", tt)], rows=rws[i])
            for g4 in range(0, 18, 4):
                nb = min(4, 18 - g4)
                ps, pk = self.ps_s()
                for bi in range(nb):
                    tb = g4 + bi
                    self.mmk(ps[:, bi * 128:(bi + 1) * 128], [(h[:, kc, tb * 128:(tb + 1) * 128], wV[:, kc, :]) for kc in range(KC)],
                             [kV, ("h", self.tile_of_block(tb))], pk)
                src = ps[:, 0:nb * 128].rearrange("p (a b) -> p a b", b=128)
                P.op("act", lambda e, src=src, g4=g4, nb=nb: e.activation(V[:, g4:g4 + nb, :], src, AF.Copy), R=[pk], W=["V"])
            pend_o = None

            def make_q(tt, wQ=wQ, kQ=kQ):
                t0, N = TT[tt]
                qT, qk = qr.next()
                psA, pkA = self.ps_s()
                self.mmk(psA[:, 0:N], [(wQ[:, kc, 0:128], h[:, kc, t0:t0 + N]) for kc in range(KC)], [kQ, ("h", tt)], pkA)
                if tt == 0:
                    P.op("act", lambda e, psA=psA, N=N, qT=qT: e.activation(qT[:, 0:N], psA[:, 0:N], AF.Copy), R=[pkA], W=[qk])
                else:
                    psB, pkB = self.ps_s()
                    self.mmk(psB[:, 0:N], [(wQ[:, kc, 128:256], h[:, kc, t0:t0 + N]) for kc in range(KC)], [kQ, ("h", tt)], pkB)
                    self.rope_evac(psA, pkA, psB, pkB, qT[:, 0:N], N, t0 - CT, [qk])
                return qT, qk

            qnext = make_q(tiles[0])
            for ti, tt in enumerate(tiles):
                t0, N = TT[tt]
                qT, qk = qnext
                if ti + 1 < len(tiles):
                    qnext = make_q(tiles[ti + 1])
                kbs = [0, 1] if tt == 0 else list(range(18))
                tjs = []
                for j in range(2):
                    O, Ok = self.ps_a()
                    Dn, Dk = self.ps_a()
                    def issue_pair(pr, j=j, qT=qT, qk=qk, N=N):
                        dps, dk = self.ps_s2()
                        for hf, kb in enumerate(pr):
                            self.mm(dps[:, hf * 512:hf * 512 + N], kTs[j][:, kb * 128:(kb + 1) * 128], qT[:, 0:N], True, True,
                                    [("kT", self.tile_of_block(kb)), qk], [dk[hf]])
                        PT, ptk = ptr.next()
                        P.op("act", lambda e, PT=PT, dps=dps, N=N: e.activation(
                            PT.rearrange("p (a b) -> p a b", a=2)[:, :, 0:N], dps.rearrange("p (a b) -> p a b", a=2)[:, :, 0:N],
                            AF.Exp, scale=0.125), R=[dk[0], dk[1]], W=[ptk])
                        return PT, ptk
                    pairs = [kbs[i:i + 2] for i in range(0, len(kbs), 2)]
                    pend = [issue_pair(pairs[0])]
                    nkb = len(kbs)
                    for pi, pr in enumerate(pairs):
                        PT, ptk = pend[pi]
                        if pi + 1 < len(pairs):
                            pend.append(issue_pair(pairs[pi + 1]))
                        for hf, kb in enumerate(pr):
                            kbi = 2 * pi + hf
                            self.mm(O[:, 0:N], V[:, kb, :], PT[:, hf * 512:hf * 512 + N], kbi == 0, kbi == nkb - 1, [ptk, "V"], [Ok])
                            self.mm(Dn[:, 0:N], self.ones_bf, PT[:, hf * 512:hf * 512 + N], kbi == 0, kbi == nkb - 1, [ptk], [Dk])
                    rj, rk = rr.next()
                    tj, tk = tr.next()
                    P.op("dve", lambda e, rj=rj, Dn=Dn, N=N: e.reciprocal(rj[:, 0:N], Dn[:, 0:N]), R=[Dk], W=[rk])
                    P.op("dve", lambda e, tj=tj, O=O, rj=rj, N=N: e.tensor_tensor(tj[:, 0:N], O[:, 0:N], rj[:, 0:N], ALU.mult), R=[Ok, rk], W=[tk])
                    tjs.append((tj, tk))
                comb, ck = combr.next()
                P.op("dve", lambda e, tjs=tjs, N=N, comb=comb: e.scalar_tensor_tensor(
                    comb[:, 0:N], tjs[1][0][:, 0:N], self.neglam[:, 0:1], tjs[0][0][:, 0:N], ALU.mult, ALU.add),
                    R=[tjs[0][1], tjs[1][1]], W=[ck])
                if pend_o is not None:
                    finish(*pend_o)
                pend_o = (tt, comb, ck, wO, kO)
            finish(*pend_o)
        self.mixer_end(m0)

    def hgrn(self, s, l):
        P, mem = self.P, self.mem
        d = self.dram
        tiles = [0, 1, 2, 3, 4]
        m0, h = self.mixer_begin(s, l, tiles)
        Vtok = mem.alloc([36, 128], BF16)
        oacc = mem.alloc([T], F32)
        m1 = mem.mark()
        r64 = slice(0, 64)
        for hd in range(8):
            wA, kA = self.wload(d["hgrn_pA"][hd].rearrange("(k p) n -> p k n", p=128), (KC, 256))
            wB, kB = self.wload(d["hgrn_pB"][hd].rearrange("(k p) n -> p k n", p=128), (KC, 256))
            wV, kV = self.wload(d["hgrn_pV"][hd].rearrange("(k p) n -> p k n", p=128), (KC, 128))
            wO, kO = self.wload(d["hgrn_w_out"][hd * 128:(hd + 1) * 128, :].rearrange("(k p) n -> p k n", p=128), (1, D))
            for g4 in range(0, 36, 4):
                ps, pk = self.ps_s()
                for bi in range(4):
                    ch = g4 + bi
                    self.mmk(ps[r64, bi * 128:(bi + 1) * 128], [(h[:, kc, ch * 64:(ch + 1) * 64], wV[:, kc, :]) for kc in range(KC)],
                             [kV, ("h", self.tile_of_block(ch // 2))], pk)
                src = ps[r64, 0:512].rearrange("p (a b) -> p a b", b=128)
                P.op("act", lambda e, src=src, g4=g4: e.activation(Vtok[r64, g4:g4 + 4, :], src, AF.Copy), R=[pk], W=["Vtok"])
            qs_all = self.rope.rearrange("p a b -> p (a b)")[:, 0:T]
            for tt in tiles:
                t0, N = TT[tt]
                ps, pk = self.ps_s()
                self.mmk(ps[:, 0:N], [(wA[:, kc, 0:128], h[:, kc, t0:t0 + N]) for kc in range(KC)], [kA, ("h", tt)], pk)
                P.op("act", lambda e, ps=ps, t0=t0, N=N: e.activation(qs_all[:, t0:t0 + N], ps[:, 0:N], AF.Silu), R=[pk], W=["rope"])
            HT = [(i * 256, 256) for i in range(9)]
            P.op("dve", lambda e: e.memset(oacc, 0.0), W=[("oacc", i) for i in range(9)])

            def chain(dr):
                NN = 256
                F_ = mem.alloc([NN], F32)
                LF = mem.alloc([NN], F32)
                Pg = mem.alloc([NN], F32)
                D1, D4 = F_, LF
                E3 = mem.alloc([NN], F32)
                tot = mem.alloc([4], F32)
                kk = mem.alloc([NN], BF16)
                E1 = mem.alloc([NN], BF16)
                E2 = mem.alloc([NN], BF16)
                E4 = E1
                QG = mem.alloc([NN], BF16)
                KG = mem.alloc([NN], BF16)
                QE = mem.alloc([NN], BF16)
                KE = mem.alloc([NN], BF16)
                AT4 = mem.alloc([4, 64], BF16)
                KE4 = mem.alloc([4, 128], BF16)
                srot = Rot(mem, "S%d" % dr, 3, [128], F32)
                sbr = Rot(mem, "Sbf%d" % dr, 3, [128], BF16)
                yield
                K_ = lambda nm: (nm, dr)
                mi, li = (32, 63) if dr == 0 else (31, 0)
                order = list(range(9)) if dr == 0 else [0] + list(range(8, 0, -1))
                Sc, Sck = srot.next()
                P.op("dve", lambda e, Sc=Sc: e.memset(Sc, 0.0), W=[Sck])
                sb, sbk = sbr.next()
                P.op("dve", lambda e, sb=sb: e.memset(sb, 0.0), W=[sbk])
                v3 = lambda a: a[:, 0:NN].rearrange("p (c t) -> p c t", t=64)
                nch = 4
                for ti in order:
                    t0, N = HT[ti]
                    hk = ("h", 0 if ti == 0 else 1 + (ti - 1) // 2)
                    qs = qs_all[:, t0:t0 + NN]
                    ps, pk = self.ps_s()
                    wf, kf, c0 = (wA, kA, 128) if dr == 0 else (wB, kB, 0)
                    self.mmk(ps[:, 0:N], [(wf[:, kc, c0:c0 + 128], h[:, kc, t0:t0 + N]) for kc in range(KC)], [kf, hk], pk)
                    P.op("act", lambda e, ps=ps: e.activation(F_, ps[:, 0:NN], AF.Exp, scale=-1.0), R=[pk], W=[K_("F")])
                    yield
                    P.op("act", lambda e: e.activation(Pg, F_, AF.Ln, bias=self.one_ap), R=[K_("F")], W=[K_("Pg")])
                    yield
                    P.op("act", lambda e, hd=hd: e.activation(LF, F_, AF.Ln, bias=self.one_ap, scale=self.lb[:, hd:hd + 1]), R=[K_("F")], W=[K_("LF")])
                    yield
                    P.op("act", lambda e: e.activation(F_, Pg, AF.Exp, scale=-1.0), R=[K_("Pg")], W=[K_("F")])
                    P.op("dve", lambda e: e.tensor_tensor(LF, LF, Pg, ALU.subtract), R=[K_("LF"), K_("Pg")], W=[K_("LF")])
                    yield
                    P.op("dve", lambda e, hd=hd: e.tensor_scalar(kk, F_, self.noml[:, hd:hd + 1], self.oml[:, hd:hd + 1], ALU.mult, ALU.add),
                         R=[K_("F")], W=[K_("kk")])
                    yield
                    P.op("dve", lambda e: e.tensor_tensor_scan(Pg, self.rmask[:, 0:NN], LF, 0.0, ALU.mult, ALU.add), R=[K_("LF")], W=[K_("Pg")])
                    yield
                    if dr == 1:
                        P.op("dve", lambda e: e.tensor_copy(tot[:, 0:nch], v3(Pg)[:, :, 63]), R=[K_("Pg")], W=[K_("tot")])
                        P.op("dve", lambda e: e.tensor_tensor(Pg, LF, Pg, ALU.subtract), R=[K_("LF"), K_("Pg")], W=[K_("Pg")])
                        yield
                        P.op("dve", lambda e: e.tensor_tensor(v3(Pg), v3(Pg), tot[:, 0:nch].unsqueeze(2).broadcast_to([128, nch, 64]), ALU.add),
                             R=[K_("Pg"), K_("tot")], W=[K_("Pg")])
                        yield
                    P.op("dve", lambda e: e.tensor_tensor(v3(D1), v3(Pg), v3(Pg)[:, :, mi:mi + 1].broadcast_to([128, nch, 64]), ALU.subtract),
                         R=[K_("Pg")], W=[K_("F")])
                    P.op("act", lambda e: e.activation(E3, Pg, AF.Exp), R=[K_("Pg")], W=[K_("E3")])
                    yield
                    P.op("dve", lambda e: e.tensor_tensor(v3(D4), v3(Pg)[:, :, li:li + 1].broadcast_to([128, nch, 64]), v3(Pg), ALU.subtract),
                         R=[K_("Pg")], W=[K_("LF")])
                    P.op("act", lambda e: e.activation(E1, D1, AF.Exp), R=[K_("F")], W=[K_("E1")])
                    yield
                    P.op("dve", lambda e, qs=qs: e.tensor_tensor(QE, qs, E3, ALU.mult), R=["rope", K_("E3")], W=[K_("QE")])
                    P.op("act", lambda e: e.activation(E2, D1, AF.Exp, scale=-1.0), R=[K_("F")], W=[K_("E2")])
                    yield
                    P.op("dve", lambda e, qs=qs: e.tensor_tensor(QG, qs, E1, ALU.mult), R=["rope", K_("E1")], W=[K_("QG")])
                    P.op("act", lambda e: e.activation(E4, D4, AF.Exp), R=[K_("LF")], W=[K_("E1")])
                    yield
                    P.op("dve", lambda e: e.tensor_tensor(KG, kk, E2, ALU.mult), R=[K_("kk"), K_("E2")], W=[K_("KG")])
                    yield
                    P.op("dve", lambda e: e.tensor_tensor(KE, kk, E4, ALU.mult), R=[K_("kk"), K_("E1")], W=[K_("KE")])
                    yield
                    corder = list(range(nch)) if dr == 0 else list(range(nch - 1, -1, -1))
                    psA, pkA = self.ps_s()
                    for ci in range(nch):
                        cs = slice(ci * 64, (ci + 1) * 64)
                        self.mm(psA[r64, cs], KG[:, cs], QG[:, cs], True, True, [K_("KG"), K_("QG")], [pkA])
                    pst, pstk = self.ps_s()
                    pstb = pst[:, 0:256].bitcast(BF16)
                    for ci in range(nch):
                        cs = slice(ci * 64, (ci + 1) * 64)
                        P.op("pe", lambda e, pstb=pstb, cs=cs, ci=ci: e.transpose(pstb[r64, ci * 128:(ci + 1) * 128], KE[:, cs], self.ident),
                             R=[K_("KE")], W=[pstk])
                    yield
                    P.op("dve", lambda e, psA=psA: e.tensor_tensor(
                        AT4[r64, :, :], psA[r64, 0:256].rearrange("p (c t) -> p c t", t=64),
                        self.trimask[r64, dr, :].unsqueeze(1).broadcast_to([64, nch, 64]), ALU.mult), R=[pkA], W=[K_("AT")])
                    P.op("act", lambda e, pstb=pstb: e.activation(KE4[r64, :, :], pstb[r64, 0:512].rearrange("p (c t) -> p c t", t=128), AF.Copy),
                         R=[pstk], W=[K_("KEt")])
                    yield
                    pS, pSk = self.ps_a()
                    for ci in range(nch):
                        ch = t0 // 64 + ci
                        self.mm(pS[:, ci * 128:(ci + 1) * 128], KE4[r64, ci, :], Vtok[r64, ch, :], True, True, [K_("KEt"), "Vtok"], [pSk])
                    yield
                    O, Ok = self.ps_a()
                    for ci in corder:
                        ch = t0 // 64 + ci
                        cs = slice(ci * 64, (ci + 1) * 64)
                        self.mm(O[:, cs], Vtok[r64, ch, :], AT4[r64, ci, :], True, False, ["Vtok", K_("AT")], [Ok])
                        self.mm(O[:, cs], sb, QE[:, cs], False, True, [sbk, K_("QE")], [Ok])
                        di = ci * 64 + li
                        Sn, Snk = srot.next()
                        P.op("dve", lambda e, pS=pS, ci=ci, di=di, Sn=Sn, Sc=Sc: e.scalar_tensor_tensor(
                            Sn, Sc, E3[:, di:di + 1], pS[:, ci * 128:(ci + 1) * 128], ALU.mult, ALU.add),
                            R=[Sck, K_("E3"), pSk], W=[Snk])
                        Sc, Sck = Sn, Snk
                        sb, sbk = sbr.next()
                        P.op("act", lambda e, sb=sb, Sc=Sc: e.activation(sb, Sc, AF.Copy), R=[Sck], W=[sbk])
                        yield
                    P.op("dve", lambda e, O=O, t0=t0: e.tensor_tensor(oacc[:, t0:t0 + NN], O[:, 0:NN], oacc[:, t0:t0 + NN], ALU.add),
                         R=[Ok, ("oacc", ti)], W=[("oacc", ti)])
                    yield

            gens = [chain(0), chain(1)]
            alive = list(gens)
            while alive:
                for g in list(alive):
                    try:
                        next(g)
                    except StopIteration:
                        alive.remove(g)
            P.barrier()
            mem.reset(m1)
            gs_all = mem.alloc([T], BF16)
            sqr = Rot(mem, "sqb", 2, [512], BF16)
            sdr = Rot(mem, "sd", 2, [512], F32)
            tmr = Rot(mem, "tmp", 2, [512], F32)
            otr = Rot(mem, "oT", 2, [1, 512], BF16)
            for tt in tiles:
                t0, N = TT[tt]
                ps, pk = self.ps_s()
                self.mmk(ps[:, 0:N], [(wB[:, kc, 128:256], h[:, kc, t0:t0 + N]) for kc in range(KC)], [kB, ("h", tt)], pk)
                P.op("act", lambda e, ps=ps, t0=t0, N=N: e.activation(gs_all[:, t0:t0 + N], ps[:, 0:N], AF.Silu), R=[pk], W=[("gs", tt)])
            pend_o = None
            for tt in tiles:
                t0, N = TT[tt]
                sqb, sqk = sqr.next()
                sd, sdk = sdr.next()
                tmp, tmk = tmr.next()
                P.op("act", lambda e, sqb=sqb, t0=t0, N=N: e.activation(sqb[:, 0:N], oacc[:, t0:t0 + N], AF.Square), R=[], W=[sqk])
                ps2, pk2 = self.ps_s()
                self.mm(ps2[:, 0:N], self.ones_bf, sqb[:, 0:N], True, True, [sqk], [pk2])
                P.op("act", lambda e, sd=sd, ps2=ps2, N=N: e.activation(sd[:, 0:N], ps2[:, 0:N], AF.Ln, bias=self.eps_ap, scale=1.0 / 128), R=[pk2], W=[sdk])
                P.op("act", lambda e, sd=sd, N=N: e.activation(sd[:, 0:N], sd[:, 0:N], AF.Exp, scale=-0.5), R=[sdk], W=[sdk])
                P.op("dve", lambda e, tmp=tmp, sd=sd, t0=t0, N=N: e.tensor_tensor(tmp[:, 0:N], oacc[:, t0:t0 + N], sd[:, 0:N], ALU.mult), R=[sdk], W=[tmk])
                oT, ok = otr.next()
                P.op("dve", lambda e, oT=oT, tmp=tmp, t0=t0, N=N: e.scalar_tensor_tensor(
                    oT[:, 0, 0:N], tmp[:, 0:N], self.hnw[:, 0:1], gs_all[:, t0:t0 + N], ALU.mult, ALU.mult),
                    R=[tmk, ("gs", tt)], W=[ok])
                if pend_o is not None:
                    self.outproj_tile(*pend_o)
                pend_o = (s, l, tt, wO, kO, oT, ok, 1)
            self.outproj_tile(*pend_o)
            P.barrier()
            mem.reset(m1)
        self.mixer_end(m0)

    def mla(self, s, l):
        P, mem = self.P, self.mem
        d = self.dram
        tiles = [0, 1, 2, 3, 4]
        xt = [1, 2, 3, 4]
        P.barrier()
        m0 = mem.mark()
        cqT = mem.alloc([2, T], BF16)
        ckvT = mem.alloc([2, T], BF16)
        krT = mem.alloc([T], BF16)
        m1 = mem.mark()
        h = mem.alloc([KC, T], BF16)
        rots = (Rot(mem, "sq", 1, [KC, 512], BF16), Rot(mem, "sd", 2, [512], F32), Rot(mem, "ntmp", 3, [512], F32))
        self.prenorm(s, l, 1, tiles, h, rots)
        P.barrier()
        mem.reset(m1)
        h = mem.alloc([KC, T], BF16)
        P.dma("pool", self.rope, d["rope32"], W=["rope"])
        self.t1 = mem.alloc([512], F32)
        self.t2 = mem.alloc([512], F32)
        sq2 = mem.alloc([2, 512], BF16)
        sdr = Rot(mem, "sd", 2, [512], F32)
        wD1, kD1 = self.wload(d["mla_pD"][0].rearrange("(k p) n -> p k n", p=128), (KC, 256))
        wD2, kD2 = self.wload(d["mla_pD"][1].rearrange("(k p) n -> p k n", p=128), (KC, 256))
        wD3, kD3 = self.wload(d["mla_pD"][2].rearrange("(k p) n -> p k n", p=128), (KC, 256))
        r96 = slice(64, 96)
        for tt in tiles:
            t0, N = TT[tt]
            for (wD, kD, dstT, nwi) in ((wD1, kD1, cqT, 0), (wD2, kD2, ckvT, 1)):
                pcs = []
                for c in range(2):
                    ps, pk = self.ps_s()
                    self.mmk(ps[:, 0:N], [(wD[:, kc, c * 128:(c + 1) * 128], h[:, kc, t0:t0 + N]) for kc in range(KC)], [kD, ("h", tt)], pk)
                    P.op("act", lambda e, ps=ps, c=c, N=N: e.activation(sq2[:, c, 0:N], ps[:, 0:N], AF.Square), R=[pk], W=["sq2"])
                    pcs.append((ps, pk))
                pss, pssk = self.ps_s()
                self.mmk(pss[:, 0:N], [(self.ones_bf, sq2[:, c, 0:N]) for c in range(2)], ["sq2"], pssk)
                sd, sdk = sdr.next()
                P.op("act", lambda e, sd=sd, pss=pss, N=N: e.activation(sd[:, 0:N], pss[:, 0:N], AF.Ln, bias=self.eps_ap, scale=1.0 / 256), R=[pssk], W=[sdk])
                P.op("act", lambda e, sd=sd, N=N: e.activation(sd[:, 0:N], sd[:, 0:N], AF.Exp, scale=-0.5), R=[sdk], W=[sdk])
                for c in range(2):
                    ps, pk = pcs[c]
                    P.op("dve", lambda e, ps=ps, c=c, sd=sd, dstT=dstT, nwi=nwi, t0=t0, N=N: e.scalar_tensor_tensor(
                        dstT[:, c, t0:t0 + N], ps[:, 0:N], self.mla_nw[:, nwi, c:c + 1], sd[:, 0:N], ALU.mult, ALU.mult),
                        R=[pk, sdk], W=[("lora", nwi, tt)])
            psA, pkA = self.ps_s()
            self.mmk(psA[:, 0:N], [(wD3[:, kc, 0:128], h[:, kc, t0:t0 + N]) for kc in range(KC)], [kD3, ("h", tt)], pkA)
            if tt == 0:
                P.op("act", lambda e, psA=psA, t0=t0, N=N: e.activation(krT[r96, t0:t0 + N], psA[r96, 0:N], AF.Copy), R=[pkA], W=[("krT", tt)])
            else:
                psB, pkB = self.ps_s()
                self.mmk(psB[:, 0:N], [(wD3[:, kc, 128:256], h[:, kc, t0:t0 + N]) for kc in range(KC)], [kD3, ("h", tt)], pkB)
                self.rope_evac(psA, pkA, psB, pkB, krT[r96, t0:t0 + N], N, t0 - CT, [("krT", tt)], rows=r96)
        P.barrier()
        mem.reset(m1)
        self.t1 = mem.alloc([512], F32)
        self.t2 = mem.alloc([512], F32)
        kh = [mem.alloc([T], BF16), mem.alloc([T], BF16)]
        vaug = [mem.alloc([18, 128], BF16), mem.alloc([18, 128], BF16)]
        qr = Rot(mem, "qT", 2, [512], BF16)
        ptr = Rot(mem, "PT", 3, [1024], BF16)
        dtr = Rot(mem, "dt", 2, [512], F32)
        otr = Rot(mem, "oT", 2, [1, 512], BF16)
        scale = 96.0 ** -0.5
        for i in range(2):
            P.op("dve", lambda e, i=i: e.memset(kh[i], 0.0), W=[("kh", i)])
            P.op("dve", lambda e, i=i: e.memset(vaug[i], 1.0), W=[("vaug", i)])
        for q in qr.bufs:
            P.op("dve", lambda e, q=q: e.memset(q, 0.0), W=[])
        P.barrier()
        for pr in range(8):
            wO, kO = self.wload(d["mla_w_out"][pr * 128:(pr + 1) * 128, :].rearrange("(k p) n -> p k n", p=128), (1, D))
            wq_, wkv_ = [], []
            for hh in range(2):
                hd = 2 * pr + hh
                wq_.append(self.wload(d["mla_pUQ"][hd].rearrange("(k p) n -> p k n", p=128), (2, 256)))
                wkv_.append(self.wload(d["mla_pUKV"][hd].rearrange("(k p) n -> p k n", p=128), (2, 128)))
            for hh in range(2):
                wkv, kkv = wkv_[hh]
                vcols = slice(0, 64) if hh == 0 else slice(64, 128)
                for tt in tiles:
                    t0, N = TT[tt]
                    ps, pk = self.ps_s()
                    self.mmk(ps[0:64, 0:N], [(wkv[:, kc, 0:64], ckvT[:, kc, t0:t0 + N]) for kc in range(2)], [kkv, ("lora", 1, tt)], pk)
                    P.op("act", lambda e, ps=ps, hh=hh, t0=t0, N=N: e.activation(kh[hh][0:64, t0:t0 + N], ps[0:64, 0:N], AF.Copy), R=[pk], W=[("kh", hh)])
                P.op("act", lambda e, hh=hh: e.activation(kh[hh][r96, :], krT[r96, :], AF.Copy), R=[("krT", tt) for tt in tiles], W=[("kh", hh)])
                for g8 in range(0, 18, 8):
                    nb = min(8, 18 - g8)
                    ps, pk = self.ps_s()
                    for bi in range(nb):
                        tb = g8 + bi
                        self.mmk(ps[:, bi * 64:(bi + 1) * 64], [(ckvT[:, kc, tb * 128:(tb + 1) * 128], wkv[:, kc, 64:128]) for kc in range(2)],
                                 [kkv, ("lora", 1, self.tile_of_block(tb))], pk)
                    src = ps[:, 0:nb * 64].rearrange("p (a b) -> p a b", b=64)
                    P.op("act", lambda e, src=src, g8=g8, nb=nb, hh=hh, vcols=vcols: e.activation(vaug[hh][:, g8:g8 + nb, vcols], src, AF.Copy),
                         R=[pk], W=[("vaug", hh)])
            pend_o = None

            def make_q(tt, hh, wq_=wq_):
                t0, N = TT[tt]
                wq, kq = wq_[hh]
                qT, qk = qr.next()
                psA, pkA = self.ps_s()
                self.mmk(psA[0:96, 0:N], [(wq[:, kc, 0:96], cqT[:, kc, t0:t0 + N]) for kc in range(2)], [kq, ("lora", 0, tt)], pkA)
                psB, pkB = self.ps_s()
                self.mmk(psB[0:96, 0:N], [(wq[:, kc, 128:224], cqT[:, kc, t0:t0 + N]) for kc in range(2)], [kq, ("lora", 0, tt)], pkB)
                P.op("act", lambda e, qT=qT, psA=psA, N=N: e.activation(qT[0:64, 0:N], psA[0:64, 0:N], AF.Copy), R=[pkA], W=[qk])
                self.rope_evac(psA, pkA, psB, pkB, qT[r96, 0:N], N, t0 - CT, [qk], rows=r96)
                return qT, qk

            items = [(tt, hh) for tt in xt for hh in range(2)]
            qnext = make_q(*items[0])
            for tt in xt:
                t0, N = TT[tt]
                oT, ok = otr.next()
                for hh in range(2):
                    orow, drow = (slice(0, 64), slice(64, 128)) if hh == 0 else (slice(64, 128), slice(0, 64))
                    qT, qk = qnext
                    ii = items.index((tt, hh))
                    if ii + 1 < len(items):
                        qnext = make_q(*items[ii + 1])
                    O, Ok = self.ps_a()
                    def issue_pair(kb0, hh=hh, qT=qT, qk=qk, N=N):
                        dps, dk2 = self.ps_s2()
                        for hf in range(2):
                            kb = kb0 + hf
                            self.mm(dps[:, hf * 512:hf * 512 + N], kh[hh][:, kb * 128:(kb + 1) * 128], qT[:, 0:N], True, True, [("kh", hh), qk], [dk2[hf]])
                        PT, ptk = ptr.next()
                        P.op("act", lambda e, PT=PT, dps=dps, N=N: e.activation(PT[:, 0:1024], dps[:, 0:1024], AF.Exp, scale=scale),
                             R=[dk2[0], dk2[1]], W=[ptk])
                        return PT, ptk
                    pend = [issue_pair(0)]
                    for pi in range(9):
                        PT, ptk = pend[pi]
                        if pi + 1 < 9:
                            pend.append(issue_pair(2 * (pi + 1)))
                        for hf in range(2):
                            kb = 2 * pi + hf
                            self.mm(O[:, 0:N], vaug[hh][:, kb, :], PT[:, hf * 512:hf * 512 + N], kb == 0, kb == 17, [ptk, ("vaug", hh)], [Ok])
                    dt, dk = dtr.next()
                    P.op("act", lambda e, dt=dt, O=O, orow=orow, drow=drow, N=N: e.activation(dt[orow, 0:N], O[drow, 0:N], AF.Copy), R=[Ok], W=[dk])
                    P.op("dve", lambda e, dt=dt, orow=orow, N=N: e.reciprocal(dt[orow, 0:N], dt[orow, 0:N]), R=[dk], W=[dk])
                    P.op("dve", lambda e, dt=dt, O=O, orow=orow, oT=oT, N=N: e.tensor_tensor(oT[orow, 0, 0:N], O[orow, 0:N], dt[orow, 0:N], ALU.mult),
                         R=[Ok, dk], W=[ok])
                if pend_o is not None:
                    self.outproj_tile(*pend_o)
                pend_o = (s, l, tt, wO, kO, oT, ok, 1)
            self.outproj_tile(*pend_o)
        self.mixer_end(m0)


def host_inputs(inputs, core):
    b0 = 2 * core
    x = inputs["x"]
    ctx = inputs["ctx"]
    hT = np.empty((2, D, T), np.float32)
    for i in range(2):
        hT[i, :, :CT] = ctx[b0 + i].T
        hT[i, :, CT:] = x[b0 + i].T
    cvec = np.stack([inputs["c"][b0], inputs["c"][b0 + 1], inputs["c_ctx"]], axis=1)
    cT = np.ascontiguousarray(cvec.reshape(KC, 128, 3).transpose(1, 0, 2))
    m = {"hT": hT, "cT": cT}
    return m


def rope_tables(rot_dim):
    rows = S // 64
    row = np.repeat(np.arange(rows, dtype=np.float32), 64)
    col = np.tile(np.arange(64, dtype=np.float32), rows)
    axis_dim = rot_dim // 2
    inv_freq = (np.float32(10000.0) ** (-np.arange(0, axis_dim, 2, dtype=np.float32) / np.float32(axis_dim))).astype(np.float32)
    ang_r = row[:, None] * inv_freq[None, :]
    ang_c = col[:, None] * inv_freq[None, :]
    ang = np.concatenate([ang_r, ang_r, ang_c, ang_c], axis=-1).astype(np.float32)
    f = rot_dim // 4
    sgn = np.concatenate([-np.ones(f), np.ones(f), -np.ones(f), np.ones(f)]).astype(np.float32)
    perm = np.concatenate([np.arange(f, 2 * f), np.arange(0, f), np.arange(3 * f, 4 * f), np.arange(2 * f, 3 * f)])
    return np.cos(ang).astype(np.float32), (np.sin(ang) * sgn[None, :]).astype(np.float32), perm


def host_consts():
    c = {}
    cos, sins, perm64 = rope_tables(64)
    r = np.zeros((128, 2, S), np.float32)
    r[:, 0, :] = np.tile(cos.T, (2, 1))
    r[:, 1, :] = np.tile(sins.T, (2, 1))
    c["rope64"] = r
    cos, sins, perm32 = rope_tables(32)
    r = np.zeros((128, 2, S), np.float32)
    r[64:96, 0, :] = cos.T
    r[64:96, 1, :] = sins.T
    c["rope32"] = r
    c["ident"] = np.eye(128, dtype=np.float32)
    NEG = -30000.0
    kk = np.arange(128)[:, None]
    qq = np.arange(128)[None, :]
    gm = np.zeros((128, 2, 512), np.float32)
    gm[:, 0, :] = np.tile(np.where(qq <= kk, 0.0, NEG), (1, 4))
    gm[:, 1, :] = np.tile(np.where(kk <= qq, 0.0, NEG), (1, 4))
    c["gmask"] = gm
    return c, perm64, perm32


def host_shared(inputs):
    sh, perm64, perm32 = host_consts()
    w_in = inputs["gqa_w_in"][0]
    hp = (np.arange(16)[:, None] * 64 + perm64[None, :]).reshape(-1)
    wq, wqp = w_in[:, :1024], w_in[:, :1024][:, hp]
    pA = np.empty((4, 2, D, 256), np.float32)
    pB = np.empty((4, D, 256), np.float32)
    pV = np.empty((4, D, 64), np.float32)
    for j in range(4):
        pA[j, 0] = wq[:, j * 256:(j + 1) * 256]
        pA[j, 1] = wqp[:, j * 256:(j + 1) * 256]
        kj = w_in[:, 1024 + j * 64:1024 + (j + 1) * 64]
        kjp = kj[:, perm64]
        pB[j] = np.concatenate([kj, kj, kjp, kjp], axis=1)
        pV[j] = w_in[:, 1280 + j * 64:1280 + (j + 1) * 64]
    sh["gqa_pA"], sh["gqa_pB"], sh["gqa_pV"] = pA, pB, pV
    sh["gqa_w_out"] = np.ascontiguousarray(inputs["gqa_w_out"][0])
    w_in = inputs["diff_w_in"][0]
    sp = (np.arange(16)[:, None] * 64 + perm64[None, :]).reshape(-1)
    wq, wk, wv = w_in[:, :1024], w_in[:, 1024:2048], w_in[:, 2048:]
    wqp, wkp = wq[:, sp], wk[:, sp]
    sh["diff_pQ"] = np.stack([np.concatenate([wq[:, i * 128:(i + 1) * 128], wqp[:, i * 128:(i + 1) * 128]], axis=1) for i in range(8)])
    sh["diff_pK"] = np.stack([np.concatenate([wk[:, i * 128:(i + 1) * 128], wkp[:, i * 128:(i + 1) * 128]], axis=1) for i in range(8)])
    sh["diff_pV"] = np.stack([wv[:, i * 128:(i + 1) * 128] for i in range(8)])
    sh["diff_w_out"] = np.ascontiguousarray(inputs["diff_w_out"][0])
    sh["diff_lam_b"] = np.ascontiguousarray(np.broadcast_to(inputs["diff_lambda"][0][None], (128, 4, 64)))
    sh["diff_subln_b"] = np.ascontiguousarray(inputs["diff_subln_w"][0].reshape(128, 1))
    wd = inputs["mla_w_down"][0]
    pD = np.zeros((3, D, 256), np.float32)
    pD[0] = wd[:, 0:256]
    pD[1] = wd[:, 256:512]
    pD[2][:, 64:96] = wd[:, 512:544]
    pD[2][:, 128 + 64:128 + 96] = wd[:, 512:544][:, perm32]
    sh["mla_pD"] = pD
    wuq = inputs["mla_w_uq"][0]
    pUQ = np.zeros((16, 256, 256), np.float32)
    for hd in range(16):
        blk = wuq[:, hd * 96:(hd + 1) * 96]
        pUQ[hd][:, 0:96] = blk
        pUQ[hd][:, 128:128 + 64] = blk[:, 0:64]
        pUQ[hd][:, 128 + 64:128 + 96] = blk[:, 64:96][:, perm32]
    sh["mla_pUQ"] = pUQ
    sh["mla_pUKV"] = np.ascontiguousarray(inputs["mla_w_ukv"][0].reshape(256, 16, 128).transpose(1, 0, 2))
    sh["mla_w_out"] = np.ascontiguousarray(inputs["mla_w_out"][0])
    nw = np.stack([inputs["mla_q_norm_w"][0].reshape(2, 128).T, inputs["mla_kv_norm_w"][0].reshape(2, 128).T], axis=1)
    sh["mla_nw"] = np.ascontiguousarray(nw)
    w_in = inputs["hgrn_w_in"][0]
    cq, cf, cb, cv, cg = (w_in[:, i * 1024:(i + 1) * 1024] for i in range(5))
    sh["hgrn_pA"] = np.stack([np.concatenate([cq[:, i * 128:(i + 1) * 128], cf[:, i * 128:(i + 1) * 128]], axis=1) for i in range(8)])
    sh["hgrn_pB"] = np.stack([np.concatenate([cb[:, i * 128:(i + 1) * 128], cg[:, i * 128:(i + 1) * 128]], axis=1) for i in range(8)])
    sh["hgrn_pV"] = np.stack([cv[:, i * 128:(i + 1) * 128] for i in range(8)])
    sh["hgrn_w_out"] = np.ascontiguousarray(inputs["hgrn_w_out"][0])
    sh["hgrn_lbT"] = np.ascontiguousarray(inputs["hgrn_lower_bounds"].reshape(4, 8, 128).transpose(2, 0, 1))
    sh["hgrn_nw"] = np.ascontiguousarray(inputs["hgrn_norm_w"][0].reshape(128, 1))
    rm = np.ones((128, 512), np.float32)
    rm[:, ::64] = 0.0
    sh["rmask"] = rm
    ss_, tt_ = np.arange(64)[:, None], np.arange(64)[None, :]
    tm = np.zeros((128, 2, 64), np.float32)
    tm[0:64, 0, :] = (ss_ <= tt_)
    tm[0:64, 1, :] = (ss_ >= tt_)
    sh["trimask"] = tm
    sh["gqa_sinks_b"] = np.ascontiguousarray(np.broadcast_to(inputs["gqa_sinks"][0][None, :], (128, 16)))
    sh["ada_w"] = np.ascontiguousarray(inputs["ada_w"])
    sh["ada_bT"] = np.ascontiguousarray(inputs["ada_b"].reshape(NL, 72, 128).transpose(2, 0, 1))
    sh["norm_wT"] = np.ascontiguousarray(inputs["norm_w"].reshape(NL * 3, KC, 128).transpose(2, 0, 1))
    sh["fnorm_wT"] = np.ascontiguousarray(inputs["final_norm_w"].reshape(KC, 128).T)
    for nm, src in (("ffn_wg_l", "ffn_w_gate"), ("ffn_wu_l", "ffn_w_up")):
        w = inputs[src].reshape(NL, 2, KC, 128, 11, 256)
        sh[nm] = np.ascontiguousarray(w.transpose(0, 1, 4, 3, 2, 5)).reshape(NL, 2, 11, 128, KC * 256)
    sh["ffn_w_down"] = np.ascontiguousarray(inputs["ffn_w_down"])
    return sh


def kernel(**inputs):
    inputs = {k: np.asarray(v) for k, v in inputs.items()}
    b = Builder()
    nc = b.build()
    sh = host_shared(inputs)
    in_maps = []
    for core in range(NCORES):
        m = dict(sh)
        m.update(host_inputs(inputs, core))
        in_maps.append({k: v for k, v in m.items() if k in b.dram})
    res = run_bass_kernel_spmd(nc, in_maps, core_ids=list(range(NCORES)))
    out = np.empty((16, S, D), np.float32)
    for core in range(NCORES):
        o = res.results[core]["outT"]
        for i in range(2):
            out[2 * core + i] = o[i].T
    return out
```

```python
import math
import contextlib
import numpy as np
import concourse.bass as bass
import concourse.mybir as mybir
from concourse.bass_utils import run_bass_kernel_spmd

F32 = mybir.dt.float32
BF16 = mybir.dt.bfloat16
AF = mybir.ActivationFunctionType
ALU = mybir.AluOpType
AX = mybir.AxisListType

ENGS = ("pe", "act", "dve", "pool", "sp")

D = 1024
KC = 8
S = 2048
CT = 256
T = CT + S
DFF = 2816
NL = 4
EPS = 1e-6
NCORES = 8
TT = [(0, 256), (256, 512), (768, 512), (1280, 512), (1792, 512)]


class Ev:
    __slots__ = ("eng", "need_inc", "semval", "dma_sem", "dma_val", "idx")

    def __init__(self, eng):
        self.eng = eng
        self.idx = -1
        self.need_inc = False
        self.semval = None
        self.dma_sem = None
        self.dma_val = None


class Rec:
    __slots__ = ("fn", "deps", "ev", "is_dma")

    def __init__(self, fn, deps, ev, is_dma=False):
        self.fn = fn
        self.deps = deps
        self.ev = ev
        self.is_dma = is_dma


class Prog:
    def __init__(self, nc, n_dma_sems=40):
        self.nc = nc
        self.streams = {e: [] for e in ENGS}
        self.last_w = {}
        self.readers = {}
        self.last_ev = {e: None for e in ENGS}
        self.n_dma_sems = n_dma_sems
        self.dma_cnt = 0
        self.dma_last = [None] * n_dma_sems
        self.dma_vals = [0] * n_dma_sems
        self.n_ops = 0

    def _collect(self, eng, R, W, is_dma):
        best = {}

        def add(e):
            if e is None:
                return
            k = ("d", e.dma_sem) if e.dma_sem is not None else ("e", e.eng)
            o = best.get(k)
            if o is None or (e.dma_val if e.dma_sem is not None else e.idx) > (o.dma_val if o.dma_sem is not None else o.idx):
                best[k] = e

        for k in R:
            add(self.last_w.get(k))
        for k in W:
            w = self.last_w.get(k)
            if w is not None and (w.eng != eng or w.dma_sem is not None or is_dma):
                add(w)
            for r in self.readers.get(k, ()):
                if r.eng != eng or r.dma_sem is not None or is_dma:
                    add(r)
        return list(best.values())

    def _update(self, ev, R, W):
        for k in R:
            self.readers.setdefault(k, []).append(ev)
        for k in W:
            self.last_w[k] = ev
            self.readers[k] = []

    def op(self, eng, fn, R=(), W=()):
        ev = Ev(eng)
        deps = self._collect(eng, R, W, False)
        for d in deps:
            if d.dma_sem is None:
                d.need_inc = True
        ev.idx = len(self.streams[eng])
        self.streams[eng].append(Rec(fn, deps, ev))
        self._update(ev, R, W)
        self.last_ev[eng] = ev
        self.n_ops += 1
        return ev

    def dma(self, eng, out, in_, R=(), W=(), **kw):
        ev = Ev(eng)
        slot = self.dma_cnt % self.n_dma_sems
        self.dma_cnt += 1
        deps = self._collect(eng, R, W, True)
        prev = self.dma_last[slot]
        if prev is not None:
            deps.append(prev)
        for d in deps:
            if d.dma_sem is None:
                d.need_inc = True
        self.dma_vals[slot] += 16
        ev.dma_sem = slot
        ev.dma_val = self.dma_vals[slot]
        self.dma_last[slot] = ev
        ev.idx = len(self.streams[eng])
        self.streams[eng].append(
            Rec(lambda e, o=out, i=in_, k=kw: e.dma_start(out=o, in_=i, **k), deps, ev, True))
        self._update(ev, R, W)
        self.n_ops += 1
        return ev

    def barrier(self, wait_dma=False):
        lasts = [self.last_ev[e] for e in ENGS if self.last_ev[e] is not None]
        for l in lasts:
            l.need_inc = True
        dmas = [d for d in self.dma_last if d is not None] if wait_dma else []
        for e in ENGS:
            deps = [l for l in lasts if l.eng != e] + dmas
            if deps:
                self.streams[e].append(Rec(None, deps, None))
        if wait_dma:
            self.last_w = {}
            self.readers = {}
        else:
            self.last_w = {k: v for k, v in self.last_w.items() if v.dma_sem is not None}
            rd = {k: [r for r in v if r.dma_sem is not None] for k, v in self.readers.items()}
            self.readers = {k: v for k, v in rd.items() if v}

    def wait_all_dma(self, eng="sp"):
        deps = [d for d in self.dma_last if d is not None]
        self.streams[eng].append(Rec(None, deps, None))

    def emit(self):
        nc = self.nc
        with contextlib.ExitStack() as st:
            sems = {e: st.enter_context(nc.semaphore("s_" + e)) for e in ENGS}
            dsems = [st.enter_context(nc.semaphore("d%d" % i)) for i in range(self.n_dma_sems)]
            for e in ENGS:
                c = 0
                for r in self.streams[e]:
                    if r.ev is not None and r.ev.dma_sem is None and r.ev.need_inc:
                        c += 1
                        r.ev.semval = c
            block = st.enter_context(nc.Block())

            def run(engname, engobj):
                waited = {}
                for r in self.streams[engname]:
                    for d in r.deps:
                        if d.dma_sem is not None:
                            key, sem, val = ("d", d.dma_sem), dsems[d.dma_sem], d.dma_val
                        else:
                            key, sem, val = ("e", d.eng), sems[d.eng], d.semval
                        if waited.get(key, 0) >= val:
                            continue
                        waited[key] = val
                        engobj.wait_ge(sem, val)
                    if r.fn is None:
                        continue
                    ins = r.fn(engobj)
                    if r.is_dma:
                        ins.then_inc(dsems[r.ev.dma_sem], 16)
                    elif r.ev.need_inc:
                        ins.then_inc(sems[engname], 1)

            @block.tensor
            def _(eng):
                run("pe", eng)

            @block.scalar
            def _(eng):
                run("act", eng)

            @block.vector
            def _(eng):
                run("dve", eng)

            @block.gpsimd
            def _(eng):
                run("pool", eng)

            @block.sync
            def _(eng):
                run("sp", eng)


class Mem:
    def __init__(self, nc, total_bytes):
        self.n = total_bytes // 4
        self.t = nc.alloc_sbuf_tensor("sbuf_all", [128, self.n], F32)
        self.off = 0
        self.hi = 0

    def alloc(self, shape, dt):
        n = 1
        for s in shape:
            n *= s
        nb = n * (4 if dt == F32 else 2)
        nf = ((nb + 31) // 32) * 8
        assert self.off + nf <= self.n, ("SBUF overflow", self.off * 4, nf * 4, self.n * 4)
        a = self.t[:, self.off:self.off + nf]
        self.off += nf
        self.hi = max(self.hi, self.off)
        if dt != F32:
            a = a.bitcast(dt)
        a = a[:, 0:n]
        if len(shape) == 2:
            a = a.rearrange("p (a b) -> p a b", a=shape[0])
        elif len(shape) == 3:
            a = a.rearrange("p (a b c) -> p a b c", a=shape[0], b=shape[1])
        return a

    def mark(self):
        return self.off

    def reset(self, m):
        self.off = m


class Rot:
    def __init__(self, mem, name, n, shape, dt):
        self.bufs = [mem.alloc(shape, dt) for _ in range(n)]
        self.name = name
        self.i = 0

    def next(self):
        j = self.i % len(self.bufs)
        self.i += 1
        return self.bufs[j], (self.name, j)


class Builder:
    def __init__(self, debug=None, layers=(0, 1, 2, 3), n_seq=2, stop_after=None):
        self.debug = debug or []
        self.layers = list(layers)
        self.n_seq = n_seq
        self.stop_after = stop_after
        self.cut = 99
        nc = self.nc = bass.Bass("TRN2", target_bir_lowering=False)
        self.P = Prog(nc)
        self.dram = {}
        self.dbg_out = {}

    def din(self, name, shape, dt=F32):
        t = self.nc.dram_tensor(name, list(shape), dt, kind="ExternalInput").ap()
        self.dram[name] = t
        return t

    def dout(self, name, shape, dt=F32):
        t = self.nc.dram_tensor(name, list(shape), dt, kind="ExternalOutput").ap()
        self.dram[name] = t
        return t

    def ps_s(self):
        i = self._ps_s % 4
        self._ps_s += 1
        return self.psb[i], ("ps", i)

    def ps_s2(self):
        if self._ps_s % 2 == 1:
            self._ps_s += 1
        i = self._ps_s % 4
        self._ps_s += 2
        return self.psd[i // 2], (("ps", i), ("ps", i + 1))

    def ps_a(self):
        i = 4 + self._ps_a % 4
        self._ps_a += 1
        return self.psb[i], ("ps", i)

    def wload(self, view, shape):
        a, b = shape
        assert a * b <= 2048, (a, b)
        s = self._wslot % len(self.wring)
        self._wslot += 1
        dst = self.wring[s][:, 0:a * b].rearrange("p (a b) -> p a b", a=a)
        key = ("w", s)
        self.P.dma("pool", dst, view, W=[key])
        return dst, key

    def mmk(self, out, pairs, R, pk):
        n = len(pairs)
        for i, (lt, rh) in enumerate(pairs):
            self.P.op("pe", lambda e, out=out, lt=lt, rh=rh, i=i, n=n: e.matmul(out, lt, rh, start=(i == 0), stop=(i == n - 1)),
                      R=R, W=[pk])

    def dump(self, name, ap, n, R):
        if name not in self.debug:
            return
        o = self.dout("dbg_" + name, [128, n])
        self.P.dma("sp", o, ap, R=R)

    def build(self):
        nc, P = self.nc, self.P
        hT = self.din("hT", [2, D, T])
        cT = self.din("cT", [128, KC, 3])
        ada_w = self.din("ada_w", [NL, D, 9 * D])
        ada_bT = self.din("ada_bT", [128, NL, 72])
        norm_wT = self.din("norm_wT", [128, NL * 3, KC])
        fnorm_wT = self.din("fnorm_wT", [128, KC])
        ffn_wg = self.din("ffn_wg_l", [NL, 2, 11, 128, KC * 256])
        ffn_wu = self.din("ffn_wu_l", [NL, 2, 11, 128, KC * 256])
        ffn_wd = self.din("ffn_w_down", [NL, 2, DFF, D])
        outT = self.dout("outT", [2, D, S])
        self.declare_mixer_inputs()

        mem = self.mem = Mem(nc, 206 * 1024)
        self.psd = [nc.alloc_psum_tensor("psd%d" % i, [128, 1024], F32) for i in range(2)]
        self.psb = [self.psd[i // 2][:, (i % 2) * 512:(i % 2 + 1) * 512] for i in range(4)]
        self.psb += [nc.alloc_psum_tensor("ps%d" % i, [128, 512], F32) for i in range(4, 8)]
        self._ps_s = 0
        self._ps_a = 0
        self._wslot = 0
        hx = self.hx = mem.alloc([KC, T], F32)
        self.wring = [mem.alloc([2048], BF16) for _ in range(8)]
        self.mod = mem.alloc([NL, 72, 3], F32)
        self.Amod = mem.alloc([NL * 3, KC, 3], F32)
        self.Gmod = mem.alloc([NL * 3, KC, 3], F32)
        self.normw = mem.alloc([NL * 3, KC], F32)
        self.fnormw = mem.alloc([KC], F32)
        self.ones_bf = mem.alloc([128], BF16)
        self.ostage = Rot(mem, "ostage", 2, [512], F32)
        self.alloc_mixer_consts()
        self.arena0 = mem.mark()

        P.op("dve", lambda e: e.memset(self.ones_bf, 1.0), W=["ones"])
        P.dma("sp", self.normw, norm_wT, W=["normw"])
        P.dma("sp", self.fnormw, fnorm_wT, W=["fnormw"])
        self.load_mixer_consts()

        self.prologue_mod(cT, ada_w, ada_bT)

        for s in range(self.n_seq):
            P.dma("sp", hx, hT[s].rearrange("(c p) t -> p c t", p=128), W=[("hx", tt) for tt in range(5)])
            for l in self.layers:
                last = (l == NL - 1)
                self.ffn(s, l, 0, [0, 1, 2, 3, 4], ffn_wg, ffn_wu, ffn_wd)
                if self.stop_after == ("ffn0", l):
                    break
                self.mixer(s, l)
                if self.stop_after == ("mixer", l):
                    break
                self.ffn(s, l, 1, [1, 2, 3, 4] if last else [0, 1, 2, 3, 4], ffn_wg, ffn_wu, ffn_wd)
            if "hx" in self.debug and s == 0:
                o = self.dout("dbg_hx", [128, KC, T])
                P.dma("sp", o, hx, R=[("hx", tt) for tt in range(5)])
            self.final_out(s, outT)
        P.wait_all_dma("sp")
        P.emit()
        return nc

    def prologue_mod(self, cT, ada_w, ada_bT):
        P, mem = self.P, self.mem
        m0 = mem.mark()
        sc = mem.alloc([KC, 3], F32)
        bT = mem.alloc([NL, 72], F32)
        stage = Rot(mem, "adastage", 3, [KC, 512], BF16)
        scb = mem.alloc([KC, 3], BF16)
        P.dma("sp", sc, cT, W=["sc"])
        P.dma("sp", bT, ada_bT, W=["bT"])
        P.op("act", lambda e: e.activation(scb, sc, AF.Silu), R=["sc"], W=["sc"])
        for l in self.layers:
            ps, pk = self.ps_a()
            psv = ps[:, 0:216].rearrange("p (a b) -> p a b", b=3)
            for j2 in range(18):
                st, sk = stage.next()
                P.dma("pool", st, ada_w[l, :, j2 * 512:(j2 + 1) * 512].rearrange("(k p) n -> p k n", p=128), W=[sk])
                for c in range(4):
                    for kc in range(8):
                        P.op("pe", lambda e, st=st, c=c, kc=kc, j2=j2, psv=psv: e.matmul(
                            psv[:, j2 * 4 + c, :], st[:, kc, c * 128:(c + 1) * 128], scb[:, kc, :],
                            start=(kc == 0), stop=(kc == 7)), R=[sk, "sc"], W=[pk])
            P.op("dve", lambda e, l=l, psv=psv: e.tensor_tensor(
                self.mod[:, l], psv, bT[:, l].unsqueeze(2).broadcast_to([128, 72, 3]), ALU.add),
                R=[pk, "bT"], W=["mod"])
            for n in range(3):
                ln = l * 3 + n
                P.op("dve", lambda e, l=l, n=n, ln=ln: e.scalar_tensor_tensor(
                    self.Amod[:, ln], self.mod[:, l, (3 * n + 1) * 8:(3 * n + 2) * 8, :], 1.0,
                    self.normw[:, ln].unsqueeze(2).broadcast_to([128, KC, 3]), ALU.add, ALU.mult),
                    R=["mod", "normw"], W=["Amod"])
                P.op("dve", lambda e, l=l, n=n, ln=ln: e.tensor_scalar(
                    self.Gmod[:, ln], self.mod[:, l, (3 * n + 2) * 8:(3 * n + 3) * 8, :],
                    (1.0 if n == 1 else 0.5), None, ALU.mult), R=["mod"], W=["Gmod"])
        self.prologue_mixer()
        P.barrier(wait_dma=True)
        mem.reset(m0)

    def shift_ap(self, l, n, c, col):
        return self.mod[:, l, (3 * n) * 8 + c, col:col + 1]

    def prenorm(self, s, l, n, tiles, h, rots):
        P = self.P
        sqr, sdr, tmpr = rots
        hx = self.hx
        ln = l * 3 + n
        for tt in tiles:
            t0, N = TT[tt]
            col = 2 if tt == 0 else s
            sq, sqk = sqr.next()
            P.op("act", lambda e, sq=sq, t0=t0, N=N: e.activation(sq[:, :, 0:N], hx[:, :, t0:t0 + N], AF.Square),
                 R=[("hx", tt)], W=[sqk])
            ps, pk = self.ps_s()
            for kc in range(KC):
                P.op("pe", lambda e, ps=ps, sq=sq, kc=kc, N=N: e.matmul(
                    ps[:, 0:N], self.ones_bf, sq[:, kc, 0:N], start=(kc == 0), stop=(kc == KC - 1)),
                    R=[sqk, "ones"], W=[pk])
            sd, sdk = sdr.next()
            P.op("act", lambda e, sd=sd, ps=ps, N=N: e.activation(sd[:, 0:N], ps[:, 0:N], AF.Ln, bias=self.eps_ap, scale=1.0 / D),
                 R=[pk, "eps"], W=[sdk])
            P.op("act", lambda e, sd=sd, N=N: e.activation(sd[:, 0:N], sd[:, 0:N], AF.Exp, scale=-0.5), R=[sdk], W=[sdk])
            for c in range(KC):
                tmp, tk = tmpr.next()
                P.op("dve", lambda e, tmp=tmp, c=c, t0=t0, N=N, sd=sd: e.tensor_tensor(
                    tmp[:, 0:N], hx[:, c, t0:t0 + N], sd[:, 0:N], ALU.mult), R=[("hx", tt), sdk], W=[tk])
                P.op("act", lambda e, tmp=tmp, c=c, t0=t0, N=N, col=col: e.activation(
                    h[:, c, t0:t0 + N], tmp[:, 0:N], AF.Identity,
                    bias=self.shift_ap(l, n, c, col), scale=self.Amod[:, ln, c, col:col + 1]),
                    R=[tk], W=[("h", tt)])

    def ffn(self, s, l, which, tiles, ffn_wg, ffn_wu, ffn_wd):
        P, mem = self.P, self.mem
        hx = self.hx
        n = 0 if which == 0 else 2
        ln = l * 3 + n
        P.barrier()
        m0 = mem.mark()
        h = mem.alloc([KC, T], BF16)
        rots = (Rot(mem, "sq", 2, [KC, 512], BF16), Rot(mem, "sd", 2, [512], F32), Rot(mem, "ntmp", 3, [512], F32))
        sgr = Rot(mem, "sg", 3, [512], BF16)
        actr = Rot(mem, "act", 3, [2, 512], BF16)
        groups = [(f0, 2) for f0 in range(0, 22, 2)]

        def down(tt, act, actk, wd, wdk, nf):
            t0, N = TT[tt]
            col = 2 if tt == 0 else s
            for dc in range(KC):
                pd, pdk = self.ps_a()
                self.mmk(pd[:, 0:N], [(wd[:, fc, dc * 128:(dc + 1) * 128], act[:, fc, 0:N]) for fc in range(nf)], [wdk, actk], pdk)
                P.op("dve", lambda e, pd=pd, dc=dc: e.scalar_tensor_tensor(
                    hx[:, dc, t0:t0 + N], pd[:, 0:N], self.Gmod[:, ln, dc, col:col + 1], hx[:, dc, t0:t0 + N],
                    ALU.mult, ALU.add), R=[pdk, ("hx", tt)], W=[("hx", tt)])

        pend_d = None
        for gi, (f0, nf) in enumerate(groups):
            wg, wgk = self.wload(ffn_wg[l, which, gi].rearrange("p (k n) -> p k n", k=KC), (KC, 256))
            wu, wuk = self.wload(ffn_wu[l, which, gi].rearrange("p (k n) -> p k n", k=KC), (KC, 256))
            wd, wdk = self.wload(ffn_wd[l, which, f0 * 128:(f0 + nf) * 128, :].rearrange("(k p) n -> p k n", p=128), (nf, D))
            for tt in tiles:
                t0, N = TT[tt]
                col = 2 if tt == 0 else s
                if gi == 0:
                    self.prenorm(s, l, n, [tt], h, rots)
                act, actk = actr.next()
                for fc in range(nf):
                    pg, pgk = self.ps_s()
                    for kc in range(KC):
                        P.op("pe", lambda e, pg=pg, wg=wg, fc=fc, kc=kc, t0=t0, N=N: e.matmul(
                            pg[:, 0:N], wg[:, kc, fc * 128:(fc + 1) * 128], h[:, kc, t0:t0 + N],
                            start=(kc == 0), stop=(kc == KC - 1)), R=[wgk, ("h", tt)], W=[pgk])
                    pu, puk = self.ps_s()
                    for kc in range(KC):
                        P.op("pe", lambda e, pu=pu, wu=wu, fc=fc, kc=kc, t0=t0, N=N: e.matmul(
                            pu[:, 0:N], wu[:, kc, fc * 128:(fc + 1) * 128], h[:, kc, t0:t0 + N],
                            start=(kc == 0), stop=(kc == KC - 1)), R=[wuk, ("h", tt)], W=[puk])
                    sg, sgk = sgr.next()
                    P.op("act", lambda e, sg=sg, pg=pg, N=N: e.activation(sg[:, 0:N], pg[:, 0:N], AF.Silu), R=[pgk], W=[sgk])
                    P.op("dve", lambda e, act=act, fc=fc, pu=pu, sg=sg, N=N: e.tensor_tensor(
                        act[:, fc, 0:N], pu[:, 0:N], sg[:, 0:N], ALU.mult), R=[puk, sgk], W=[actk])
                if pend_d is not None:
                    down(*pend_d)
                pend_d = (tt, act, actk, wd, wdk, nf)
        down(*pend_d)
        P.barrier()
        mem.reset(m0)

    def final_out(self, s, outT):
        P, mem = self.P, self.mem
        hx = self.hx
        P.barrier()
        m0 = mem.mark()
        sqr = Rot(mem, "sq", 2, [KC, 512], BF16)
        sdr = Rot(mem, "sd", 2, [512], F32)
        for tt in [1, 2, 3, 4]:
            t0, N = TT[tt]
            sq, sqk = sqr.next()
            P.op("act", lambda e, sq=sq, t0=t0, N=N: e.activation(sq[:, :, 0:N], hx[:, :, t0:t0 + N], AF.Square),
                 R=[("hx", tt)], W=[sqk])
            ps, pk = self.ps_s()
            for kc in range(KC):
                P.op("pe", lambda e, ps=ps, sq=sq, kc=kc, N=N: e.matmul(
                    ps[:, 0:N], self.ones_bf, sq[:, kc, 0:N], start=(kc == 0), stop=(kc == KC - 1)),
                    R=[sqk, "ones"], W=[pk])
            sd, sdk = sdr.next()
            P.op("act", lambda e, sd=sd, ps=ps, N=N: e.activation(sd[:, 0:N], ps[:, 0:N], AF.Ln, bias=self.eps_ap, scale=1.0 / D),
                 R=[pk, "eps"], W=[sdk])
            P.op("act", lambda e, sd=sd, N=N: e.activation(sd[:, 0:N], sd[:, 0:N], AF.Exp, scale=-0.5), R=[sdk], W=[sdk])
            for c in range(KC):
                o, ok = self.ostage.next()
                P.op("dve", lambda e, o=o, c=c, t0=t0, N=N, sd=sd: e.scalar_tensor_tensor(
                    o[:, 0:N], hx[:, c, t0:t0 + N], self.fnormw[:, c:c + 1], sd[:, 0:N], ALU.mult, ALU.mult),
                    R=[("hx", tt), sdk, "fnormw"], W=[ok])
                P.dma("sp", outT[s, c * 128:(c + 1) * 128, t0 - CT:t0 - CT + N], o[:, 0:N], R=[ok])
        P.barrier()
        mem.reset(m0)

    def declare_mixer_inputs(self):
        self.din("rope64", [128, 2, S])
        self.din("rope32", [128, 2, S])
        self.din("ident", [128, 128])
        self.din("gmask", [128, 2, 512])
        self.din("gqa_pA", [4, 2, D, 256])
        self.din("gqa_pB", [4, D, 256])
        self.din("gqa_pV", [4, D, 64])
        self.din("gqa_w_out", [D, D])
        self.din("gqa_sinks_b", [128, 16])
        self.din("diff_pQ", [8, D, 256])
        self.din("diff_pK", [8, D, 256])
        self.din("diff_pV", [8, D, 128])
        self.din("diff_w_out", [D, D])
        self.din("diff_lam_b", [128, 4, 64])
        self.din("diff_subln_b", [128, 1])
        self.din("mla_pD", [3, D, 256])
        self.din("mla_pUQ", [16, 256, 256])
        self.din("mla_pUKV", [16, 256, 128])
        self.din("mla_w_out", [D, D])
        self.din("mla_nw", [128, 2, 2])
        self.din("hgrn_pA", [8, D, 256])
        self.din("hgrn_pB", [8, D, 256])
        self.din("hgrn_pV", [8, D, 128])
        self.din("hgrn_w_out", [D, D])
        self.din("hgrn_lbT", [128, 4, 8])
        self.din("hgrn_nw", [128, 1])
        self.din("rmask", [128, 512])
        self.din("trimask", [128, 2, 64])

    def alloc_mixer_consts(self):
        mem = self.mem
        self.eps_ap = mem.alloc([1], F32)
        self.rope = mem.alloc([2, S], BF16)
        self.ident = mem.alloc([128], BF16)
        self.gmask = mem.alloc([2, 512], BF16)
        self.sinkexp = mem.alloc([16], F32)
        self.neglam = mem.alloc([1], F32)
        self.sublnw = mem.alloc([1], F32)
        self.mla_nw = mem.alloc([2, 2], F32)
        self.lb = mem.alloc([8], F32)
        self.oml = mem.alloc([8], F32)
        self.noml = mem.alloc([8], F32)
        self.one_ap = mem.alloc([1], F32)
        self.hnw = mem.alloc([1], F32)
        self.rmask = mem.alloc([512], F32)
        self.trimask = mem.alloc([2, 64], BF16)

    def load_mixer_consts(self):
        P = self.P
        P.op("dve", lambda e: e.memset(self.eps_ap, EPS), W=["eps"])
        P.op("dve", lambda e: e.memset(self.one_ap, 1.0), W=["one"])
        P.dma("pool", self.ident, self.dram["ident"], W=["ident"])
        P.dma("pool", self.gmask, self.dram["gmask"], W=["gmask"])
        P.dma("sp", self.sinkexp, self.dram["gqa_sinks_b"], W=["sinkexp"])
        P.dma("sp", self.mla_nw, self.dram["mla_nw"], W=["mla_nw"])
        P.dma("sp", self.hnw, self.dram["hgrn_nw"], W=["hnw"])
        P.dma("sp", self.rmask, self.dram["rmask"], W=["rmask"])
        P.dma("pool", self.trimask, self.dram["trimask"], W=["trimask"])
        P.op("act", lambda e: e.activation(self.sinkexp, self.sinkexp, AF.Exp), R=["sinkexp"], W=["sinkexp"])

    def prologue_mixer(self):
        P, mem = self.P, self.mem
        d = self.dram
        lam_init = 0.8 - 0.6 * math.exp(-0.3 * 1)
        lp = mem.alloc([4, 64], F32)
        pr = mem.alloc([2, 64], F32)
        ss = mem.alloc([2], F32)
        P.dma("sp", lp, d["diff_lam_b"], W=["lp"])
        P.dma("sp", self.sublnw, d["diff_subln_b"], W=["sublnw"])
        P.op("dve", lambda e: e.tensor_tensor(pr[:, 0, :], lp[:, 0, :], lp[:, 1, :], ALU.mult), R=["lp"], W=["pr"])
        P.op("dve", lambda e: e.tensor_tensor(pr[:, 1, :], lp[:, 2, :], lp[:, 3, :], ALU.mult), R=["lp"], W=["pr"])
        P.op("dve", lambda e: e.reduce_sum(ss[:, 0:1], pr[:, 0, :], axis=AX.X), R=["pr"], W=["ss"])
        P.op("dve", lambda e: e.reduce_sum(ss[:, 1:2], pr[:, 1, :], axis=AX.X), R=["pr"], W=["ss"])
        P.op("act", lambda e: e.activation(ss, ss, AF.Exp), R=["ss"], W=["ss"])
        P.op("dve", lambda e: e.scalar_tensor_tensor(self.neglam, ss[:, 1:2], -lam_init, ss[:, 0:1], ALU.add, ALU.subtract),
             R=["ss"], W=["neglam"])
        P.op("dve", lambda e: e.tensor_scalar(self.sublnw, self.sublnw, 1.0 - lam_init, None, ALU.mult), R=["sublnw"], W=["sublnw"])
        le = mem.alloc([4, 8], F32)
        den = mem.alloc([8], F32)
        P.dma("sp", le, d["hgrn_lbT"], W=["le"])
        P.op("act", lambda e: e.activation(le, le, AF.Exp), R=["le"], W=["le"])
        P.op("dve", lambda e: e.tensor_tensor(self.lb, le[:, 1, :], le[:, 2, :], ALU.add), R=["le"], W=["lb"])
        P.op("dve", lambda e: e.tensor_tensor(den, le[:, 0, :], le[:, 3, :], ALU.add), R=["le"], W=["den"])
        P.op("dve", lambda e: e.tensor_tensor(den, den, self.lb, ALU.add), R=["den", "lb"], W=["den"])
        P.op("dve", lambda e: e.reciprocal(den, den), R=["den"], W=["den"])
        P.op("dve", lambda e: e.tensor_tensor(self.lb, self.lb, den, ALU.mult), R=["lb", "den"], W=["lb"])
        P.op("dve", lambda e: e.tensor_scalar(self.oml, self.lb, -1.0, 1.0, ALU.mult, ALU.add), R=["lb"], W=["oml"])
        P.op("dve", lambda e: e.tensor_scalar(self.noml, self.lb, 1.0, -1.0, ALU.mult, ALU.add), R=["lb"], W=["noml"])

    def mm(self, out, lt, rh, start, stop, R, W):
        self.P.op("pe", lambda e: e.matmul(out, lt, rh, start=start, stop=stop), R=R, W=W)

    def mixer_begin(self, s, l, tiles):
        P, mem = self.P, self.mem
        P.barrier()
        m0 = mem.mark()
        h = mem.alloc([KC, T], BF16)
        rots = (Rot(mem, "sq", 1, [KC, 512], BF16), Rot(mem, "sd", 2, [512], F32), Rot(mem, "ntmp", 3, [512], F32))
        self.prenorm(s, l, 1, tiles, h, rots)
        P.barrier()
        mem.reset(m0)
        h = mem.alloc([KC, T], BF16)
        return m0, h

    def mixer_end(self, m0):
        self.P.barrier()
        self.mem.reset(m0)

    def rope_evac(self, psA, pkA, psB, pkB, dst, N, tokx, W, rows=slice(0, 128)):
        P = self.P
        t1, t2 = self.t1, self.t2
        cos = self.rope[rows, 0, tokx:tokx + N]
        sin = self.rope[rows, 1, tokx:tokx + N]
        P.op("dve", lambda e: e.tensor_tensor(t1[rows, 0:N], psA[rows, 0:N], cos, ALU.mult), R=[pkA, "rope"], W=["t1"])
        P.op("dve", lambda e: e.tensor_tensor(t2[rows, 0:N], psB[rows, 0:N], sin, ALU.mult), R=[pkB, "rope"], W=["t2"])
        P.op("dve", lambda e: e.tensor_tensor(dst, t1[rows, 0:N], t2[rows, 0:N], ALU.add), R=["t1", "t2"], W=W)

    def outproj_tile(self, s, l, tt, wO, kO, oT, ok, nk, pool="a"):
        P = self.P
        hx = self.hx
        t0, N = TT[tt]
        col = 2 if tt == 0 else s
        ln = l * 3 + 1
        for dc in range(KC):
            pd, pdk = self.ps_s() if pool == "s" else self.ps_a()
            self.mmk(pd[:, 0:N], [(wO[:, kc, dc * 128:(dc + 1) * 128], oT[:, kc, 0:N]) for kc in range(nk)], [kO, ok], pdk)
            P.op("dve", lambda e, pd=pd, dc=dc: e.scalar_tensor_tensor(
                hx[:, dc, t0:t0 + N], pd[:, 0:N], self.Gmod[:, ln, dc, col:col + 1], hx[:, dc, t0:t0 + N],
                ALU.mult, ALU.add), R=[pdk, ("hx", tt)], W=[("hx", tt)])

    @staticmethod
    def tile_of_block(tb):
        return 0 if tb < 2 else 1 + (tb - 2) // 4

    def mixer(self, s, l):
        kind = l % 4
        if kind == 0:
            self.gqa(s, l)
        elif kind == 1:
            self.diffattn(s, l)
        elif kind == 2:
            self.hgrn(s, l)
        else:
            self.mla(s, l)

    def gqa(self, s, l):
        P, mem = self.P, self.mem
        tiles = [0, 1, 2, 3, 4]
        m0, h = self.mixer_begin(s, l, tiles)
        d = self.dram
        P.dma("pool", self.rope, d["rope64"], W=["rope"])
        kTs = [mem.alloc([T], BF16), mem.alloc([T], BF16)]
        vaug = mem.alloc([18, 2, 128], BF16)
        qr = Rot(mem, "qT", 2, [2, 512], BF16)
        ptr = Rot(mem, "PT", 2, [5, 512], BF16)
        self.t1 = mem.alloc([512], F32)
        self.t2 = mem.alloc([512], F32)
        otr = Rot(mem, "oT", 2, [2, 512], BF16)
        dtr = Rot(mem, "dt", 2, [128], F32)
        P.op("dve", lambda e: e.memset(vaug, 1.0), W=["vaug"])
        P.op("dve", lambda e: e.memset(kTs[0], 0.0), W=[("kT", tt) for tt in tiles])
        P.op("dve", lambda e: e.memset(kTs[1], 0.0), W=[("kT", tt) for tt in tiles])
        for j in range(4):
            wA1, k1 = self.wload(d["gqa_pA"][j, 0].rearrange("(k p) n -> p k n", p=128), (KC, 256))
            wA2, k2 = self.wload(d["gqa_pA"][j, 1].rearrange("(k p) n -> p k n", p=128), (KC, 256))
            wB, kB = self.wload(d["gqa_pB"][j].rearrange("(k p) n -> p k n", p=128), (KC, 256))
            wV, kV = self.wload(d["gqa_pV"][j].rearrange("(k p) n -> p k n", p=128), (KC, 64))
            wO, kO = self.wload(d["gqa_w_out"][j * 256:(j + 1) * 256, :].rearrange("(k p) n -> p k n", p=128), (2, D))
            for tt in tiles:
                t0, N = TT[tt]
                psA, pkA = self.ps_s()
                self.mmk(psA[:, 0:N], [(wB[:, kc, 0:128], h[:, kc, t0:t0 + N]) for kc in range(KC)], [kB, ("h", tt)], pkA)
                rws = (slice(0, 64), slice(64, 128))
                if tt == 0:
                    for i in range(2):
                        P.op("act", lambda e, psA=psA, t0=t0, N=N, i=i: e.activation(kTs[i][rws[i], t0:t0 + N], psA[rws[i], 0:N], AF.Copy),
                             R=[pkA], W=[("kT", tt)])
                else:
                    psB, pkB = self.ps_s()
                    self.mmk(psB[:, 0:N], [(wB[:, kc, 128:256], h[:, kc, t0:t0 + N]) for kc in range(KC)], [kB, ("h", tt)], pkB)
                    for i in range(2):
                        self.rope_evac(psA, pkA, psB, pkB, kTs[i][rws[i], t0:t0 + N], N, t0 - CT, [("kT", tt)], rows=rws[i])
            if self.cut <= 1:
                break
            for g4 in range(0, 18, 4):
                nb = min(4, 18 - g4)
                ps, pk = self.ps_s()
                for bi in range(nb):
                    tb = g4 + bi
                    self.mmk(ps[:, bi * 64:(bi + 1) * 64], [(h[:, kc, tb * 128:(tb + 1) * 128], wV[:, kc, :]) for kc in range(KC)],
                             [kV, ("h", self.tile_of_block(tb))], pk)
                src = ps[:, 0:nb * 64].rearrange("p (a b) -> p a b", b=64)
                P.op("act", lambda e, src=src, g4=g4, nb=nb: e.activation(vaug[:, g4:g4 + nb, 0, 0:64], src, AF.Copy), R=[pk], W=["vaug"])
                P.op("dve", lambda e, src=src, g4=g4, nb=nb: e.tensor_copy(vaug[:, g4:g4 + nb, 1, 64:128], src), R=[pk], W=["vaug"])
            if self.cut <= 2:
                break
            pend_o = None

            def make_q(tt, wA1=wA1, wA2=wA2, k1=k1, k2=k2):
                t0, N = TT[tt]
                qT, qk = qr.next()
                for qc in range(2):
                    psA, pkA = self.ps_s()
                    self.mmk(psA[:, 0:N], [(wA1[:, kc, qc * 128:(qc + 1) * 128], h[:, kc, t0:t0 + N]) for kc in range(KC)], [k1, ("h", tt)], pkA)
                    if tt == 0:
                        P.op("act", lambda e, psA=psA, qc=qc, N=N, qT=qT: e.activation(qT[:, qc, 0:N], psA[:, 0:N], AF.Copy), R=[pkA], W=[qk])
                    else:
                        psB, pkB = self.ps_s()
                        self.mmk(psB[:, 0:N], [(wA2[:, kc, qc * 128:(qc + 1) * 128], h[:, kc, t0:t0 + N]) for kc in range(KC)], [k2, ("h", tt)], pkB)
                        self.rope_evac(psA, pkA, psB, pkB, qT[:, qc, 0:N], N, t0 - CT, [qk])
                return qT, qk

            qnext = make_q(tiles[0])
            for ti, tt in enumerate(tiles):
                t0, N = TT[tt]
                nqb = N // 128
                qT, qk = qnext
                if ti + 1 < len(tiles):
                    qnext = make_q(tiles[ti + 1])
                oT, ok = otr.next()

                def issue_scores(qi, tt=tt, t0=t0, qT=qT, qk=qk):
                    tbq = (t0 + qi * 128) // 128
                    if tt == 0:
                        kbs = [(0, None), (1, None)]
                    else:
                        kbs = []
                        if tbq > 2:
                            kbs.append((tbq - 1, 0))
                        kbs.append((tbq, None))
                        if tbq < 17:
                            kbs.append((tbq + 1, 1))
                        kbs += [(0, None), (1, None)]
                    PT, ptk = ptr.next()
                    for kbi, (kb, m) in enumerate(kbs):
                        ps, pk = self.ps_s()
                        first = True
                        if m is not None:
                            self.mm(ps[:, 0:512], self.ident, self.gmask[:, m, :], True, False, [], [pk])
                            first = False
                        for hh in range(4):
                            qc = hh // 2
                            self.mm(ps[:, hh * 128:(hh + 1) * 128], kTs[hh % 2][:, kb * 128:(kb + 1) * 128],
                                    qT[:, qc, qi * 128:(qi + 1) * 128], first, hh == 3,
                                    [("kT", self.tile_of_block(kb)), qk], [pk])
                            first = False
                        P.op("act", lambda e, PT=PT, kbi=kbi, ps=ps: e.activation(PT[:, kbi, :], ps[:, 0:512], AF.Exp, scale=0.125),
                             R=[pk], W=[ptk])
                    return PT, ptk, kbs

                def consume(qi, PT, ptk, kbs, oT=oT, ok=ok, j=j):
                    po_, pok = self.ps_a()
                    for hh in range(4):
                        var = hh % 2
                        self.mmk(po_[:, hh * 128:(hh + 1) * 128],
                                 [(vaug[:, kb, var, :], PT[:, kbi, hh * 128:(hh + 1) * 128]) for kbi, (kb, m) in enumerate(kbs)],
                                 [ptk, "vaug"], pok)
                    for hh in range(4):
                        head = 4 * j + hh
                        qc = hh // 2
                        orow, drow = (slice(0, 64), slice(64, 128)) if hh % 2 == 0 else (slice(64, 128), slice(0, 64))
                        cols = slice(hh * 128, (hh + 1) * 128)
                        dt, dk = dtr.next()
                        P.op("act", lambda e, dt=dt, orow=orow, drow=drow, cols=cols, head=head, po_=po_: e.activation(
                            dt[orow, :], po_[drow, cols], AF.Identity, bias=self.sinkexp[drow, head:head + 1]), R=[pok], W=[dk])
                        P.op("dve", lambda e, dt=dt, orow=orow: e.reciprocal(dt[orow, :], dt[orow, :]), R=[dk], W=[dk])
                        P.op("dve", lambda e, dt=dt, orow=orow, cols=cols, qc=qc, qi=qi, oT=oT, po_=po_: e.tensor_tensor(
                            oT[orow, qc, qi * 128:(qi + 1) * 128], po_[orow, cols], dt[orow, :], ALU.mult), R=[pok, dk], W=[ok])

                nxt = issue_scores(0)
                for qi in range(nqb):
                    cur = nxt
                    if qi + 1 < nqb:
                        nxt = issue_scores(qi + 1)
                    consume(qi, *cur)
                if pend_o is not None:
                    self.outproj_tile(*pend_o)
                pend_o = (s, l, tt, wO, kO, oT, ok, 2, "s")
            self.outproj_tile(*pend_o)
        self.mixer_end(m0)

    def diffattn(self, s, l):
        P, mem = self.P, self.mem
        d = self.dram
        tiles = [0, 1, 2, 3, 4]
        m0, h = self.mixer_begin(s, l, tiles)
        P.dma("pool", self.rope, d["rope64"], W=["rope"])
        kTs = [mem.alloc([T], BF16), mem.alloc([T], BF16)]
        V = mem.alloc([18, 128], BF16)
        qr = Rot(mem, "qT", 2, [512], BF16)
        ptr = Rot(mem, "PT", 3, [1024], BF16)
        self.t1 = mem.alloc([512], F32)
        self.t2 = mem.alloc([512], F32)
        rr = Rot(mem, "rj", 2, [512], F32)
        tr = Rot(mem, "tj", 2, [512], F32)
        combr = Rot(mem, "comb", 2, [512], F32)
        sqb = mem.alloc([512], BF16)
        sd = mem.alloc([512], F32)
        otr = Rot(mem, "oT", 2, [1, 512], BF16)

        def finish(tt, comb, ck, wO, kO):
            t0, N = TT[tt]
            P.op("act", lambda e: e.activation(sqb[:, 0:N], comb[:, 0:N], AF.Square), R=[ck], W=["sqb"])
            ps, pk = self.ps_s()
            self.mm(ps[:, 0:N], self.ones_bf, sqb[:, 0:N], True, True, ["sqb"], [pk])
            P.op("act", lambda e: e.activation(sd[:, 0:N], ps[:, 0:N], AF.Ln, bias=self.eps_ap, scale=1.0 / 128), R=[pk], W=["sd"])
            P.op("act", lambda e: e.activation(sd[:, 0:N], sd[:, 0:N], AF.Exp, scale=-0.5), R=["sd"], W=["sd"])
            P.op("dve", lambda e: e.tensor_tensor(comb[:, 0:N], comb[:, 0:N], sd[:, 0:N], ALU.mult), R=[ck, "sd"], W=[ck])
            oT, ok = otr.next()
            P.op("act", lambda e: e.activation(oT[:, 0, 0:N], comb[:, 0:N], AF.Identity, scale=self.sublnw[:, 0:1]), R=[ck], W=[ok])
            self.outproj_tile(s, l, tt, wO, kO, oT, ok, 1, pool="s")

        P.op("dve", lambda e: e.memset(kTs[0], 0.0), W=[("kT", tt) for tt in tiles])
        P.op("dve", lambda e: e.memset(kTs[1], 0.0), W=[("kT", tt) for tt in tiles])
        rws = (slice(0, 64), slice(64, 128))
        for hd in range(8):
            wQ, kQ = self.wload(d["diff_pQ"][hd].rearrange("(k p) n -> p k n", p=128), (KC, 256))
            wK, kK = self.wload(d["diff_pK"][hd].rearrange("(k p) n -> p k n", p=128), (KC, 256))
            wV, kV = self.wload(d["diff_pV"][hd].rearrange("(k p) n -> p k n", p=128), (KC, 128))
            wO, kO = self.wload(d["diff_w_out"][hd * 128:(hd + 1) * 128, :].rearrange("(k p) n -> p k n", p=128), (1, D))
            for tt in tiles:
                t0, N = TT[tt]
                psA, pkA = self.ps_s()
                self.mmk(psA[:, 0:N], [(wK[:, kc, 0:128], h[:, kc, t0:t0 + N]) for kc in range(KC)], [kK, ("h", tt)], pkA)
                if tt == 0:
                    for i in range(2):
                        P.op("act", lambda e, psA=psA, t0=t0, N=N, i=i: e.activation(kTs[i][rws[i], t0:t0 + N], psA[rws[i], 0:N], AF.Copy),
                             R=[pkA], W=[("kT", tt)])
                else:
                    psB, pkB = self.ps_s()
                    self.mmk(psB[:, 0:N], [(wK[:, kc, 128:256], h[:, kc, t0:t0 + N]) for kc in range(KC)], [kK, ("h", tt)], pkB)
                    for i in range(2):
                        self.rope_evac(psA, pkA, psB, pkB, kTs[i][rws[i], t0:t0 + N], N, t0 - CT, [("kT", tt)], rows=rws[i])
            for g4 in range(0, 18, 4):
                nb = min(4, 18 - g4)
                ps, pk = self.ps_s()
                for bi in range(nb):
                    tb = g4 + bi
                    self.mmk(ps[:, bi * 128:(bi + 1) * 128], [(h[:, kc, tb * 128:(tb + 1) * 128], wV[:, kc, :]) for kc in range(KC)],
                             [kV, ("h", self.tile_of_block(tb))], pk)
                src = ps[:, 0:nb * 128].rearrange("p (a b) -> p a b", b=128)
                P.op("act", lambda e, src=src, g4=g4, nb=nb: e.activation(V[:, g4:g4 + nb, :], src, AF.Copy), R=[pk], W=["V"])
            pend_o = None

            def make_q(tt, wQ=wQ, kQ=kQ):
                t0, N = TT[tt]
                qT, qk = qr.next()
                psA, pkA = self.ps_s()
                self.mmk(psA[:, 0:N], [(wQ[:, kc, 0:128], h[:, kc, t0:t0 + N]) for kc in range(KC)], [kQ, ("h", tt)], pkA)
                if tt == 0:
                    P.op("act", lambda e, psA=psA, N=N, qT=qT: e.activation(qT[:, 0:N], psA[:, 0:N], AF.Copy), R=[pkA], W=[qk])
                else:
                    psB, pkB = self.ps_s()
                    self.mmk(psB[:, 0:N], [(wQ[:, kc, 128:256], h[:, kc, t0:t0 + N]) for kc in range(KC)], [kQ, ("h", tt)], pkB)
                    self.rope_evac(psA, pkA, psB, pkB, qT[:, 0:N], N, t0 - CT, [qk])
                return qT, qk

            qnext = make_q(tiles[0])
            for ti, tt in enumerate(tiles):
                t0, N = TT[tt]
                qT, qk = qnext
                if ti + 1 < len(tiles):
                    qnext = make_q(tiles[ti + 1])
                kbs = [0, 1] if tt == 0 else list(range(18))
                tjs = []
                for j in range(2):
                    O, Ok = self.ps_a()
                    Dn, Dk = self.ps_a()
                    def issue_pair(pr, j=j, qT=qT, qk=qk, N=N):
                        dps, dk = self.ps_s2()
                        for hf, kb in enumerate(pr):
                            self.mm(dps[:, hf * 512:hf * 512 + N], kTs[j][:, kb * 128:(kb + 1) * 128], qT[:, 0:N], True, True,
                                    [("kT", self.tile_of_block(kb)), qk], [dk[hf]])
                        PT, ptk = ptr.next()
                        P.op("act", lambda e, PT=PT, dps=dps, N=N: e.activation(
                            PT.rearrange("p (a b) -> p a b", a=2)[:, :, 0:N], dps.rearrange("p (a b) -> p a b", a=2)[:, :, 0:N],
                            AF.Exp, scale=0.125), R=[dk[0], dk[1]], W=[ptk])
                        return PT, ptk
                    pairs = [kbs[i:i + 2] for i in range(0, len(kbs), 2)]
                    pend = [issue_pair(pairs[0])]
                    nkb = len(kbs)
                    for pi, pr in enumerate(pairs):
                        PT, ptk = pend[pi]
                        if pi + 1 < len(pairs):
                            pend.append(issue_pair(pairs[pi + 1]))
                        for hf, kb in enumerate(pr):
                            kbi = 2 * pi + hf
                            self.mm(O[:, 0:N], V[:, kb, :], PT[:, hf * 512:hf * 512 + N], kbi == 0, kbi == nkb - 1, [ptk, "V"], [Ok])
                            self.mm(Dn[:, 0:N], self.ones_bf, PT[:, hf * 512:hf * 512 + N], kbi == 0, kbi == nkb - 1, [ptk], [Dk])
                    rj, rk = rr.next()
                    tj, tk = tr.next()
                    P.op("dve", lambda e, rj=rj, Dn=Dn, N=N: e.reciprocal(rj[:, 0:N], Dn[:, 0:N]), R=[Dk], W=[rk])
                    P.op("dve", lambda e, tj=tj, O=O, rj=rj, N=N: e.tensor_tensor(tj[:, 0:N], O[:, 0:N], rj[:, 0:N], ALU.mult), R=[Ok, rk], W=[tk])
                    tjs.append((tj, tk))
                comb, ck = combr.next()
                P.op("dve", lambda e, tjs=tjs, N=N, comb=comb: e.scalar_tensor_tensor(
                    comb[:, 0:N], tjs[1][0][:, 0:N], self.neglam[:, 0:1], tjs[0][0][:, 0:N], ALU.mult, ALU.add),
                    R=[tjs[0][1], tjs[1][1]], W=[ck])
                if pend_o is not None:
                    finish(*pend_o)
                pend_o = (tt, comb, ck, wO, kO)
            finish(*pend_o)
        self.mixer_end(m0)

    def hgrn(self, s, l):
        P, mem = self.P, self.mem
        d = self.dram
        tiles = [0, 1, 2, 3, 4]
        m0, h = self.mixer_begin(s, l, tiles)
        Vtok = mem.alloc([36, 128], BF16)
        oacc = mem.alloc([T], F32)
        m1 = mem.mark()
        r64 = slice(0, 64)
        for hd in range(8):
            wA, kA = self.wload(d["hgrn_pA"][hd].rearrange("(k p) n -> p k n", p=128), (KC, 256))
            wB, kB = self.wload(d["hgrn_pB"][hd].rearrange("(k p) n -> p k n", p=128), (KC, 256))
            wV, kV = self.wload(d["hgrn_pV"][hd].rearrange("(k p) n -> p k n", p=128), (KC, 128))
            wO, kO = self.wload(d["hgrn_w_out"][hd * 128:(hd + 1) * 128, :].rearrange("(k p) n -> p k n", p=128), (1, D))
            for g4 in range(0, 36, 4):
                ps, pk = self.ps_s()
                for bi in range(4):
                    ch = g4 + bi
                    self.mmk(ps[r64, bi * 128:(bi + 1) * 128], [(h[:, kc, ch * 64:(ch + 1) * 64], wV[:, kc, :]) for kc in range(KC)],
                             [kV, ("h", self.tile_of_block(ch // 2))], pk)
                src = ps[r64, 0:512].rearrange("p (a b) -> p a b", b=128)
                P.op("act", lambda e, src=src, g4=g4: e.activation(Vtok[r64, g4:g4 + 4, :], src, AF.Copy), R=[pk], W=["Vtok"])
            qs_all = self.rope.rearrange("p a b -> p (a b)")[:, 0:T]
            for tt in tiles:
                t0, N = TT[tt]
                ps, pk = self.ps_s()
                self.mmk(ps[:, 0:N], [(wA[:, kc, 0:128], h[:, kc, t0:t0 + N]) for kc in range(KC)], [kA, ("h", tt)], pk)
                P.op("act", lambda e, ps=ps, t0=t0, N=N: e.activation(qs_all[:, t0:t0 + N], ps[:, 0:N], AF.Silu), R=[pk], W=["rope"])
            HT = [(i * 256, 256) for i in range(9)]
            P.op("dve", lambda e: e.memset(oacc, 0.0), W=[("oacc", i) for i in range(9)])

            def chain(dr):
                NN = 256
                F_ = mem.alloc([NN], F32)
                LF = mem.alloc([NN], F32)
                Pg = mem.alloc([NN], F32)
                D1, D4 = F_, LF
                E3 = mem.alloc([NN], F32)
                tot = mem.alloc([4], F32)
                kk = mem.alloc([NN], BF16)
                E1 = mem.alloc([NN], BF16)
                E2 = mem.alloc([NN], BF16)
                E4 = E1
                QG = mem.alloc([NN], BF16)
                KG = mem.alloc([NN], BF16)
                QE = mem.alloc([NN], BF16)
                KE = mem.alloc([NN], BF16)
                AT4 = mem.alloc([4, 64], BF16)
                KE4 = mem.alloc([4, 128], BF16)
                srot = Rot(mem, "S%d" % dr, 3, [128], F32)
                sbr = Rot(mem, "Sbf%d" % dr, 3, [128], BF16)
                yield
                K_ = lambda nm: (nm, dr)
                mi, li = (32, 63) if dr == 0 else (31, 0)
                order = list(range(9)) if dr == 0 else [0] + list(range(8, 0, -1))
                Sc, Sck = srot.next()
                P.op("dve", lambda e, Sc=Sc: e.memset(Sc, 0.0), W=[Sck])
                sb, sbk = sbr.next()
                P.op("dve", lambda e, sb=sb: e.memset(sb, 0.0), W=[sbk])
                v3 = lambda a: a[:, 0:NN].rearrange("p (c t) -> p c t", t=64)
                nch = 4
                for ti in order:
                    t0, N = HT[ti]
                    hk = ("h", 0 if ti == 0 else 1 + (ti - 1) // 2)
                    qs = qs_all[:, t0:t0 + NN]
                    ps, pk = self.ps_s()
                    wf, kf, c0 = (wA, kA, 128) if dr == 0 else (wB, kB, 0)
                    self.mmk(ps[:, 0:N], [(wf[:, kc, c0:c0 + 128], h[:, kc, t0:t0 + N]) for kc in range(KC)], [kf, hk], pk)
                    P.op("act", lambda e, ps=ps: e.activation(F_, ps[:, 0:NN], AF.Exp, scale=-1.0), R=[pk], W=[K_("F")])
                    yield
                    P.op("act", lambda e: e.activation(Pg, F_, AF.Ln, bias=self.one_ap), R=[K_("F")], W=[K_("Pg")])
                    yield
                    P.op("act", lambda e, hd=hd: e.activation(LF, F_, AF.Ln, bias=self.one_ap, scale=self.lb[:, hd:hd + 1]), R=[K_("F")], W=[K_("LF")])
                    yield
                    P.op("act", lambda e: e.activation(F_, Pg, AF.Exp, scale=-1.0), R=[K_("Pg")], W=[K_("F")])
                    P.op("dve", lambda e: e.tensor_tensor(LF, LF, Pg, ALU.subtract), R=[K_("LF"), K_("Pg")], W=[K_("LF")])
                    yield
                    P.op("dve", lambda e, hd=hd: e.tensor_scalar(kk, F_, self.noml[:, hd:hd + 1], self.oml[:, hd:hd + 1], ALU.mult, ALU.add),
                         R=[K_("F")], W=[K_("kk")])
                    yield
                    P.op("dve", lambda e: e.tensor_tensor_scan(Pg, self.rmask[:, 0:NN], LF, 0.0, ALU.mult, ALU.add), R=[K_("LF")], W=[K_("Pg")])
                    yield
                    if dr == 1:
                        P.op("dve", lambda e: e.tensor_copy(tot[:, 0:nch], v3(Pg)[:, :, 63]), R=[K_("Pg")], W=[K_("tot")])
                        P.op("dve", lambda e: e.tensor_tensor(Pg, LF, Pg, ALU.subtract), R=[K_("LF"), K_("Pg")], W=[K_("Pg")])
                        yield
                        P.op("dve", lambda e: e.tensor_tensor(v3(Pg), v3(Pg), tot[:, 0:nch].unsqueeze(2).broadcast_to([128, nch, 64]), ALU.add),
                             R=[K_("Pg"), K_("tot")], W=[K_("Pg")])
                        yield
                    P.op("dve", lambda e: e.tensor_tensor(v3(D1), v3(Pg), v3(Pg)[:, :, mi:mi + 1].broadcast_to([128, nch, 64]), ALU.subtract),
                         R=[K_("Pg")], W=[K_("F")])
                    P.op("act", lambda e: e.activation(E3, Pg, AF.Exp), R=[K_("Pg")], W=[K_("E3")])
                    yield
                    P.op("dve", lambda e: e.tensor_tensor(v3(D4), v3(Pg)[:, :, li:li + 1].broadcast_to([128, nch, 64]), v3(Pg), ALU.subtract),
                         R=[K_("Pg")], W=[K_("LF")])
                    P.op("act", lambda e: e.activation(E1, D1, AF.Exp), R=[K_("F")], W=[K_("E1")])
                    yield
                    P.op("dve", lambda e, qs=qs: e.tensor_tensor(QE, qs, E3, ALU.mult), R=["rope", K_("E3")], W=[K_("QE")])
                    P.op("act", lambda e: e.activation(E2, D1, AF.Exp, scale=-1.0), R=[K_("F")], W=[K_("E2")])
                    yield
                    P.op("dve", lambda e, qs=qs: e.tensor_tensor(QG, qs, E1, ALU.mult), R=["rope", K_("E1")], W=[K_("QG")])
                    P.op("act", lambda e: e.activation(E4, D4, AF.Exp), R=[K_("LF")], W=[K_("E1")])
                    yield
                    P.op("dve", lambda e: e.tensor_tensor(KG, kk, E2, ALU.mult), R=[K_("kk"), K_("E2")], W=[K_("KG")])
                    yield
                    P.op("dve", lambda e: e.tensor_tensor(KE, kk, E4, ALU.mult), R=[K_("kk"), K_("E1")], W=[K_("KE")])
                    yield
                    corder = list(range(nch)) if dr == 0 else list(range(nch - 1, -1, -1))
                    psA, pkA = self.ps_s()
                    for ci in range(nch):
                        cs = slice(ci * 64, (ci + 1) * 64)
                        self.mm(psA[r64, cs], KG[:, cs], QG[:, cs], True, True, [K_("KG"), K_("QG")], [pkA])
                    pst, pstk = self.ps_s()
                    pstb = pst[:, 0:256].bitcast(BF16)
                    for ci in range(nch):
                        cs = slice(ci * 64, (ci + 1) * 64)
                        P.op("pe", lambda e, pstb=pstb, cs=cs, ci=ci: e.transpose(pstb[r64, ci * 128:(ci + 1) * 128], KE[:, cs], self.ident),
                             R=[K_("KE")], W=[pstk])
                    yield
                    P.op("dve", lambda e, psA=psA: e.tensor_tensor(
                        AT4[r64, :, :], psA[r64, 0:256].rearrange("p (c t) -> p c t", t=64),
                        self.trimask[r64, dr, :].unsqueeze(1).broadcast_to([64, nch, 64]), ALU.mult), R=[pkA], W=[K_("AT")])
                    P.op("act", lambda e, pstb=pstb: e.activation(KE4[r64, :, :], pstb[r64, 0:512].rearrange("p (c t) -> p c t", t=128), AF.Copy),
                         R=[pstk], W=[K_("KEt")])
                    yield
                    pS, pSk = self.ps_a()
                    for ci in range(nch):
                        ch = t0 // 64 + ci
                        self.mm(pS[:, ci * 128:(ci + 1) * 128], KE4[r64, ci, :], Vtok[r64, ch, :], True, True, [K_("KEt"), "Vtok"], [pSk])
                    yield
                    O, Ok = self.ps_a()
                    for ci in corder:
                        ch = t0 // 64 + ci
                        cs = slice(ci * 64, (ci + 1) * 64)
                        self.mm(O[:, cs], Vtok[r64, ch, :], AT4[r64, ci, :], True, False, ["Vtok", K_("AT")], [Ok])
                        self.mm(O[:, cs], sb, QE[:, cs], False, True, [sbk, K_("QE")], [Ok])
                        di = ci * 64 + li
                        Sn, Snk = srot.next()
                        P.op("dve", lambda e, pS=pS, ci=ci, di=di, Sn=Sn, Sc=Sc: e.scalar_tensor_tensor(
                            Sn, Sc, E3[:, di:di + 1], pS[:, ci * 128:(ci + 1) * 128], ALU.mult, ALU.add),
                            R=[Sck, K_("E3"), pSk], W=[Snk])
                        Sc, Sck = Sn, Snk
                        sb, sbk = sbr.next()
                        P.op("act", lambda e, sb=sb, Sc=Sc: e.activation(sb, Sc, AF.Copy), R=[Sck], W=[sbk])
                        yield
                    P.op("dve", lambda e, O=O, t0=t0: e.tensor_tensor(oacc[:, t0:t0 + NN], O[:, 0:NN], oacc[:, t0:t0 + NN], ALU.add),
                         R=[Ok, ("oacc", ti)], W=[("oacc", ti)])
                    yield

            gens = [chain(0), chain(1)]
            alive = list(gens)
            while alive:
                for g in list(alive):
                    try:
                        next(g)
                    except StopIteration:
                        alive.remove(g)
            P.barrier()
            mem.reset(m1)
            gs_all = mem.alloc([T], BF16)
            sqr = Rot(mem, "sqb", 2, [512], BF16)
            sdr = Rot(mem, "sd", 2, [512], F32)
            tmr = Rot(mem, "tmp", 2, [512], F32)
            otr = Rot(mem, "oT", 2, [1, 512], BF16)
            for tt in tiles:
                t0, N = TT[tt]
                ps, pk = self.ps_s()
                self.mmk(ps[:, 0:N], [(wB[:, kc, 128:256], h[:, kc, t0:t0 + N]) for kc in range(KC)], [kB, ("h", tt)], pk)
                P.op("act", lambda e, ps=ps, t0=t0, N=N: e.activation(gs_all[:, t0:t0 + N], ps[:, 0:N], AF.Silu), R=[pk], W=[("gs", tt)])
            pend_o = None
            for tt in tiles:
                t0, N = TT[tt]
                sqb, sqk = sqr.next()
                sd, sdk = sdr.next()
                tmp, tmk = tmr.next()
                P.op("act", lambda e, sqb=sqb, t0=t0, N=N: e.activation(sqb[:, 0:N], oacc[:, t0:t0 + N], AF.Square), R=[], W=[sqk])
                ps2, pk2 = self.ps_s()
                self.mm(ps2[:, 0:N], self.ones_bf, sqb[:, 0:N], True, True, [sqk], [pk2])
                P.op("act", lambda e, sd=sd, ps2=ps2, N=N: e.activation(sd[:, 0:N], ps2[:, 0:N], AF.Ln, bias=self.eps_ap, scale=1.0 / 128), R=[pk2], W=[sdk])
                P.op("act", lambda e, sd=sd, N=N: e.activation(sd[:, 0:N], sd[:, 0:N], AF.Exp, scale=-0.5), R=[sdk], W=[sdk])
                P.op("dve", lambda e, tmp=tmp, sd=sd, t0=t0, N=N: e.tensor_tensor(tmp[:, 0:N], oacc[:, t0:t0 + N], sd[:, 0:N], ALU.mult), R=[sdk], W=[tmk])
                oT, ok = otr.next()
                P.op("dve", lambda e, oT=oT, tmp=tmp, t0=t0, N=N: e.scalar_tensor_tensor(
                    oT[:, 0, 0:N], tmp[:, 0:N], self.hnw[:, 0:1], gs_all[:, t0:t0 + N], ALU.mult, ALU.mult),
                    R=[tmk, ("gs", tt)], W=[ok])
                if pend_o is not None:
                    self.outproj_tile(*pend_o)
                pend_o = (s, l, tt, wO, kO, oT, ok, 1)
            self.outproj_tile(*pend_o)
            P.barrier()
            mem.reset(m1)
        self.mixer_end(m0)

    def mla(self, s, l):
        P, mem = self.P, self.mem
        d = self.dram
        tiles = [0, 1, 2, 3, 4]
        xt = [1, 2, 3, 4]
        P.barrier()
        m0 = mem.mark()
        cqT = mem.alloc([2, T], BF16)
        ckvT = mem.alloc([2, T], BF16)
        krT = mem.alloc([T], BF16)
        m1 = mem.mark()
        h = mem.alloc([KC, T], BF16)
        rots = (Rot(mem, "sq", 1, [KC, 512], BF16), Rot(mem, "sd", 2, [512], F32), Rot(mem, "ntmp", 3, [512], F32))
        self.prenorm(s, l, 1, tiles, h, rots)
        P.barrier()
        mem.reset(m1)
        h = mem.alloc([KC, T], BF16)
        P.dma("pool", self.rope, d["rope32"], W=["rope"])
        self.t1 = mem.alloc([512], F32)
        self.t2 = mem.alloc([512], F32)
        sq2 = mem.alloc([2, 512], BF16)
        sdr = Rot(mem, "sd", 2, [512], F32)
        wD1, kD1 = self.wload(d["mla_pD"][0].rearrange("(k p) n -> p k n", p=128), (KC, 256))
        wD2, kD2 = self.wload(d["mla_pD"][1].rearrange("(k p) n -> p k n", p=128), (KC, 256))
        wD3, kD3 = self.wload(d["mla_pD"][2].rearrange("(k p) n -> p k n", p=128), (KC, 256))
        r96 = slice(64, 96)
        for tt in tiles:
            t0, N = TT[tt]
            for (wD, kD, dstT, nwi) in ((wD1, kD1, cqT, 0), (wD2, kD2, ckvT, 1)):
                pcs = []
                for c in range(2):
                    ps, pk = self.ps_s()
                    self.mmk(ps[:, 0:N], [(wD[:, kc, c * 128:(c + 1) * 128], h[:, kc, t0:t0 + N]) for kc in range(KC)], [kD, ("h", tt)], pk)
                    P.op("act", lambda e, ps=ps, c=c, N=N: e.activation(sq2[:, c, 0:N], ps[:, 0:N], AF.Square), R=[pk], W=["sq2"])
                    pcs.append((ps, pk))
                pss, pssk = self.ps_s()
                self.mmk(pss[:, 0:N], [(self.ones_bf, sq2[:, c, 0:N]) for c in range(2)], ["sq2"], pssk)
                sd, sdk = sdr.next()
                P.op("act", lambda e, sd=sd, pss=pss, N=N: e.activation(sd[:, 0:N], pss[:, 0:N], AF.Ln, bias=self.eps_ap, scale=1.0 / 256), R=[pssk], W=[sdk])
                P.op("act", lambda e, sd=sd, N=N: e.activation(sd[:, 0:N], sd[:, 0:N], AF.Exp, scale=-0.5), R=[sdk], W=[sdk])
                for c in range(2):
                    ps, pk = pcs[c]
                    P.op("dve", lambda e, ps=ps, c=c, sd=sd, dstT=dstT, nwi=nwi, t0=t0, N=N: e.scalar_tensor_tensor(
                        dstT[:, c, t0:t0 + N], ps[:, 0:N], self.mla_nw[:, nwi, c:c + 1], sd[:, 0:N], ALU.mult, ALU.mult),
                        R=[pk, sdk], W=[("lora", nwi, tt)])
            psA, pkA = self.ps_s()
            self.mmk(psA[:, 0:N], [(wD3[:, kc, 0:128], h[:, kc, t0:t0 + N]) for kc in range(KC)], [kD3, ("h", tt)], pkA)
            if tt == 0:
                P.op("act", lambda e, psA=psA, t0=t0, N=N: e.activation(krT[r96, t0:t0 + N], psA[r96, 0:N], AF.Copy), R=[pkA], W=[("krT", tt)])
            else:
                psB, pkB = self.ps_s()
                self.mmk(psB[:, 0:N], [(wD3[:, kc, 128:256], h[:, kc, t0:t0 + N]) for kc in range(KC)], [kD3, ("h", tt)], pkB)
                self.rope_evac(psA, pkA, psB, pkB, krT[r96, t0:t0 + N], N, t0 - CT, [("krT", tt)], rows=r96)
        P.barrier()
        mem.reset(m1)
        self.t1 = mem.alloc([512], F32)
        self.t2 = mem.alloc([512], F32)
        kh = [mem.alloc([T], BF16), mem.alloc([T], BF16)]
        vaug = [mem.alloc([18, 128], BF16), mem.alloc([18, 128], BF16)]
        qr = Rot(mem, "qT", 2, [512], BF16)
        ptr = Rot(mem, "PT", 3, [1024], BF16)
        dtr = Rot(mem, "dt", 2, [512], F32)
        otr = Rot(mem, "oT", 2, [1, 512], BF16)
        scale = 96.0 ** -0.5
        for i in range(2):
            P.op("dve", lambda e, i=i: e.memset(kh[i], 0.0), W=[("kh", i)])
            P.op("dve", lambda e, i=i: e.memset(vaug[i], 1.0), W=[("vaug", i)])
        for q in qr.bufs:
            P.op("dve", lambda e, q=q: e.memset(q, 0.0), W=[])
        P.barrier()
        for pr in range(8):
            wO, kO = self.wload(d["mla_w_out"][pr * 128:(pr + 1) * 128, :].rearrange("(k p) n -> p k n", p=128), (1, D))
            wq_, wkv_ = [], []
            for hh in range(2):
                hd = 2 * pr + hh
                wq_.append(self.wload(d["mla_pUQ"][hd].rearrange("(k p) n -> p k n", p=128), (2, 256)))
                wkv_.append(self.wload(d["mla_pUKV"][hd].rearrange("(k p) n -> p k n", p=128), (2, 128)))
            for hh in range(2):
                wkv, kkv = wkv_[hh]
                vcols = slice(0, 64) if hh == 0 else slice(64, 128)
                for tt in tiles:
                    t0, N = TT[tt]
                    ps, pk = self.ps_s()
                    self.mmk(ps[0:64, 0:N], [(wkv[:, kc, 0:64], ckvT[:, kc, t0:t0 + N]) for kc in range(2)], [kkv, ("lora", 1, tt)], pk)
                    P.op("act", lambda e, ps=ps, hh=hh, t0=t0, N=N: e.activation(kh[hh][0:64, t0:t0 + N], ps[0:64, 0:N], AF.Copy), R=[pk], W=[("kh", hh)])
                P.op("act", lambda e, hh=hh: e.activation(kh[hh][r96, :], krT[r96, :], AF.Copy), R=[("krT", tt) for tt in tiles], W=[("kh", hh)])
                for g8 in range(0, 18, 8):
                    nb = min(8, 18 - g8)
                    ps, pk = self.ps_s()
                    for bi in range(nb):
                        tb = g8 + bi
                        self.mmk(ps[:, bi * 64:(bi + 1) * 64], [(ckvT[:, kc, tb * 128:(tb + 1) * 128], wkv[:, kc, 64:128]) for kc in range(2)],
                                 [kkv, ("lora", 1, self.tile_of_block(tb))], pk)
                    src = ps[:, 0:nb * 64].rearrange("p (a b) -> p a b", b=64)
                    P.op("act", lambda e, src=src, g8=g8, nb=nb, hh=hh, vcols=vcols: e.activation(vaug[hh][:, g8:g8 + nb, vcols], src, AF.Copy),
                         R=[pk], W=[("vaug", hh)])
            pend_o = None

            def make_q(tt, hh, wq_=wq_):
                t0, N = TT[tt]
                wq, kq = wq_[hh]
                qT, qk = qr.next()
                psA, pkA = self.ps_s()
                self.mmk(psA[0:96, 0:N], [(wq[:, kc, 0:96], cqT[:, kc, t0:t0 + N]) for kc in range(2)], [kq, ("lora", 0, tt)], pkA)
                psB, pkB = self.ps_s()
                self.mmk(psB[0:96, 0:N], [(wq[:, kc, 128:224], cqT[:, kc, t0:t0 + N]) for kc in range(2)], [kq, ("lora", 0, tt)], pkB)
                P.op("act", lambda e, qT=qT, psA=psA, N=N: e.activation(qT[0:64, 0:N], psA[0:64, 0:N], AF.Copy), R=[pkA], W=[qk])
                self.rope_evac(psA, pkA, psB, pkB, qT[r96, 0:N], N, t0 - CT, [qk], rows=r96)
                return qT, qk

            items = [(tt, hh) for tt in xt for hh in range(2)]
            qnext = make_q(*items[0])
            for tt in xt:
                t0, N = TT[tt]
                oT, ok = otr.next()
                for hh in range(2):
                    orow, drow = (slice(0, 64), slice(64, 128)) if hh == 0 else (slice(64, 128), slice(0, 64))
                    qT, qk = qnext
                    ii = items.index((tt, hh))
                    if ii + 1 < len(items):
                        qnext = make_q(*items[ii + 1])
                    O, Ok = self.ps_a()
                    def issue_pair(kb0, hh=hh, qT=qT, qk=qk, N=N):
                        dps, dk2 = self.ps_s2()
                        for hf in range(2):
                            kb = kb0 + hf
                            self.mm(dps[:, hf * 512:hf * 512 + N], kh[hh][:, kb * 128:(kb + 1) * 128], qT[:, 0:N], True, True, [("kh", hh), qk], [dk2[hf]])
                        PT, ptk = ptr.next()
                        P.op("act", lambda e, PT=PT, dps=dps, N=N: e.activation(PT[:, 0:1024], dps[:, 0:1024], AF.Exp, scale=scale),
                             R=[dk2[0], dk2[1]], W=[ptk])
                        return PT, ptk
                    pend = [issue_pair(0)]
                    for pi in range(9):
                        PT, ptk = pend[pi]
                        if pi + 1 < 9:
                            pend.append(issue_pair(2 * (pi + 1)))
                        for hf in range(2):
                            kb = 2 * pi + hf
                            self.mm(O[:, 0:N], vaug[hh][:, kb, :], PT[:, hf * 512:hf * 512 + N], kb == 0, kb == 17, [ptk, ("vaug", hh)], [Ok])
                    dt, dk = dtr.next()
                    P.op("act", lambda e, dt=dt, O=O, orow=orow, drow=drow, N=N: e.activation(dt[orow, 0:N], O[drow, 0:N], AF.Copy), R=[Ok], W=[dk])
                    P.op("dve", lambda e, dt=dt, orow=orow, N=N: e.reciprocal(dt[orow, 0:N], dt[orow, 0:N]), R=[dk], W=[dk])
                    P.op("dve", lambda e, dt=dt, O=O, orow=orow, oT=oT, N=N: e.tensor_tensor(oT[orow, 0, 0:N], O[orow, 0:N], dt[orow, 0:N], ALU.mult),
                         R=[Ok, dk], W=[ok])
                if pend_o is not None:
                    self.outproj_tile(*pend_o)
                pend_o = (s, l, tt, wO, kO, oT, ok, 1)
            self.outproj_tile(*pend_o)
        self.mixer_end(m0)


def host_inputs(inputs, core):
    b0 = 2 * core
    x = inputs["x"]
    ctx = inputs["ctx"]
    hT = np.empty((2, D, T), np.float32)
    for i in range(2):
        hT[i, :, :CT] = ctx[b0 + i].T
        hT[i, :, CT:] = x[b0 + i].T
    cvec = np.stack([inputs["c"][b0], inputs["c"][b0 + 1], inputs["c_ctx"]], axis=1)
    cT = np.ascontiguousarray(cvec.reshape(KC, 128, 3).transpose(1, 0, 2))
    m = {"hT": hT, "cT": cT}
    return m


def rope_tables(rot_dim):
    rows = S // 64
    row = np.repeat(np.arange(rows, dtype=np.float32), 64)
    col = np.tile(np.arange(64, dtype=np.float32), rows)
    axis_dim = rot_dim // 2
    inv_freq = (np.float32(10000.0) ** (-np.arange(0, axis_dim, 2, dtype=np.float32) / np.float32(axis_dim))).astype(np.float32)
    ang_r = row[:, None] * inv_freq[None, :]
    ang_c = col[:, None] * inv_freq[None, :]
    ang = np.concatenate([ang_r, ang_r, ang_c, ang_c], axis=-1).astype(np.float32)
    f = rot_dim // 4
    sgn = np.concatenate([-np.ones(f), np.ones(f), -np.ones(f), np.ones(f)]).astype(np.float32)
    perm = np.concatenate([np.arange(f, 2 * f), np.arange(0, f), np.arange(3 * f, 4 * f), np.arange(2 * f, 3 * f)])
    return np.cos(ang).astype(np.float32), (np.sin(ang) * sgn[None, :]).astype(np.float32), perm


def host_consts():
    c = {}
    cos, sins, perm64 = rope_tables(64)
    r = np.zeros((128, 2, S), np.float32)
    r[:, 0, :] = np.tile(cos.T, (2, 1))
    r[:, 1, :] = np.tile(sins.T, (2, 1))
    c["rope64"] = r
    cos, sins, perm32 = rope_tables(32)
    r = np.zeros((128, 2, S), np.float32)
    r[64:96, 0, :] = cos.T
    r[64:96, 1, :] = sins.T
    c["rope32"] = r
    c["ident"] = np.eye(128, dtype=np.float32)
    NEG = -30000.0
    kk = np.arange(128)[:, None]
    qq = np.arange(128)[None, :]
    gm = np.zeros((128, 2, 512), np.float32)
    gm[:, 0, :] = np.tile(np.where(qq <= kk, 0.0, NEG), (1, 4))
    gm[:, 1, :] = np.tile(np.where(kk <= qq, 0.0, NEG), (1, 4))
    c["gmask"] = gm
    return c, perm64, perm32


def host_shared(inputs):
    sh, perm64, perm32 = host_consts()
    w_in = inputs["gqa_w_in"][0]
    hp = (np.arange(16)[:, None] * 64 + perm64[None, :]).reshape(-1)
    wq, wqp = w_in[:, :1024], w_in[:, :1024][:, hp]
    pA = np.empty((4, 2, D, 256), np.float32)
    pB = np.empty((4, D, 256), np.float32)
    pV = np.empty((4, D, 64), np.float32)
    for j in range(4):
        pA[j, 0] = wq[:, j * 256:(j + 1) * 256]
        pA[j, 1] = wqp[:, j * 256:(j + 1) * 256]
        kj = w_in[:, 1024 + j * 64:1024 + (j + 1) * 64]
        kjp = kj[:, perm64]
        pB[j] = np.concatenate([kj, kj, kjp, kjp], axis=1)
        pV[j] = w_in[:, 1280 + j * 64:1280 + (j + 1) * 64]
    sh["gqa_pA"], sh["gqa_pB"], sh["gqa_pV"] = pA, pB, pV
    sh["gqa_w_out"] = np.ascontiguousarray(inputs["gqa_w_out"][0])
    w_in = inputs["diff_w_in"][0]
    sp = (np.arange(16)[:, None] * 64 + perm64[None, :]).reshape(-1)
    wq, wk, wv = w_in[:, :1024], w_in[:, 1024:2048], w_in[:, 2048:]
    wqp, wkp = wq[:, sp], wk[:, sp]
    sh["diff_pQ"] = np.stack([np.concatenate([wq[:, i * 128:(i + 1) * 128], wqp[:, i * 128:(i + 1) * 128]], axis=1) for i in range(8)])
    sh["diff_pK"] = np.stack([np.concatenate([wk[:, i * 128:(i + 1) * 128], wkp[:, i * 128:(i + 1) * 128]], axis=1) for i in range(8)])
    sh["diff_pV"] = np.stack([wv[:, i * 128:(i + 1) * 128] for i in range(8)])
    sh["diff_w_out"] = np.ascontiguousarray(inputs["diff_w_out"][0])
    sh["diff_lam_b"] = np.ascontiguousarray(np.broadcast_to(inputs["diff_lambda"][0][None], (128, 4, 64)))
    sh["diff_subln_b"] = np.ascontiguousarray(inputs["diff_subln_w"][0].reshape(128, 1))
    wd = inputs["mla_w_down"][0]
    pD = np.zeros((3, D, 256), np.float32)
    pD[0] = wd[:, 0:256]
    pD[1] = wd[:, 256:512]
    pD[2][:, 64:96] = wd[:, 512:544]
    pD[2][:, 128 + 64:128 + 96] = wd[:, 512:544][:, perm32]
    sh["mla_pD"] = pD
    wuq = inputs["mla_w_uq"][0]
    pUQ = np.zeros((16, 256, 256), np.float32)
    for hd in range(16):
        blk = wuq[:, hd * 96:(hd + 1) * 96]
        pUQ[hd][:, 0:96] = blk
        pUQ[hd][:, 128:128 + 64] = blk[:, 0:64]
        pUQ[hd][:, 128 + 64:128 + 96] = blk[:, 64:96][:, perm32]
    sh["mla_pUQ"] = pUQ
    sh["mla_pUKV"] = np.ascontiguousarray(inputs["mla_w_ukv"][0].reshape(256, 16, 128).transpose(1, 0, 2))
    sh["mla_w_out"] = np.ascontiguousarray(inputs["mla_w_out"][0])
    nw = np.stack([inputs["mla_q_norm_w"][0].reshape(2, 128).T, inputs["mla_kv_norm_w"][0].reshape(2, 128).T], axis=1)
    sh["mla_nw"] = np.ascontiguousarray(nw)
    w_in = inputs["hgrn_w_in"][0]
    cq, cf, cb, cv, cg = (w_in[:, i * 1024:(i + 1) * 1024] for i in range(5))
    sh["hgrn_pA"] = np.stack([np.concatenate([cq[:, i * 128:(i + 1) * 128], cf[:, i * 128:(i + 1) * 128]], axis=1) for i in range(8)])
    sh["hgrn_pB"] = np.stack([np.concatenate([cb[:, i * 128:(i + 1) * 128], cg[:, i * 128:(i + 1) * 128]], axis=1) for i in range(8)])
    sh["hgrn_pV"] = np.stack([cv[:, i * 128:(i + 1) * 128] for i in range(8)])
    sh["hgrn_w_out"] = np.ascontiguousarray(inputs["hgrn_w_out"][0])
    sh["hgrn_lbT"] = np.ascontiguousarray(inputs["hgrn_lower_bounds"].reshape(4, 8, 128).transpose(2, 0, 1))
    sh["hgrn_nw"] = np.ascontiguousarray(inputs["hgrn_norm_w"][0].reshape(128, 1))
    rm = np.ones((128, 512), np.float32)
    rm[:, ::64] = 0.0
    sh["rmask"] = rm
    ss_, tt_ = np.arange(64)[:, None], np.arange(64)[None, :]
    tm = np.zeros((128, 2, 64), np.float32)
    tm[0:64, 0, :] = (ss_ <= tt_)
    tm[0:64, 1, :] = (ss_ >= tt_)
    sh["trimask"] = tm
    sh["gqa_sinks_b"] = np.ascontiguousarray(np.broadcast_to(inputs["gqa_sinks"][0][None, :], (128, 16)))
    sh["ada_w"] = np.ascontiguousarray(inputs["ada_w"])
    sh["ada_bT"] = np.ascontiguousarray(inputs["ada_b"].reshape(NL, 72, 128).transpose(2, 0, 1))
    sh["norm_wT"] = np.ascontiguousarray(inputs["norm_w"].reshape(NL * 3, KC, 128).transpose(2, 0, 1))
    sh["fnorm_wT"] = np.ascontiguousarray(inputs["final_norm_w"].reshape(KC, 128).T)
    for nm, src in (("ffn_wg_l", "ffn_w_gate"), ("ffn_wu_l", "ffn_w_up")):
        w = inputs[src].reshape(NL, 2, KC, 128, 11, 256)
        sh[nm] = np.ascontiguousarray(w.transpose(0, 1, 4, 3, 2, 5)).reshape(NL, 2, 11, 128, KC * 256)
    sh["ffn_w_down"] = np.ascontiguousarray(inputs["ffn_w_down"])
    return sh


def kernel(**inputs):
    inputs = {k: np.asarray(v) for k, v in inputs.items()}
    b = Builder()
    nc = b.build()
    sh = host_shared(inputs)
    in_maps = []
    for core in range(NCORES):
        m = dict(sh)
        m.update(host_inputs(inputs, core))
        in_maps.append({k: v for k, v in m.items() if k in b.dram})
    res = run_bass_kernel_spmd(nc, in_maps, core_ids=list(range(NCORES)))
    out = np.empty((16, S, D), np.float32)
    for core in range(NCORES):
        o = res.results[core]["outT"]
        for i in range(2):
            out[2 * core + i] = o[i].T
    return out
```

```python
import math
import contextlib
import numpy as np
import concourse.bass as bass
import concourse.mybir as mybir
from concourse.bass_utils import run_bass_kernel_spmd

F32 = mybir.dt.float32
BF16 = mybir.dt.bfloat16
AF = mybir.ActivationFunctionType
ALU = mybir.AluOpType
AX = mybir.AxisListType

ENGS = ("pe", "act", "dve", "pool", "sp")

D = 1024
KC = 8
S = 2048
CT = 256
T = CT + S
DFF = 2816
NL = 4
EPS = 1e-6
NCORES = 8
TT = [(0, 256), (256, 512), (768, 512), (1280, 512), (1792, 512)]


class Ev:
    __slots__ = ("eng", "need_inc", "semval", "dma_sem", "dma_val", "idx")

    def __init__(self, eng):
        self.eng = eng
        self.idx = -1
        self.need_inc = False
        self.semval = None
        self.dma_sem = None
        self.dma_val = None


class Rec:
    __slots__ = ("fn", "deps", "ev", "is_dma")

    def __init__(self, fn, deps, ev, is_dma=False):
        self.fn = fn
        self.deps = deps
        self.ev = ev
        self.is_dma = is_dma


class Prog:
    def __init__(self, nc, n_dma_sems=40):
        self.nc = nc
        self.streams = {e: [] for e in ENGS}
        self.last_w = {}
        self.readers = {}
        self.last_ev = {e: None for e in ENGS}
        self.n_dma_sems = n_dma_sems
        self.dma_cnt = 0
        self.dma_last = [None] * n_dma_sems
        self.dma_vals = [0] * n_dma_sems
        self.n_ops = 0

    def _collect(self, eng, R, W, is_dma):
        best = {}

        def add(e):
            if e is None:
                return
            k = ("d", e.dma_sem) if e.dma_sem is not None else ("e", e.eng)
            o = best.get(k)
            if o is None or (e.dma_val if e.dma_sem is not None else e.idx) > (o.dma_val if o.dma_sem is not None else o.idx):
                best[k] = e

        for k in R:
            add(self.last_w.get(k))
        for k in W:
            w = self.last_w.get(k)
            if w is not None and (w.eng != eng or w.dma_sem is not None or is_dma):
                add(w)
            for r in self.readers.get(k, ()):
                if r.eng != eng or r.dma_sem is not None or is_dma:
                    add(r)
        return list(best.values())

    def _update(self, ev, R, W):
        for k in R:
            self.readers.setdefault(k, []).append(ev)
        for k in W:
            self.last_w[k] = ev
            self.readers[k] = []

    def op(self, eng, fn, R=(), W=()):
        ev = Ev(eng)
        deps = self._collect(eng, R, W, False)
        for d in deps:
            if d.dma_sem is None:
                d.need_inc = True
        ev.idx = len(self.streams[eng])
        self.streams[eng].append(Rec(fn, deps, ev))
        self._update(ev, R, W)
        self.last_ev[eng] = ev
        self.n_ops += 1
        return ev

    def dma(self, eng, out, in_, R=(), W=(), **kw):
        ev = Ev(eng)
        slot = self.dma_cnt % self.n_dma_sems
        self.dma_cnt += 1
        deps = self._collect(eng, R, W, True)
        prev = self.dma_last[slot]
        if prev is not None:
            deps.append(prev)
        for d in deps:
            if d.dma_sem is None:
                d.need_inc = True
        self.dma_vals[slot] += 16
        ev.dma_sem = slot
        ev.dma_val = self.dma_vals[slot]
        self.dma_last[slot] = ev
        ev.idx = len(self.streams[eng])
        self.streams[eng].append(
            Rec(lambda e, o=out, i=in_, k=kw: e.dma_start(out=o, in_=i, **k), deps, ev, True))
        self._update(ev, R, W)
        self.n_ops += 1
        return ev

    def barrier(self, wait_dma=False):
        lasts = [self.last_ev[e] for e in ENGS if self.last_ev[e] is not None]
        for l in lasts:
            l.need_inc = True
        dmas = [d for d in self.dma_last if d is not None] if wait_dma else []
        for e in ENGS:
            deps = [l for l in lasts if l.eng != e] + dmas
            if deps:
                self.streams[e].append(Rec(None, deps, None))
        if wait_dma:
            self.last_w = {}
            self.readers = {}
        else:
            self.last_w = {k: v for k, v in self.last_w.items() if v.dma_sem is not None}
            rd = {k: [r for r in v if r.dma_sem is not None] for k, v in self.readers.items()}
            self.readers = {k: v for k, v in rd.items() if v}

    def wait_all_dma(self, eng="sp"):
        deps = [d for d in self.dma_last if d is not None]
        self.streams[eng].append(Rec(None, deps, None))

    def emit(self):
        nc = self.nc
        with contextlib.ExitStack() as st:
            sems = {e: st.enter_context(nc.semaphore("s_" + e)) for e in ENGS}
            dsems = [st.enter_context(nc.semaphore("d%d" % i)) for i in range(self.n_dma_sems)]
            for e in ENGS:
                c = 0
                for r in self.streams[e]:
                    if r.ev is not None and r.ev.dma_sem is None and r.ev.need_inc:
                        c += 1
                        r.ev.semval = c
            block = st.enter_context(nc.Block())

            def run(engname, engobj):
                waited = {}
                for r in self.streams[engname]:
                    for d in r.deps:
                        if d.dma_sem is not None:
                            key, sem, val = ("d", d.dma_sem), dsems[d.dma_sem], d.dma_val
                        else:
                            key, sem, val = ("e", d.eng), sems[d.eng], d.semval
                        if waited.get(key, 0) >= val:
                            continue
                        waited[key] = val
                        engobj.wait_ge(sem, val)
                    if r.fn is None:
                        continue
                    ins = r.fn(engobj)
                    if r.is_dma:
                        ins.then_inc(dsems[r.ev.dma_sem], 16)
                    elif r.ev.need_inc:
                        ins.then_inc(sems[engname], 1)

            @block.tensor
            def _(eng):
                run("pe", eng)

            @block.scalar
            def _(eng):
                run("act", eng)

            @block.vector
            def _(eng):
                run("dve", eng)

            @block.gpsimd
            def _(eng):
                run("pool", eng)

            @block.sync
            def _(eng):
                run("sp", eng)


class Mem:
    def __init__(self, nc, total_bytes):
        self.n = total_bytes // 4
        self.t = nc.alloc_sbuf_tensor("sbuf_all", [128, self.n], F32)
        self.off = 0
        self.hi = 0

    def alloc(self, shape, dt):
        n = 1
        for s in shape:
            n *= s
        nb = n * (4 if dt == F32 else 2)
        nf = ((nb + 31) // 32) * 8
        assert self.off + nf <= self.n, ("SBUF overflow", self.off * 4, nf * 4, self.n * 4)
        a = self.t[:, self.off:self.off + nf]
        self.off += nf
        self.hi = max(self.hi, self.off)
        if dt != F32:
            a = a.bitcast(dt)
        a = a[:, 0:n]
        if len(shape) == 2:
            a = a.rearrange("p (a b) -> p a b", a=shape[0])
        elif len(shape) == 3:
            a = a.rearrange("p (a b c) -> p a b c", a=shape[0], b=shape[1])
        return a

    def mark(self):
        return self.off

    def reset(self, m):
        self.off = m


class Rot:
    def __init__(self, mem, name, n, shape, dt):
        self.bufs = [mem.alloc(shape, dt) for _ in range(n)]
        self.name = name
        self.i = 0

    def next(self):
        j = self.i % len(self.bufs)
        self.i += 1
        return self.bufs[j], (self.name, j)


class Builder:
    def __init__(self, debug=None, layers=(0, 1, 2, 3), n_seq=2, stop_after=None):
        self.debug = debug or []
        self.layers = list(layers)
        self.n_seq = n_seq
        self.stop_after = stop_after
        self.cut = 99
        nc = self.nc = bass.Bass("TRN2", target_bir_lowering=False)
        self.P = Prog(nc)
        self.dram = {}
        self.dbg_out = {}

    def din(self, name, shape, dt=F32):
        t = self.nc.dram_tensor(name, list(shape), dt, kind="ExternalInput").ap()
        self.dram[name] = t
        return t

    def dout(self, name, shape, dt=F32):
        t = self.nc.dram_tensor(name, list(shape), dt, kind="ExternalOutput").ap()
        self.dram[name] = t
        return t

    def ps_s(self):
        i = self._ps_s % 4
        self._ps_s += 1
        return self.psb[i], ("ps", i)

    def ps_s2(self):
        if self._ps_s % 2 == 1:
            self._ps_s += 1
        i = self._ps_s % 4
        self._ps_s += 2
        return self.psd[i // 2], (("ps", i), ("ps", i + 1))

    def ps_a(self):
        i = 4 + self._ps_a % 4
        self._ps_a += 1
        return self.psb[i], ("ps", i)

    def wload(self, view, shape):
        a, b = shape
        assert a * b <= 2048, (a, b)
        s = self._wslot % len(self.wring)
        self._wslot += 1
        dst = self.wring[s][:, 0:a * b].rearrange("p (a b) -> p a b", a=a)
        key = ("w", s)
        self.P.dma("pool", dst, view, W=[key])
        return dst, key

    def mmk(self, out, pairs, R, pk):
        n = len(pairs)
        for i, (lt, rh) in enumerate(pairs):
            self.P.op("pe", lambda e, out=out, lt=lt, rh=rh, i=i, n=n: e.matmul(out, lt, rh, start=(i == 0), stop=(i == n - 1)),
                      R=R, W=[pk])

    def dump(self, name, ap, n, R):
        if name not in self.debug:
            return
        o = self.dout("dbg_" + name, [128, n])
        self.P.dma("sp", o, ap, R=R)

    def build(self):
        nc, P = self.nc, self.P
        hT = self.din("hT", [2, D, T])
        cT = self.din("cT", [128, KC, 3])
        ada_w = self.din("ada_w", [NL, D, 9 * D])
        ada_bT = self.din("ada_bT", [128, NL, 72])
        norm_wT = self.din("norm_wT", [128, NL * 3, KC])
        fnorm_wT = self.din("fnorm_wT", [128, KC])
        ffn_wg = self.din("ffn_wg_l", [NL, 2, 11, 128, KC * 256])
        ffn_wu = self.din("ffn_wu_l", [NL, 2, 11, 128, KC * 256])
        ffn_wd = self.din("ffn_w_down", [NL, 2, DFF, D])
        outT = self.dout("outT", [2, D, S])
        self.declare_mixer_inputs()

        mem = self.mem = Mem(nc, 206 * 1024)
        self.psd = [nc.alloc_psum_tensor("psd%d" % i, [128, 1024], F32) for i in range(2)]
        self.psb = [self.psd[i // 2][:, (i % 2) * 512:(i % 2 + 1) * 512] for i in range(4)]
        self.psb += [nc.alloc_psum_tensor("ps%d" % i, [128, 512], F32) for i in range(4, 8)]
        self._ps_s = 0
        self._ps_a = 0
        self._wslot = 0
        hx = self.hx = mem.alloc([KC, T], F32)
        self.wring = [mem.alloc([2048], BF16) for _ in range(8)]
        self.mod = mem.alloc([NL, 72, 3], F32)
        self.Amod = mem.alloc([NL * 3, KC, 3], F32)
        self.Gmod = mem.alloc([NL * 3, KC, 3], F32)
        self.normw = mem.alloc([NL * 3, KC], F32)
        self.fnormw = mem.alloc([KC], F32)
        self.ones_bf = mem.alloc([128], BF16)
        self.ostage = Rot(mem, "ostage", 2, [512], F32)
        self.alloc_mixer_consts()
        self.arena0 = mem.mark()

        P.op("dve", lambda e: e.memset(self.ones_bf, 1.0), W=["ones"])
        P.dma("sp", self.normw, norm_wT, W=["normw"])
        P.dma("sp", self.fnormw, fnorm_wT, W=["fnormw"])
        self.load_mixer_consts()

        self.prologue_mod(cT, ada_w, ada_bT)

        for s in range(self.n_seq):
            P.dma("sp", hx, hT[s].rearrange("(c p) t -> p c t", p=128), W=[("hx", tt) for tt in range(5)])
            for l in self.layers:
                last = (l == NL - 1)
                self.ffn(s, l, 0, [0, 1, 2, 3, 4], ffn_wg, ffn_wu, ffn_wd)
                if self.stop_after == ("ffn0", l):
                    break
                self.mixer(s, l)
                if self.stop_after == ("mixer", l):
                    break
                self.ffn(s, l, 1, [1, 2, 3, 4] if last else [0, 1, 2, 3, 4], ffn_wg, ffn_wu, ffn_wd)
            if "hx" in self.debug and s == 0:
                o = self.dout("dbg_hx", [128, KC, T])
                P.dma("sp", o, hx, R=[("hx", tt) for tt in range(5)])
            self.final_out(s, outT)
        P.wait_all_dma("sp")
        P.emit()
        return nc

    def prologue_mod(self, cT, ada_w, ada_bT):
        P, mem = self.P, self.mem
        m0 = mem.mark()
        sc = mem.alloc([KC, 3], F32)
        bT = mem.alloc([NL, 72], F32)
        stage = Rot(mem, "adastage", 3, [KC, 512], BF16)
        scb = mem.alloc([KC, 3], BF16)
        P.dma("sp", sc, cT, W=["sc"])
        P.dma("sp", bT, ada_bT, W=["bT"])
        P.op("act", lambda e: e.activation(scb, sc, AF.Silu), R=["sc"], W=["sc"])
        for l in self.layers:
            ps, pk = self.ps_a()
            psv = ps[:, 0:216].rearrange("p (a b) -> p a b", b=3)
            for j2 in range(18):
                st, sk = stage.next()
                P.dma("pool", st, ada_w[l, :, j2 * 512:(j2 + 1) * 512].rearrange("(k p) n -> p k n", p=128), W=[sk])
                for c in range(4):
                    for kc in range(8):
                        P.op("pe", lambda e, st=st, c=c, kc=kc, j2=j2, psv=psv: e.matmul(
                            psv[:, j2 * 4 + c, :], st[:, kc, c * 128:(c + 1) * 128], scb[:, kc, :],
                            start=(kc == 0), stop=(kc == 7)), R=[sk, "sc"], W=[pk])
            P.op("dve", lambda e, l=l, psv=psv: e.tensor_tensor(
                self.mod[:, l], psv, bT[:, l].unsqueeze(2).broadcast_to([128, 72, 3]), ALU.add),
                R=[pk, "bT"], W=["mod"])
            for n in range(3):
                ln = l * 3 + n
                P.op("dve", lambda e, l=l, n=n, ln=ln: e.scalar_tensor_tensor(
                    self.Amod[:, ln], self.mod[:, l, (3 * n + 1) * 8:(3 * n + 2) * 8, :], 1.0,
                    self.normw[:, ln].unsqueeze(2).broadcast_to([128, KC, 3]), ALU.add, ALU.mult),
                    R=["mod", "normw"], W=["Amod"])
                P.op("dve", lambda e, l=l, n=n, ln=ln: e.tensor_scalar(
                    self.Gmod[:, ln], self.mod[:, l, (3 * n + 2) * 8:(3 * n + 3) * 8, :],
                    (1.0 if n == 1 else 0.5), None, ALU.mult), R=["mod"], W=["Gmod"])
        self.prologue_mixer()
        P.barrier(wait_dma=True)
        mem.reset(m0)

    def shift_ap(self, l, n, c, col):
        return self.mod[:, l, (3 * n) * 8 + c, col:col + 1]

    def prenorm(self, s, l, n, tiles, h, rots):
        P = self.P
        sqr, sdr, tmpr = rots
        hx = self.hx
        ln = l * 3 + n
        for tt in tiles:
            t0, N = TT[tt]
            col = 2 if tt == 0 else s
            sq, sqk = sqr.next()
            P.op("act", lambda e, sq=sq, t0=t0, N=N: e.activation(sq[:, :, 0:N], hx[:, :, t0:t0 + N], AF.Square),
                 R=[("hx", tt)], W=[sqk])
            ps, pk = self.ps_s()
            for kc in range(KC):
                P.op("pe", lambda e, ps=ps, sq=sq, kc=kc, N=N: e.matmul(
                    ps[:, 0:N], self.ones_bf, sq[:, kc, 0:N], start=(kc == 0), stop=(kc == KC - 1)),
                    R=[sqk, "ones"], W=[pk])
            sd, sdk = sdr.next()
            P.op("act", lambda e, sd=sd, ps=ps, N=N: e.activation(sd[:, 0:N], ps[:, 0:N], AF.Ln, bias=self.eps_ap, scale=1.0 / D),
                 R=[pk, "eps"], W=[sdk])
            P.op("act", lambda e, sd=sd, N=N: e.activation(sd[:, 0:N], sd[:, 0:N], AF.Exp, scale=-0.5), R=[sdk], W=[sdk])
            for c in range(KC):
                tmp, tk = tmpr.next()
                P.op("dve", lambda e, tmp=tmp, c=c, t0=t0, N=N, sd=sd: e.tensor_tensor(
                    tmp[:, 0:N], hx[:, c, t0:t0 + N], sd[:, 0:N], ALU.mult), R=[("hx", tt), sdk], W=[tk])
                P.op("act", lambda e, tmp=tmp, c=c, t0=t0, N=N, col=col: e.activation(
                    h[:, c, t0:t0 + N], tmp[:, 0:N], AF.Identity,
                    bias=self.shift_ap(l, n, c, col), scale=self.Amod[:, ln, c, col:col + 1]),
                    R=[tk], W=[("h", tt)])

    def ffn(self, s, l, which, tiles, ffn_wg, ffn_wu, ffn_wd):
        P, mem = self.P, self.mem
        hx = self.hx
        n = 0 if which == 0 else 2
        ln = l * 3 + n
        P.barrier()
        m0 = mem.mark()
        h = mem.alloc([KC, T], BF16)
        rots = (Rot(mem, "sq", 2, [KC, 512], BF16), Rot(mem, "sd", 2, [512], F32), Rot(mem, "ntmp", 3, [512], F32))
        sgr = Rot(mem, "sg", 3, [512], BF16)
        actr = Rot(mem, "act", 3, [2, 512], BF16)
        groups = [(f0, 2) for f0 in range(0, 22, 2)]

        def down(tt, act, actk, wd, wdk, nf):
            t0, N = TT[tt]
            col = 2 if tt == 0 else s
            for dc in range(KC):
                pd, pdk = self.ps_a()
                self.mmk(pd[:, 0:N], [(wd[:, fc, dc * 128:(dc + 1) * 128], act[:, fc, 0:N]) for fc in range(nf)], [wdk, actk], pdk)
                P.op("dve", lambda e, pd=pd, dc=dc: e.scalar_tensor_tensor(
                    hx[:, dc, t0:t0 + N], pd[:, 0:N], self.Gmod[:, ln, dc, col:col + 1], hx[:, dc, t0:t0 + N],
                    ALU.mult, ALU.add), R=[pdk, ("hx", tt)], W=[("hx", tt)])

        pend_d = None
        for gi, (f0, nf) in enumerate(groups):
            wg, wgk = self.wload(ffn_wg[l, which, gi].rearrange("p (k n) -> p k n", k=KC), (KC, 256))
            wu, wuk = self.wload(ffn_wu[l, which, gi].rearrange("p (k n) -> p k n", k=KC), (KC, 256))
            wd, wdk = self.wload(ffn_wd[l, which, f0 * 128:(f0 + nf) * 128, :].rearrange("(k p) n -> p k n", p=128), (nf, D))
            for tt in tiles:
                t0, N = TT[tt]
                col = 2 if tt == 0 else s
                if gi == 0:
                    self.prenorm(s, l, n, [tt], h, rots)
                act, actk = actr.next()
                for fc in range(nf):
                    pg, pgk = self.ps_s()
                    for kc in range(KC):
                        P.op("pe", lambda e, pg=pg, wg=wg, fc=fc, kc=kc, t0=t0, N=N: e.matmul(
                            pg[:, 0:N], wg[:, kc, fc * 128:(fc + 1) * 128], h[:, kc, t0:t0 + N],
                            start=(kc == 0), stop=(kc == KC - 1)), R=[wgk, ("h", tt)], W=[pgk])
                    pu, puk = self.ps_s()
                    for kc in range(KC):
                        P.op("pe", lambda e, pu=pu, wu=wu, fc=fc, kc=kc, t0=t0, N=N: e.matmul(
                            pu[:, 0:N], wu[:, kc, fc * 128:(fc + 1) * 128], h[:, kc, t0:t0 + N],
                            start=(kc == 0), stop=(kc == KC - 1)), R=[wuk, ("h", tt)], W=[puk])
                    sg, sgk = sgr.next()
                    P.op("act", lambda e, sg=sg, pg=pg, N=N: e.activation(sg[:, 0:N], pg[:, 0:N], AF.Silu), R=[pgk], W=[sgk])
                    P.op("dve", lambda e, act=act, fc=fc, pu=pu, sg=sg, N=N: e.tensor_tensor(
                        act[:, fc, 0:N], pu[:, 0:N], sg[:, 0:N], ALU.mult), R=[puk, sgk], W=[actk])
                if pend_d is not None:
                    down(*pend_d)
                pend_d = (tt, act, actk, wd, wdk, nf)
        down(*pend_d)
        P.barrier()
        mem.reset(m0)

    def final_out(self, s, outT):
        P, mem = self.P, self.mem
        hx = self.hx
        P.barrier()
        m0 = mem.mark()
        sqr = Rot(mem, "sq", 2, [KC, 512], BF16)
        sdr = Rot(mem, "sd", 2, [512], F32)
        for tt in [1, 2, 3, 4]:
            t0, N = TT[tt]
            sq, sqk = sqr.next()
            P.op("act", lambda e, sq=sq, t0=t0, N=N: e.activation(sq[:, :, 0:N], hx[:, :, t0:t0 + N], AF.Square),
                 R=[("hx", tt)], W=[sqk])
            ps, pk = self.ps_s()
            for kc in range(KC):
                P.op("pe", lambda e, ps=ps, sq=sq, kc=kc, N=N: e.matmul(
                    ps[:, 0:N], self.ones_bf, sq[:, kc, 0:N], start=(kc == 0), stop=(kc == KC - 1)),
                    R=[sqk, "ones"], W=[pk])
            sd, sdk = sdr.next()
            P.op("act", lambda e, sd=sd, ps=ps, N=N: e.activation(sd[:, 0:N], ps[:, 0:N], AF.Ln, bias=self.eps_ap, scale=1.0 / D),
                 R=[pk, "eps"], W=[sdk])
            P.op("act", lambda e, sd=sd, N=N: e.activation(sd[:, 0:N], sd[:, 0:N], AF.Exp, scale=-0.5), R=[sdk], W=[sdk])
            for c in range(KC):
                o, ok = self.ostage.next()
                P.op("dve", lambda e, o=o, c=c, t0=t0, N=N, sd=sd: e.scalar_tensor_tensor(
                    o[:, 0:N], hx[:, c, t0:t0 + N], self.fnormw[:, c:c + 1], sd[:, 0:N], ALU.mult, ALU.mult),
                    R=[("hx", tt), sdk, "fnormw"], W=[ok])
                P.dma("sp", outT[s, c * 128:(c + 1) * 128, t0 - CT:t0 - CT + N], o[:, 0:N], R=[ok])
        P.barrier()
        mem.reset(m0)

    def declare_mixer_inputs(self):
        self.din("rope64", [128, 2, S])
        self.din("rope32", [128, 2, S])
        self.din("ident", [128, 128])
        self.din("gmask", [128, 2, 512])
        self.din("gqa_pA", [4, 2, D, 256])
        self.din("gqa_pB", [4, D, 256])
        self.din("gqa_pV", [4, D, 64])
        self.din("gqa_w_out", [D, D])
        self.din("gqa_sinks_b", [128, 16])
        self.din("diff_pQ", [8, D, 256])
        self.din("diff_pK", [8, D, 256])
        self.din("diff_pV", [8, D, 128])
        self.din("diff_w_out", [D, D])
        self.din("diff_lam_b", [128, 4, 64])
        self.din("diff_subln_b", [128, 1])
        self.din("mla_pD", [3, D, 256])
        self.din("mla_pUQ", [16, 256, 256])
        self.din("mla_pUKV", [16, 256, 128])
        self.din("mla_w_out", [D, D])
        self.din("mla_nw", [128, 2, 2])
        self.din("hgrn_pA", [8, D, 256])
        self.din("hgrn_pB", [8, D, 256])
        self.din("hgrn_pV", [8, D, 128])
        self.din("hgrn_w_out", [D, D])
        self.din("hgrn_lbT", [128, 4, 8])
        self.din("hgrn_nw", [128, 1])
        self.din("rmask", [128, 512])
        self.din("trimask", [128, 2, 64])

    def alloc_mixer_consts(self):
        mem = self.mem
        self.eps_ap = mem.alloc([1], F32)
        self.rope = mem.alloc([2, S], BF16)
        self.ident = mem.alloc([128], BF16)
        self.gmask = mem.alloc([2, 512], BF16)
        self.sinkexp = mem.alloc([16], F32)
        self.neglam = mem.alloc([1], F32)
        self.sublnw = mem.alloc([1], F32)
        self.mla_nw = mem.alloc([2, 2], F32)
        self.lb = mem.alloc([8], F32)
        self.oml = mem.alloc([8], F32)
        self.noml = mem.alloc([8], F32)
        self.one_ap = mem.alloc([1], F32)
        self.hnw = mem.alloc([1], F32)
        self.rmask = mem.alloc([512], F32)
        self.trimask = mem.alloc([2, 64], BF16)

    def load_mixer_consts(self):
        P = self.P
        P.op("dve", lambda e: e.memset(self.eps_ap, EPS), W=["eps"])
        P.op("dve", lambda e: e.memset(self.one_ap, 1.0), W=["one"])
        P.dma("pool", self.ident, self.dram["ident"], W=["ident"])
        P.dma("pool", self.gmask, self.dram["gmask"], W=["gmask"])
        P.dma("sp", self.sinkexp, self.dram["gqa_sinks_b"], W=["sinkexp"])
        P.dma("sp", self.mla_nw, self.dram["mla_nw"], W=["mla_nw"])
        P.dma("sp", self.hnw, self.dram["hgrn_nw"], W=["hnw"])
        P.dma("sp", self.rmask, self.dram["rmask"], W=["rmask"])
        P.dma("pool", self.trimask, self.dram["trimask"], W=["trimask"])
        P.op("act", lambda e: e.activation(self.sinkexp, self.sinkexp, AF.Exp), R=["sinkexp"], W=["sinkexp"])

    def prologue_mixer(self):
        P, mem = self.P, self.mem
        d = self.dram
        lam_init = 0.8 - 0.6 * math.exp(-0.3 * 1)
        lp = mem.alloc([4, 64], F32)
        pr = mem.alloc([2, 64], F32)
        ss = mem.alloc([2], F32)
        P.dma("sp", lp, d["diff_lam_b"], W=["lp"])
        P.dma("sp", self.sublnw, d["diff_subln_b"], W=["sublnw"])
        P.op("dve", lambda e: e.tensor_tensor(pr[:, 0, :], lp[:, 0, :], lp[:, 1, :], ALU.mult), R=["lp"], W=["pr"])
        P.op("dve", lambda e: e.tensor_tensor(pr[:, 1, :], lp[:, 2, :], lp[:, 3, :], ALU.mult), R=["lp"], W=["pr"])
        P.op("dve", lambda e: e.reduce_sum(ss[:, 0:1], pr[:, 0, :], axis=AX.X), R=["pr"], W=["ss"])
        P.op("dve", lambda e: e.reduce_sum(ss[:, 1:2], pr[:, 1, :], axis=AX.X), R=["pr"], W=["ss"])
        P.op("act", lambda e: e.activation(ss, ss, AF.Exp), R=["ss"], W=["ss"])
        P.op("dve", lambda e: e.scalar_tensor_tensor(self.neglam, ss[:, 1:2], -lam_init, ss[:, 0:1], ALU.add, ALU.subtract),
             R=["ss"], W=["neglam"])
        P.op("dve", lambda e: e.tensor_scalar(self.sublnw, self.sublnw, 1.0 - lam_init, None, ALU.mult), R=["sublnw"], W=["sublnw"])
        le = mem.alloc([4, 8], F32)
        den = mem.alloc([8], F32)
        P.dma("sp", le, d["hgrn_lbT"], W=["le"])
        P.op("act", lambda e: e.activation(le, le, AF.Exp), R=["le"], W=["le"])
        P.op("dve", lambda e: e.tensor_tensor(self.lb, le[:, 1, :], le[:, 2, :], ALU.add), R=["le"], W=["lb"])
        P.op("dve", lambda e: e.tensor_tensor(den, le[:, 0, :], le[:, 3, :], ALU.add), R=["le"], W=["den"])
        P.op("dve", lambda e: e.tensor_tensor(den, den, self.lb, ALU.add), R=["den", "lb"], W=["den"])
        P.op("dve", lambda e: e.reciprocal(den, den), R=["den"], W=["den"])
        P.op("dve", lambda e: e.tensor_tensor(self.lb, self.lb, den, ALU.mult), R=["lb", "den"], W=["lb"])
        P.op("dve", lambda e: e.tensor_scalar(self.oml, self.lb, -1.0, 1.0, ALU.mult, ALU.add), R=["lb"], W=["oml"])
        P.op("dve", lambda e: e.tensor_scalar(self.noml, self.lb, 1.0, -1.0, ALU.mult, ALU.add), R=["lb"], W=["noml"])

    def mm(self, out, lt, rh, start, stop, R, W):
        self.P.op("pe", lambda e: e.matmul(out, lt, rh, start=start, stop=stop), R=R, W=W)

    def mixer_begin(self, s, l, tiles):
        P, mem = self.P, self.mem
        P.barrier()
        m0 = mem.mark()
        h = mem.alloc([KC, T], BF16)
        rots = (Rot(mem, "sq", 1, [KC, 512], BF16), Rot(mem, "sd", 2, [512], F32), Rot(mem, "ntmp", 3, [512], F32))
        self.prenorm(s, l, 1, tiles, h, rots)
        P.barrier()
        mem.reset(m0)
        h = mem.alloc([KC, T], BF16)
        return m0, h

    def mixer_end(self, m0):
        self.P.barrier()
        self.mem.reset(m0)

    def rope_evac(self, psA, pkA, psB, pkB, dst, N, tokx, W, rows=slice(0, 128)):
        P = self.P
        t1, t2 = self.t1, self.t2
        cos = self.rope[rows, 0, tokx:tokx + N]
        sin = self.rope[rows, 1, tokx:tokx + N]
        P.op("dve", lambda e: e.tensor_tensor(t1[rows, 0:N], psA[rows, 0:N], cos, ALU.mult), R=[pkA, "rope"], W=["t1"])
        P.op("dve", lambda e: e.tensor_tensor(t2[rows, 0:N], psB[rows, 0:N], sin, ALU.mult), R=[pkB, "rope"], W=["t2"])
        P.op("dve", lambda e: e.tensor_tensor(dst, t1[rows, 0:N], t2[rows, 0:N], ALU.add), R=["t1", "t2"], W=W)

    def outproj_tile(self, s, l, tt, wO, kO, oT, ok, nk, pool="a"):
        P = self.P
        hx = self.hx
        t0, N = TT[tt]
        col = 2 if tt == 0 else s
        ln = l * 3 + 1
        for dc in range(KC):
            pd, pdk = self.ps_s() if pool == "s" else self.ps_a()
            self.mmk(pd[:, 0:N], [(wO[:, kc, dc * 128:(dc + 1) * 128], oT[:, kc, 0:N]) for kc in range(nk)], [kO, ok], pdk)
            P.op("dve", lambda e, pd=pd, dc=dc: e.scalar_tensor_tensor(
                hx[:, dc, t0:t0 + N], pd[:, 0:N], self.Gmod[:, ln, dc, col:col + 1], hx[:, dc, t0:t0 + N],
                ALU.mult, ALU.add), R=[pdk, ("hx", tt)], W=[("hx", tt)])

    @staticmethod
    def tile_of_block(tb):
        return 0 if tb < 2 else 1 + (tb - 2) // 4

    def mixer(self, s, l):
        kind = l % 4
        if kind == 0:
            self.gqa(s, l)
        elif kind == 1:
            self.diffattn(s, l)
        elif kind == 2:
            self.hgrn(s, l)
        else:
            self.mla(s, l)

    def gqa(self, s, l):
        P, mem = self.P, self.mem
        tiles = [0, 1, 2, 3, 4]
        m0, h = self.mixer_begin(s, l, tiles)
        d = self.dram
        P.dma("pool", self.rope, d["rope64"], W=["rope"])
        kTs = [mem.alloc([T], BF16), mem.alloc([T], BF16)]
        vaug = mem.alloc([18, 2, 128], BF16)
        qr = Rot(mem, "qT", 2, [2, 512], BF16)
        ptr = Rot(mem, "PT", 2, [5, 512], BF16)
        self.t1 = mem.alloc([512], F32)
        self.t2 = mem.alloc([512], F32)
        otr = Rot(mem, "oT", 2, [2, 512], BF16)
        dtr = Rot(mem, "dt", 2, [128], F32)
        P.op("dve", lambda e: e.memset(vaug, 1.0), W=["vaug"])
        P.op("dve", lambda e: e.memset(kTs[0], 0.0), W=[("kT", tt) for tt in tiles])
        P.op("dve", lambda e: e.memset(kTs[1], 0.0), W=[("kT", tt) for tt in tiles])
        for j in range(4):
            wA1, k1 = self.wload(d["gqa_pA"][j, 0].rearrange("(k p) n -> p k n", p=128), (KC, 256))
            wA2, k2 = self.wload(d["gqa_pA"][j, 1].rearrange("(k p) n -> p k n", p=128), (KC, 256))
            wB, kB = self.wload(d["gqa_pB"][j].rearrange("(k p) n -> p k n", p=128), (KC, 256))
            wV, kV = self.wload(d["gqa_pV"][j].rearrange("(k p) n -> p k n", p=128), (KC, 64))
            wO, kO = self.wload(d["gqa_w_out"][j * 256:(j + 1) * 256, :].rearrange("(k p) n -> p k n", p=128), (2, D))
            for tt in tiles:
                t0, N = TT[tt]
                psA, pkA = self.ps_s()
                self.mmk(psA[:, 0:N], [(wB[:, kc, 0:128], h[:, kc, t0:t0 + N]) for kc in range(KC)], [kB, ("h", tt)], pkA)
                rws = (slice(0, 64), slice(64, 128))
                if tt == 0:
                    for i in range(2):
                        P.op("act", lambda e, psA=psA, t0=t0, N=N, i=i: e.activation(kTs[i][rws[i], t0:t0 + N], psA[rws[i], 0:N], AF.Copy),
                             R=[pkA], W=[("kT", tt)])
                else:
                    psB, pkB = self.ps_s()
                    self.mmk(psB[:, 0:N], [(wB[:, kc, 128:256], h[:, kc, t0:t0 + N]) for kc in range(KC)], [kB, ("h", tt)], pkB)
                    for i in range(2):
                        self.rope_evac(psA, pkA, psB, pkB, kTs[i][rws[i], t0:t0 + N], N, t0 - CT, [("kT", tt)], rows=rws[i])
            if self.cut <= 1:
                break
            for g4 in range(0, 18, 4):
                nb = min(4, 18 - g4)
                ps, pk = self.ps_s()
                for bi in range(nb):
                    tb = g4 + bi
                    self.mmk(ps[:, bi * 64:(bi + 1) * 64], [(h[:, kc, tb * 128:(tb + 1) * 128], wV[:, kc, :]) for kc in range(KC)],
                             [kV, ("h", self.tile_of_block(tb))], pk)
                src = ps[:, 0:nb * 64].rearrange("p (a b) -> p a b", b=64)
                P.op("act", lambda e, src=src, g4=g4, nb=nb: e.activation(vaug[:, g4:g4 + nb, 0, 0:64], src, AF.Copy), R=[pk], W=["vaug"])
                P.op("dve", lambda e, src=src, g4=g4, nb=nb: e.tensor_copy(vaug[:, g4:g4 + nb, 1, 64:128], src), R=[pk], W=["vaug"])
            if self.cut <= 2:
                break
            pend_o = None

            def make_q(tt, wA1=wA1, wA2=wA2, k1=k1, k2=k2):
                t0, N = TT[tt]
                qT, qk = qr.next()
                for qc in range(2):
                    psA, pkA = self.ps_s()
                    self.mmk(psA[:, 0:N], [(wA1[:, kc, qc * 128:(qc + 1) * 128], h[:, kc, t0:t0 + N]) for kc in range(KC)], [k1, ("h", tt)], pkA)
                    if tt == 0:
                        P.op("act", lambda e, psA=psA, qc=qc, N=N, qT=qT: e.activation(qT[:, qc, 0:N], psA[:, 0:N], AF.Copy), R=[pkA], W=[qk])
                    else:
                        psB, pkB = self.ps_s()
                        self.mmk(psB[:, 0:N], [(wA2[:, kc, qc * 128:(qc + 1) * 128], h[:, kc, t0:t0 + N]) for kc in range(KC)], [k2, ("h", tt)], pkB)
                        self.rope_evac(psA, pkA, psB, pkB, qT[:, qc, 0:N], N, t0 - CT, [qk])
                return qT, qk

            qnext = make_q(tiles[0])
            for ti, tt in enumerate(tiles):
                t0, N = TT[tt]
                nqb = N // 128
                qT, qk = qnext
                if ti + 1 < len(tiles):
                    qnext = make_q(tiles[ti + 1])
                oT, ok = otr.next()

                def issue_scores(qi, tt=tt, t0=t0, qT=qT, qk=qk):
                    tbq = (t0 + qi * 128) // 128
                    if tt == 0:
                        kbs = [(0, None), (1, None)]
                    else:
                        kbs = []
                        if tbq > 2:
                            kbs.append((tbq - 1, 0))
                        kbs.append((tbq, None))
                        if tbq < 17:
                            kbs.append((tbq + 1, 1))
                        kbs += [(0, None), (1, None)]
                    PT, ptk = ptr.next()
                    for kbi, (kb, m) in enumerate(kbs):
                        ps, pk = self.ps_s()
                        first = True
                        if m is not None:
                            self.mm(ps[:, 0:512], self.ident, self.gmask[:, m, :], True, False, [], [pk])
                            first = False
                        for hh in range(4):
                            qc = hh // 2
                            self.mm(ps[:, hh * 128:(hh + 1) * 128], kTs[hh % 2][:, kb * 128:(kb + 1) * 128],
                                    qT[:, qc, qi * 128:(qi + 1) * 128], first, hh == 3,
                                    [("kT", self.tile_of_block(kb)), qk], [pk])
                            first = False
                        P.op("act", lambda e, PT=PT, kbi=kbi, ps=ps: e.activation(PT[:, kbi, :], ps[:, 0:512], AF.Exp, scale=0.125),
                             R=[pk], W=[ptk])
                    return PT, ptk, kbs

                def consume(qi, PT, ptk, kbs, oT=oT, ok=ok, j=j):
                    po_, pok = self.ps_a()
                    for hh in range(4):
                        var = hh % 2
                        self.mmk(po_[:, hh * 128:(hh + 1) * 128],
                                 [(vaug[:, kb, var, :], PT[:, kbi, hh * 128:(hh + 1) * 128]) for kbi, (kb, m) in enumerate(kbs)],
                                 [ptk, "vaug"], pok)
                    for hh in range(4):
                        head = 4 * j + hh
                        qc = hh // 2
                        orow, drow = (slice(0, 64), slice(64, 128)) if hh % 2 == 0 else (slice(64, 128), slice(0, 64))
                        cols = slice(hh * 128, (hh + 1) * 128)
                        dt, dk = dtr.next()
                        P.op("act", lambda e, dt=dt, orow=orow, drow=drow, cols=cols, head=head, po_=po_: e.activation(
                            dt[orow, :], po_[drow, cols], AF.Identity, bias=self.sinkexp[drow, head:head + 1]), R=[pok], W=[dk])
                        P.op("dve", lambda e, dt=dt, orow=orow: e.reciprocal(dt[orow, :], dt[orow, :]), R=[dk], W=[dk])
                        P.op("dve", lambda e, dt=dt, orow=orow, cols=cols, qc=qc, qi=qi, oT=oT, po_=po_: e.tensor_tensor(
                            oT[orow, qc, qi * 128:(qi + 1) * 128], po_[orow, cols], dt[orow, :], ALU.mult), R=[pok, dk], W=[ok])

                nxt = issue_scores(0)
                for qi in range(nqb):
                    cur = nxt
                    if qi + 1 < nqb:
                        nxt = issue_scores(qi + 1)
                    consume(qi, *cur)
                if pend_o is not None:
                    self.outproj_tile(*pend_o)
                pend_o = (s, l, tt, wO, kO, oT, ok, 2, "s")
            self.outproj_tile(*pend_o)
        self.mixer_end(m0)

    def diffattn(self, s, l):
        P, mem = self.P, self.mem
        d = self.dram
        tiles = [0, 1, 2, 3, 4]
        m0, h = self.mixer_begin(s, l, tiles)
        P.dma("pool", self.rope, d["rope64"], W=["rope"])
        kTs = [mem.alloc([T], BF16), mem.alloc([T], BF16)]
        V = mem.alloc([18, 128], BF16)
        qr = Rot(mem, "qT", 2, [512], BF16)
        ptr = Rot(mem, "PT", 3, [1024], BF16)
        self.t1 = mem.alloc([512], F32)
        self.t2 = mem.alloc([512], F32)
        p2r = Rot(mem, "p2", 2, [512], BF16)
        rr = Rot(mem, "rj", 1, [512], F32)
        tr = Rot(mem, "tj", 2, [512], F32)
        combr = Rot(mem, "comb", 2, [512], F32)
        sqb = mem.alloc([512], BF16)
        sd = mem.alloc([512], F32)
        otr = Rot(mem, "oT", 2, [1, 512], BF16)

        def finish(tt, comb, ck, wO, kO):
            t0, N = TT[tt]
            P.op("act", lambda e: e.activation(sqb[:, 0:N], comb[:, 0:N], AF.Square), R=[ck], W=["sqb"])
            ps, pk = self.ps_s()
            self.mm(ps[:, 0:N], self.ones_bf, sqb[:, 0:N], True, True, ["sqb"], [pk])
            P.op("act", lambda e: e.activation(sd[:, 0:N], ps[:, 0:N], AF.Ln, bias=self.eps_ap, scale=1.0 / 128), R=[pk], W=["sd"])
            P.op("act", lambda e: e.activation(sd[:, 0:N], sd[:, 0:N], AF.Exp, scale=-0.5), R=["sd"], W=["sd"])
            P.op("dve", lambda e: e.tensor_tensor(comb[:, 0:N], comb[:, 0:N], sd[:, 0:N], ALU.mult), R=[ck, "sd"], W=[ck])
            oT, ok = otr.next()
            P.op("act", lambda e: e.activation(oT[:, 0, 0:N], comb[:, 0:N], AF.Identity, scale=self.sublnw[:, 0:1]), R=[ck], W=[ok])
            self.outproj_tile(s, l, tt, wO, kO, oT, ok, 1, pool="s")

        P.op("dve", lambda e: e.memset(kTs[0], 0.0), W=[("kT", tt) for tt in tiles])
        P.op("dve", lambda e: e.memset(kTs[1], 0.0), W=[("kT", tt) for tt in tiles])
        rws = (slice(0, 64), slice(64, 128))
        for hd in range(8):
            wQ, kQ = self.wload(d["diff_pQ"][hd].rearrange("(k p) n -> p k n", p=128), (KC, 256))
            wK, kK = self.wload(d["diff_pK"][hd].rearrange("(k p) n -> p k n", p=128), (KC, 256))
            wV, kV = self.wload(d["diff_pV"][hd].rearrange("(k p) n -> p k n", p=128), (KC, 128))
            wO, kO = self.wload(d["diff_w_out"][hd * 128:(hd + 1) * 128, :].rearrange("(k p) n -> p k n", p=128), (1, D))
            for tt in tiles:
                t0, N = TT[tt]
                psA, pkA = self.ps_s()
                self.mmk(psA[:, 0:N], [(wK[:, kc, 0:128], h[:, kc, t0:t0 + N]) for kc in range(KC)], [kK, ("h", tt)], pkA)
                if tt == 0:
                    for i in range(2):
                        P.op("act", lambda e, psA=psA, t0=t0, N=N, i=i: e.activation(kTs[i][rws[i], t0:t0 + N], psA[rws[i], 0:N], AF.Copy),
                             R=[pkA], W=[("kT", tt)])
                else:
                    psB, pkB = self.ps_s()
                    self.mmk(psB[:, 0:N], [(wK[:, kc, 128:256], h[:, kc, t0:t0 + N]) for kc in range(KC)], [kK, ("h", tt)], pkB)
                    for i in range(2):
                        self.rope_evac(psA, pkA, psB, pkB, kTs[i][rws[i], t0:t0 + N], N, t0 - CT, [("kT", tt)], rows=rws[i])
            for g4 in range(0, 18, 4):
                nb = min(4, 18 - g4)
                ps, pk = self.ps_s()
                for bi in range(nb):
                    tb = g4 + bi
                    self.mmk(ps[:, bi * 128:(bi + 1) * 128], [(h[:, kc, tb * 128:(tb + 1) * 128], wV[:, kc, :]) for kc in range(KC)],
                             [kV, ("h", self.tile_of_block(tb))], pk)
                src = ps[:, 0:nb * 128].rearrange("p (a b) -> p a b", b=128)
                P.op("act", lambda e, src=src, g4=g4, nb=nb: e.activation(V[:, g4:g4 + nb, :], src, AF.Copy), R=[pk], W=["V"])
            pend_o = None

            def make_q(tt, wQ=wQ, kQ=kQ):
                t0, N = TT[tt]
                qT, qk = qr.next()
                psA, pkA = self.ps_s()
                self.mmk(psA[:, 0:N], [(wQ[:, kc, 0:128], h[:, kc, t0:t0 + N]) for kc in range(KC)], [kQ, ("h", tt)], pkA)
                if tt == 0:
                    P.op("act", lambda e, psA=psA, N=N, qT=qT: e.activation(qT[:, 0:N], psA[:, 0:N], AF.Copy), R=[pkA], W=[qk])
                else:
                    psB, pkB = self.ps_s()
                    self.mmk(psB[:, 0:N], [(wQ[:, kc, 128:256], h[:, kc, t0:t0 + N]) for kc in range(KC)], [kQ, ("h", tt)], pkB)
                    self.rope_evac(psA, pkA, psB, pkB, qT[:, 0:N], N, t0 - CT, [qk])
                return qT, qk

            qnext = make_q(tiles[0])
            for ti, tt in enumerate(tiles):
                t0, N = TT[tt]
                qT, qk = qnext
                if ti + 1 < len(tiles):
                    qnext = make_q(tiles[ti + 1])
                kbs = [0, 1] if tt == 0 else list(range(18))
                tjs = []
                for j in range(2):
                    O, Ok = self.ps_a()
                    Dn, Dk = self.ps_a()
                    def issue_pair(pr, j=j, qT=qT, qk=qk, N=N):
                        dps, dk = self.ps_s2()
                        for hf, kb in enumerate(pr):
                            self.mm(dps[:, hf * 512:hf * 512 + N], kTs[j][:, kb * 128:(kb + 1) * 128], qT[:, 0:N], True, True,
                                    [("kT", self.tile_of_block(kb)), qk], [dk[hf]])
                        PT, ptk = ptr.next()
                        P.op("act", lambda e, PT=PT, dps=dps, N=N: e.activation(
                            PT.rearrange("p (a b) -> p a b", a=2)[:, :, 0:N], dps.rearrange("p (a b) -> p a b", a=2)[:, :, 0:N],
                            AF.Exp, scale=0.125), R=[dk[0], dk[1]], W=[ptk])
                        return PT, ptk
                    pairs = [kbs[i:i + 2] for i in range(0, len(kbs), 2)]
                    pend = [issue_pair(pairs[0])]
                    nkb = len(kbs)
                    for pi, pr in enumerate(pairs):
                        PT, ptk = pend[pi]
                        if pi + 1 < len(pairs):
                            pend.append(issue_pair(pairs[pi + 1]))
                        p2, p2k = p2r.next()
                        P.op("dve", lambda e, p2=p2, PT=PT, N=N: e.tensor_tensor(p2[:, 0:N], PT[:, 0:N], PT[:, 512:512 + N], ALU.add),
                             R=[ptk], W=[p2k])
                        for hf, kb in enumerate(pr):
                            kbi = 2 * pi + hf
                            self.mm(O[:, 0:N], V[:, kb, :], PT[:, hf * 512:hf * 512 + N], kbi == 0, kbi == nkb - 1, [ptk, "V"], [Ok])
                        self.mm(Dn[:, 0:N], self.ones_bf, p2[:, 0:N], pi == 0, pi == len(pairs) - 1, [p2k], [Dk])
                    rj, rk = rr.next()
                    tj, tk = tr.next()
                    P.op("dve", lambda e, rj=rj, Dn=Dn, N=N: e.reciprocal(rj[:, 0:N], Dn[:, 0:N]), R=[Dk], W=[rk])
                    P.op("dve", lambda e, tj=tj, O=O, rj=rj, N=N: e.tensor_tensor(tj[:, 0:N], O[:, 0:N], rj[:, 0:N], ALU.mult), R=[Ok, rk], W=[tk])
                    tjs.append((tj, tk))
                comb, ck = combr.next()
                P.op("dve", lambda e, tjs=tjs, N=N, comb=comb: e.scalar_tensor_tensor(
                    comb[:, 0:N], tjs[1][0][:, 0:N], self.neglam[:, 0:1], tjs[0][0][:, 0:N], ALU.mult, ALU.add),
                    R=[tjs[0][1], tjs[1][1]], W=[ck])
                if pend_o is not None:
                    finish(*pend_o)
                pend_o = (tt, comb, ck, wO, kO)
            finish(*pend_o)
        self.mixer_end(m0)

    def hgrn(self, s, l):
        P, mem = self.P, self.mem
        d = self.dram
        tiles = [0, 1, 2, 3, 4]
        m0, h = self.mixer_begin(s, l, tiles)
        Vtok = mem.alloc([36, 128], BF16)
        oacc = mem.alloc([T], F32)
        m1 = mem.mark()
        r64 = slice(0, 64)
        for hd in range(8):
            wA, kA = self.wload(d["hgrn_pA"][hd].rearrange("(k p) n -> p k n", p=128), (KC, 256))
            wB, kB = self.wload(d["hgrn_pB"][hd].rearrange("(k p) n -> p k n", p=128), (KC, 256))
            wV, kV = self.wload(d["hgrn_pV"][hd].rearrange("(k p) n -> p k n", p=128), (KC, 128))
            wO, kO = self.wload(d["hgrn_w_out"][hd * 128:(hd + 1) * 128, :].rearrange("(k p) n -> p k n", p=128), (1, D))
            for g4 in range(0, 36, 4):
                ps, pk = self.ps_s()
                for bi in range(4):
                    ch = g4 + bi
                    self.mmk(ps[r64, bi * 128:(bi + 1) * 128], [(h[:, kc, ch * 64:(ch + 1) * 64], wV[:, kc, :]) for kc in range(KC)],
                             [kV, ("h", self.tile_of_block(ch // 2))], pk)
                src = ps[r64, 0:512].rearrange("p (a b) -> p a b", b=128)
                P.op("act", lambda e, src=src, g4=g4: e.activation(Vtok[r64, g4:g4 + 4, :], src, AF.Copy), R=[pk], W=["Vtok"])
            qs_all = self.rope.rearrange("p a b -> p (a b)")[:, 0:T]
            for tt in tiles:
                t0, N = TT[tt]
                ps, pk = self.ps_s()
                self.mmk(ps[:, 0:N], [(wA[:, kc, 0:128], h[:, kc, t0:t0 + N]) for kc in range(KC)], [kA, ("h", tt)], pk)
                P.op("act", lambda e, ps=ps, t0=t0, N=N: e.activation(qs_all[:, t0:t0 + N], ps[:, 0:N], AF.Silu), R=[pk], W=["rope"])
            HT = [(i * 256, 256) for i in range(9)]
            P.op("dve", lambda e: e.memset(oacc, 0.0), W=[("oacc", i) for i in range(9)])

            def chain(dr):
                NN = 256
                F_ = mem.alloc([NN], F32)
                LF = mem.alloc([NN], F32)
                Pg = mem.alloc([NN], F32)
                D1, D4 = F_, LF
                E3 = mem.alloc([NN], F32)
                tot = mem.alloc([4], F32)
                kk = mem.alloc([NN], BF16)
                E1 = mem.alloc([NN], BF16)
                E2 = mem.alloc([NN], BF16)
                E4 = E1
                QG = mem.alloc([NN], BF16)
                KG = mem.alloc([NN], BF16)
                QE = mem.alloc([NN], BF16)
                KE = mem.alloc([NN], BF16)
                AT4 = mem.alloc([4, 64], BF16)
                KE4 = mem.alloc([4, 128], BF16)
                srot = Rot(mem, "S%d" % dr, 3, [128], F32)
                sbr = Rot(mem, "Sbf%d" % dr, 3, [128], BF16)
                yield
                K_ = lambda nm: (nm, dr)
                mi, li = (32, 63) if dr == 0 else (31, 0)
                order = list(range(9)) if dr == 0 else [0] + list(range(8, 0, -1))
                Sc, Sck = srot.next()
                P.op("dve", lambda e, Sc=Sc: e.memset(Sc, 0.0), W=[Sck])
                sb, sbk = sbr.next()
                P.op("dve", lambda e, sb=sb: e.memset(sb, 0.0), W=[sbk])
                v3 = lambda a: a[:, 0:NN].rearrange("p (c t) -> p c t", t=64)
                nch = 4
                for ti in order:
                    t0, N = HT[ti]
                    hk = ("h", 0 if ti == 0 else 1 + (ti - 1) // 2)
                    qs = qs_all[:, t0:t0 + NN]
                    ps, pk = self.ps_s()
                    wf, kf, c0 = (wA, kA, 128) if dr == 0 else (wB, kB, 0)
                    self.mmk(ps[:, 0:N], [(wf[:, kc, c0:c0 + 128], h[:, kc, t0:t0 + N]) for kc in range(KC)], [kf, hk], pk)
                    P.op("act", lambda e, ps=ps: e.activation(F_, ps[:, 0:NN], AF.Exp, scale=-1.0), R=[pk], W=[K_("F")])
                    yield
                    P.op("act", lambda e: e.activation(Pg, F_, AF.Ln, bias=self.one_ap), R=[K_("F")], W=[K_("Pg")])
                    yield
                    P.op("act", lambda e, hd=hd: e.activation(LF, F_, AF.Ln, bias=self.one_ap, scale=self.lb[:, hd:hd + 1]), R=[K_("F")], W=[K_("LF")])
                    yield
                    P.op("act", lambda e: e.activation(F_, Pg, AF.Exp, scale=-1.0), R=[K_("Pg")], W=[K_("F")])
                    P.op("dve", lambda e: e.tensor_tensor(LF, LF, Pg, ALU.subtract), R=[K_("LF"), K_("Pg")], W=[K_("LF")])
                    yield
                    P.op("dve", lambda e, hd=hd: e.tensor_scalar(kk, F_, self.noml[:, hd:hd + 1], self.oml[:, hd:hd + 1], ALU.mult, ALU.add),
                         R=[K_("F")], W=[K_("kk")])
                    yield
                    P.op("dve", lambda e: e.tensor_tensor_scan(Pg, self.rmask[:, 0:NN], LF, 0.0, ALU.mult, ALU.add), R=[K_("LF")], W=[K_("Pg")])
                    yield
                    if dr == 1:
                        P.op("dve", lambda e: e.tensor_copy(tot[:, 0:nch], v3(Pg)[:, :, 63]), R=[K_("Pg")], W=[K_("tot")])
                        P.op("dve", lambda e: e.tensor_tensor(Pg, LF, Pg, ALU.subtract), R=[K_("LF"), K_("Pg")], W=[K_("Pg")])
                        yield
                        P.op("dve", lambda e: e.tensor_tensor(v3(Pg), v3(Pg), tot[:, 0:nch].unsqueeze(2).broadcast_to([128, nch, 64]), ALU.add),
                             R=[K_("Pg"), K_("tot")], W=[K_("Pg")])
                        yield
                    P.op("dve", lambda e: e.tensor_tensor(v3(D1), v3(Pg), v3(Pg)[:, :, mi:mi + 1].broadcast_to([128, nch, 64]), ALU.subtract),
                         R=[K_("Pg")], W=[K_("F")])
                    P.op("act", lambda e: e.activation(E3, Pg, AF.Exp), R=[K_("Pg")], W=[K_("E3")])
                    yield
                    P.op("dve", lambda e: e.tensor_tensor(v3(D4), v3(Pg)[:, :, li:li + 1].broadcast_to([128, nch, 64]), v3(Pg), ALU.subtract),
                         R=[K_("Pg")], W=[K_("LF")])
                    P.op("act", lambda e: e.activation(E1, D1, AF.Exp), R=[K_("F")], W=[K_("E1")])
                    yield
                    P.op("dve", lambda e, qs=qs: e.tensor_tensor(QE, qs, E3, ALU.mult), R=["rope", K_("E3")], W=[K_("QE")])
                    P.op("act", lambda e: e.activation(E2, D1, AF.Exp, scale=-1.0), R=[K_("F")], W=[K_("E2")])
                    yield
                    P.op("dve", lambda e, qs=qs: e.tensor_tensor(QG, qs, E1, ALU.mult), R=["rope", K_("E1")], W=[K_("QG")])
                    P.op("act", lambda e: e.activation(E4, D4, AF.Exp), R=[K_("LF")], W=[K_("E1")])
                    yield
                    P.op("dve", lambda e: e.tensor_tensor(KG, kk, E2, ALU.mult), R=[K_("kk"), K_("E2")], W=[K_("KG")])
                    yield
                    P.op("dve", lambda e: e.tensor_tensor(KE, kk, E4, ALU.mult), R=[K_("kk"), K_("E1")], W=[K_("KE")])
                    yield
                    corder = list(range(nch)) if dr == 0 else list(range(nch - 1, -1, -1))
                    psA, pkA = self.ps_s()
                    for ci in range(nch):
                        cs = slice(ci * 64, (ci + 1) * 64)
                        self.mm(psA[r64, cs], KG[:, cs], QG[:, cs], True, True, [K_("KG"), K_("QG")], [pkA])
                    pst, pstk = self.ps_s()
                    pstb = pst[:, 0:256].bitcast(BF16)
                    for ci in range(nch):
                        cs = slice(ci * 64, (ci + 1) * 64)
                        P.op("pe", lambda e, pstb=pstb, cs=cs, ci=ci: e.transpose(pstb[r64, ci * 128:(ci + 1) * 128], KE[:, cs], self.ident),
                             R=[K_("KE")], W=[pstk])
                    yield
                    P.op("dve", lambda e, psA=psA: e.tensor_tensor(
                        AT4[r64, :, :], psA[r64, 0:256].rearrange("p (c t) -> p c t", t=64),
                        self.trimask[r64, dr, :].unsqueeze(1).broadcast_to([64, nch, 64]), ALU.mult), R=[pkA], W=[K_("AT")])
                    P.op("act", lambda e, pstb=pstb: e.activation(KE4[r64, :, :], pstb[r64, 0:512].rearrange("p (c t) -> p c t", t=128), AF.Copy),
                         R=[pstk], W=[K_("KEt")])
                    yield
                    pS, pSk = self.ps_a()
                    for ci in range(nch):
                        ch = t0 // 64 + ci
                        self.mm(pS[:, ci * 128:(ci + 1) * 128], KE4[r64, ci, :], Vtok[r64, ch, :], True, True, [K_("KEt"), "Vtok"], [pSk])
                    yield
                    O, Ok = self.ps_a()
                    for ci in corder:
                        ch = t0 // 64 + ci
                        cs = slice(ci * 64, (ci + 1) * 64)
                        self.mm(O[:, cs], Vtok[r64, ch, :], AT4[r64, ci, :], True, False, ["Vtok", K_("AT")], [Ok])
                        self.mm(O[:, cs], sb, QE[:, cs], False, True, [sbk, K_("QE")], [Ok])
                        di = ci * 64 + li
                        Sn, Snk = srot.next()
                        P.op("dve", lambda e, pS=pS, ci=ci, di=di, Sn=Sn, Sc=Sc: e.scalar_tensor_tensor(
                            Sn, Sc, E3[:, di:di + 1], pS[:, ci * 128:(ci + 1) * 128], ALU.mult, ALU.add),
                            R=[Sck, K_("E3"), pSk], W=[Snk])
                        Sc, Sck = Sn, Snk
                        sb, sbk = sbr.next()
                        P.op("act", lambda e, sb=sb, Sc=Sc: e.activation(sb, Sc, AF.Copy), R=[Sck], W=[sbk])
                        yield
                    P.op("dve", lambda e, O=O, t0=t0: e.tensor_tensor(oacc[:, t0:t0 + NN], O[:, 0:NN], oacc[:, t0:t0 + NN], ALU.add),
                         R=[Ok, ("oacc", ti)], W=[("oacc", ti)])
                    yield

            gens = [chain(0), chain(1)]
            alive = list(gens)
            while alive:
                for g in list(alive):
                    try:
                        next(g)
                    except StopIteration:
                        alive.remove(g)
            P.barrier()
            mem.reset(m1)
            gs_all = mem.alloc([T], BF16)
            sqr = Rot(mem, "sqb", 2, [512], BF16)
            sdr = Rot(mem, "sd", 2, [512], F32)
            tmr = Rot(mem, "tmp", 2, [512], F32)
            otr = Rot(mem, "oT", 2, [1, 512], BF16)
            for tt in tiles:
                t0, N = TT[tt]
                ps, pk = self.ps_s()
                self.mmk(ps[:, 0:N], [(wB[:, kc, 128:256], h[:, kc, t0:t0 + N]) for kc in range(KC)], [kB, ("h", tt)], pk)
                P.op("act", lambda e, ps=ps, t0=t0, N=N: e.activation(gs_all[:, t0:t0 + N], ps[:, 0:N], AF.Silu), R=[pk], W=[("gs", tt)])
            pend_o = None
            for tt in tiles:
                t0, N = TT[tt]
                sqb, sqk = sqr.next()
                sd, sdk = sdr.next()
                tmp, tmk = tmr.next()
                P.op("act", lambda e, sqb=sqb, t0=t0, N=N: e.activation(sqb[:, 0:N], oacc[:, t0:t0 + N], AF.Square), R=[], W=[sqk])
                ps2, pk2 = self.ps_s()
                self.mm(ps2[:, 0:N], self.ones_bf, sqb[:, 0:N], True, True, [sqk], [pk2])
                P.op("act", lambda e, sd=sd, ps2=ps2, N=N: e.activation(sd[:, 0:N], ps2[:, 0:N], AF.Ln, bias=self.eps_ap, scale=1.0 / 128), R=[pk2], W=[sdk])
                P.op("act", lambda e, sd=sd, N=N: e.activation(sd[:, 0:N], sd[:, 0:N], AF.Exp, scale=-0.5), R=[sdk], W=[sdk])
                P.op("dve", lambda e, tmp=tmp, sd=sd, t0=t0, N=N: e.tensor_tensor(tmp[:, 0:N], oacc[:, t0:t0 + N], sd[:, 0:N], ALU.mult), R=[sdk], W=[tmk])
                oT, ok = otr.next()
                P.op("dve", lambda e, oT=oT, tmp=tmp, t0=t0, N=N: e.scalar_tensor_tensor(
                    oT[:, 0, 0:N], tmp[:, 0:N], self.hnw[:, 0:1], gs_all[:, t0:t0 + N], ALU.mult, ALU.mult),
                    R=[tmk, ("gs", tt)], W=[ok])
                if pend_o is not None:
                    self.outproj_tile(*pend_o)
                pend_o = (s, l, tt, wO, kO, oT, ok, 1)
            self.outproj_tile(*pend_o)
            P.barrier()
            mem.reset(m1)
        self.mixer_end(m0)

    def mla(self, s, l):
        P, mem = self.P, self.mem
        d = self.dram
        tiles = [0, 1, 2, 3, 4]
        xt = [1, 2, 3, 4]
        P.barrier()
        m0 = mem.mark()
        cqT = mem.alloc([2, T], BF16)
        ckvT = mem.alloc([2, T], BF16)
        krT = mem.alloc([T], BF16)
        m1 = mem.mark()
        h = mem.alloc([KC, T], BF16)
        rots = (Rot(mem, "sq", 1, [KC, 512], BF16), Rot(mem, "sd", 2, [512], F32), Rot(mem, "ntmp", 3, [512], F32))
        self.prenorm(s, l, 1, tiles, h, rots)
        P.barrier()
        mem.reset(m1)
        h = mem.alloc([KC, T], BF16)
        P.dma("pool", self.rope, d["rope32"], W=["rope"])
        self.t1 = mem.alloc([512], F32)
        self.t2 = mem.alloc([512], F32)
        sq2 = mem.alloc([2, 512], BF16)
        sdr = Rot(mem, "sd", 2, [512], F32)
        wD1, kD1 = self.wload(d["mla_pD"][0].rearrange("(k p) n -> p k n", p=128), (KC, 256))
        wD2, kD2 = self.wload(d["mla_pD"][1].rearrange("(k p) n -> p k n", p=128), (KC, 256))
        wD3, kD3 = self.wload(d["mla_pD"][2].rearrange("(k p) n -> p k n", p=128), (KC, 256))
        r96 = slice(64, 96)
        for tt in tiles:
            t0, N = TT[tt]
            for (wD, kD, dstT, nwi) in ((wD1, kD1, cqT, 0), (wD2, kD2, ckvT, 1)):
                pcs = []
                for c in range(2):
                    ps, pk = self.ps_s()
                    self.mmk(ps[:, 0:N], [(wD[:, kc, c * 128:(c + 1) * 128], h[:, kc, t0:t0 + N]) for kc in range(KC)], [kD, ("h", tt)], pk)
                    P.op("act", lambda e, ps=ps, c=c, N=N: e.activation(sq2[:, c, 0:N], ps[:, 0:N], AF.Square), R=[pk], W=["sq2"])
                    pcs.append((ps, pk))
                pss, pssk = self.ps_s()
                self.mmk(pss[:, 0:N], [(self.ones_bf, sq2[:, c, 0:N]) for c in range(2)], ["sq2"], pssk)
                sd, sdk = sdr.next()
                P.op("act", lambda e, sd=sd, pss=pss, N=N: e.activation(sd[:, 0:N], pss[:, 0:N], AF.Ln, bias=self.eps_ap, scale=1.0 / 256), R=[pssk], W=[sdk])
                P.op("act", lambda e, sd=sd, N=N: e.activation(sd[:, 0:N], sd[:, 0:N], AF.Exp, scale=-0.5), R=[sdk], W=[sdk])
                for c in range(2):
                    ps, pk = pcs[c]
                    P.op("dve", lambda e, ps=ps, c=c, sd=sd, dstT=dstT, nwi=nwi, t0=t0, N=N: e.scalar_tensor_tensor(
                        dstT[:, c, t0:t0 + N], ps[:, 0:N], self.mla_nw[:, nwi, c:c + 1], sd[:, 0:N], ALU.mult, ALU.mult),
                        R=[pk, sdk], W=[("lora", nwi, tt)])
            psA, pkA = self.ps_s()
            self.mmk(psA[:, 0:N], [(wD3[:, kc, 0:128], h[:, kc, t0:t0 + N]) for kc in range(KC)], [kD3, ("h", tt)], pkA)
            if tt == 0:
                P.op("act", lambda e, psA=psA, t0=t0, N=N: e.activation(krT[r96, t0:t0 + N], psA[r96, 0:N], AF.Copy), R=[pkA], W=[("krT", tt)])
            else:
                psB, pkB = self.ps_s()
                self.mmk(psB[:, 0:N], [(wD3[:, kc, 128:256], h[:, kc, t0:t0 + N]) for kc in range(KC)], [kD3, ("h", tt)], pkB)
                self.rope_evac(psA, pkA, psB, pkB, krT[r96, t0:t0 + N], N, t0 - CT, [("krT", tt)], rows=r96)
        P.barrier()
        mem.reset(m1)
        self.t1 = mem.alloc([512], F32)
        self.t2 = mem.alloc([512], F32)
        kh = [mem.alloc([T], BF16), mem.alloc([T], BF16)]
        vaug = [mem.alloc([18, 128], BF16), mem.alloc([18, 128], BF16)]
        qr = Rot(mem, "qT", 2, [512], BF16)
        ptr = Rot(mem, "PT", 3, [1024], BF16)
        dtr = Rot(mem, "dt", 2, [512], F32)
        otr = Rot(mem, "oT", 2, [1, 512], BF16)
        scale = 96.0 ** -0.5
        for i in range(2):
            P.op("dve", lambda e, i=i: e.memset(kh[i], 0.0), W=[("kh", i)])
            P.op("dve", lambda e, i=i: e.memset(vaug[i], 1.0), W=[("vaug", i)])
        for q in qr.bufs:
            P.op("dve", lambda e, q=q: e.memset(q, 0.0), W=[])
        P.barrier()
        for pr in range(8):
            wO, kO = self.wload(d["mla_w_out"][pr * 128:(pr + 1) * 128, :].rearrange("(k p) n -> p k n", p=128), (1, D))
            wq_, wkv_ = [], []
            for hh in range(2):
                hd = 2 * pr + hh
                wq_.append(self.wload(d["mla_pUQ"][hd].rearrange("(k p) n -> p k n", p=128), (2, 256)))
                wkv_.append(self.wload(d["mla_pUKV"][hd].rearrange("(k p) n -> p k n", p=128), (2, 128)))
            for hh in range(2):
                wkv, kkv = wkv_[hh]
                vcols = slice(0, 64) if hh == 0 else slice(64, 128)
                for tt in tiles:
                    t0, N = TT[tt]
                    ps, pk = self.ps_s()
                    self.mmk(ps[0:64, 0:N], [(wkv[:, kc, 0:64], ckvT[:, kc, t0:t0 + N]) for kc in range(2)], [kkv, ("lora", 1, tt)], pk)
                    P.op("act", lambda e, ps=ps, hh=hh, t0=t0, N=N: e.activation(kh[hh][0:64, t0:t0 + N], ps[0:64, 0:N], AF.Copy), R=[pk], W=[("kh", hh)])
                P.op("act", lambda e, hh=hh: e.activation(kh[hh][r96, :], krT[r96, :], AF.Copy), R=[("krT", tt) for tt in tiles], W=[("kh", hh)])
                for g8 in range(0, 18, 8):
                    nb = min(8, 18 - g8)
                    ps, pk = self.ps_s()
                    for bi in range(nb):
                        tb = g8 + bi
                        self.mmk(ps[:, bi * 64:(bi + 1) * 64], [(ckvT[:, kc, tb * 128:(tb + 1) * 128], wkv[:, kc, 64:128]) for kc in range(2)],
                                 [kkv, ("lora", 1, self.tile_of_block(tb))], pk)
                    src = ps[:, 0:nb * 64].rearrange("p (a b) -> p a b", b=64)
                    P.op("act", lambda e, src=src, g8=g8, nb=nb, hh=hh, vcols=vcols: e.activation(vaug[hh][:, g8:g8 + nb, vcols], src, AF.Copy),
                         R=[pk], W=[("vaug", hh)])
            pend_o = None

            def make_q(tt, hh, wq_=wq_):
                t0, N = TT[tt]
                wq, kq = wq_[hh]
                qT, qk = qr.next()
                psA, pkA = self.ps_s()
                self.mmk(psA[0:96, 0:N], [(wq[:, kc, 0:96], cqT[:, kc, t0:t0 + N]) for kc in range(2)], [kq, ("lora", 0, tt)], pkA)
                psB, pkB = self.ps_s()
                self.mmk(psB[0:96, 0:N], [(wq[:, kc, 128:224], cqT[:, kc, t0:t0 + N]) for kc in range(2)], [kq, ("lora", 0, tt)], pkB)
                P.op("act", lambda e, qT=qT, psA=psA, N=N: e.activation(qT[0:64, 0:N], psA[0:64, 0:N], AF.Copy), R=[pkA], W=[qk])
                self.rope_evac(psA, pkA, psB, pkB, qT[r96, 0:N], N, t0 - CT, [qk], rows=r96)
                return qT, qk

            items = [(tt, hh) for tt in xt for hh in range(2)]
            qnext = make_q(*items[0])
            for tt in xt:
                t0, N = TT[tt]
                oT, ok = otr.next()
                for hh in range(2):
                    orow, drow = (slice(0, 64), slice(64, 128)) if hh == 0 else (slice(64, 128), slice(0, 64))
                    qT, qk = qnext
                    ii = items.index((tt, hh))
                    if ii + 1 < len(items):
                        qnext = make_q(*items[ii + 1])
                    O, Ok = self.ps_a()
                    def issue_pair(kb0, hh=hh, qT=qT, qk=qk, N=N):
                        dps, dk2 = self.ps_s2()
                        for hf in range(2):
                            kb = kb0 + hf
                            self.mm(dps[:, hf * 512:hf * 512 + N], kh[hh][:, kb * 128:(kb + 1) * 128], qT[:, 0:N], True, True, [("kh", hh), qk], [dk2[hf]])
                        PT, ptk = ptr.next()
                        P.op("act", lambda e, PT=PT, dps=dps, N=N: e.activation(PT[:, 0:1024], dps[:, 0:1024], AF.Exp, scale=scale),
                             R=[dk2[0], dk2[1]], W=[ptk])
                        return PT, ptk
                    pend = [issue_pair(0)]
                    for pi in range(9):
                        PT, ptk = pend[pi]
                        if pi + 1 < 9:
                            pend.append(issue_pair(2 * (pi + 1)))
                        for hf in range(2):
                            kb = 2 * pi + hf
                            self.mm(O[:, 0:N], vaug[hh][:, kb, :], PT[:, hf * 512:hf * 512 + N], kb == 0, kb == 17, [ptk, ("vaug", hh)], [Ok])
                    dt, dk = dtr.next()
                    P.op("act", lambda e, dt=dt, O=O, orow=orow, drow=drow, N=N: e.activation(dt[orow, 0:N], O[drow, 0:N], AF.Copy), R=[Ok], W=[dk])
                    P.op("dve", lambda e, dt=dt, orow=orow, N=N: e.reciprocal(dt[orow, 0:N], dt[orow, 0:N]), R=[dk], W=[dk])
                    P.op("dve", lambda e, dt=dt, O=O, orow=orow, oT=oT, N=N: e.tensor_tensor(oT[orow, 0, 0:N], O[orow, 0:N], dt[orow, 0:N], ALU.mult),
                         R=[Ok, dk], W=[ok])
                if pend_o is not None:
                    self.outproj_tile(*pend_o)
                pend_o = (s, l, tt, wO, kO, oT, ok, 1)
            self.outproj_tile(*pend_o)
        self.mixer_end(m0)


def host_inputs(inputs, core):
    b0 = 2 * core
    x = inputs["x"]
    ctx = inputs["ctx"]
    hT = np.empty((2, D, T), np.float32)
    for i in range(2):
        hT[i, :, :CT] = ctx[b0 + i].T
        hT[i, :, CT:] = x[b0 + i].T
    cvec = np.stack([inputs["c"][b0], inputs["c"][b0 + 1], inputs["c_ctx"]], axis=1)
    cT = np.ascontiguousarray(cvec.reshape(KC, 128, 3).transpose(1, 0, 2))
    m = {"hT": hT, "cT": cT}
    return m


def rope_tables(rot_dim):
    rows = S // 64
    row = np.repeat(np.arange(rows, dtype=np.float32), 64)
    col = np.tile(np.arange(64, dtype=np.float32), rows)
    axis_dim = rot_dim // 2
    inv_freq = (np.float32(10000.0) ** (-np.arange(0, axis_dim, 2, dtype=np.float32) / np.float32(axis_dim))).astype(np.float32)
    ang_r = row[:, None] * inv_freq[None, :]
    ang_c = col[:, None] * inv_freq[None, :]
    ang = np.concatenate([ang_r, ang_r, ang_c, ang_c], axis=-1).astype(np.float32)
    f = rot_dim // 4
    sgn = np.concatenate([-np.ones(f), np.ones(f), -np.ones(f), np.ones(f)]).astype(np.float32)
    perm = np.concatenate([np.arange(f, 2 * f), np.arange(0, f), np.arange(3 * f, 4 * f), np.arange(2 * f, 3 * f)])
    return np.cos(ang).astype(np.float32), (np.sin(ang) * sgn[None, :]).astype(np.float32), perm


def host_consts():
    c = {}
    cos, sins, perm64 = rope_tables(64)
    r = np.zeros((128, 2, S), np.float32)
    r[:, 0, :] = np.tile(cos.T, (2, 1))
    r[:, 1, :] = np.tile(sins.T, (2, 1))
    c["rope64"] = r
    cos, sins, perm32 = rope_tables(32)
    r = np.zeros((128, 2, S), np.float32)
    r[64:96, 0, :] = cos.T
    r[64:96, 1, :] = sins.T
    c["rope32"] = r
    c["ident"] = np.eye(128, dtype=np.float32)
    NEG = -30000.0
    kk = np.arange(128)[:, None]
    qq = np.arange(128)[None, :]
    gm = np.zeros((128, 2, 512), np.float32)
    gm[:, 0, :] = np.tile(np.where(qq <= kk, 0.0, NEG), (1, 4))
    gm[:, 1, :] = np.tile(np.where(kk <= qq, 0.0, NEG), (1, 4))
    c["gmask"] = gm
    return c, perm64, perm32


def host_shared(inputs):
    sh, perm64, perm32 = host_consts()
    w_in = inputs["gqa_w_in"][0]
    hp = (np.arange(16)[:, None] * 64 + perm64[None, :]).reshape(-1)
    wq, wqp = w_in[:, :1024], w_in[:, :1024][:, hp]
    pA = np.empty((4, 2, D, 256), np.float32)
    pB = np.empty((4, D, 256), np.float32)
    pV = np.empty((4, D, 64), np.float32)
    for j in range(4):
        pA[j, 0] = wq[:, j * 256:(j + 1) * 256]
        pA[j, 1] = wqp[:, j * 256:(j + 1) * 256]
        kj = w_in[:, 1024 + j * 64:1024 + (j + 1) * 64]
        kjp = kj[:, perm64]
        pB[j] = np.concatenate([kj, kj, kjp, kjp], axis=1)
        pV[j] = w_in[:, 1280 + j * 64:1280 + (j + 1) * 64]
    sh["gqa_pA"], sh["gqa_pB"], sh["gqa_pV"] = pA, pB, pV
    sh["gqa_w_out"] = np.ascontiguousarray(inputs["gqa_w_out"][0])
    w_in = inputs["diff_w_in"][0]
    sp = (np.arange(16)[:, None] * 64 + perm64[None, :]).reshape(-1)
    wq, wk, wv = w_in[:, :1024], w_in[:, 1024:2048], w_in[:, 2048:]
    wqp, wkp = wq[:, sp], wk[:, sp]
    sh["diff_pQ"] = np.stack([np.concatenate([wq[:, i * 128:(i + 1) * 128], wqp[:, i * 128:(i + 1) * 128]], axis=1) for i in range(8)])
    sh["diff_pK"] = np.stack([np.concatenate([wk[:, i * 128:(i + 1) * 128], wkp[:, i * 128:(i + 1) * 128]], axis=1) for i in range(8)])
    sh["diff_pV"] = np.stack([wv[:, i * 128:(i + 1) * 128] for i in range(8)])
    sh["diff_w_out"] = np.ascontiguousarray(inputs["diff_w_out"][0])
    sh["diff_lam_b"] = np.ascontiguousarray(np.broadcast_to(inputs["diff_lambda"][0][None], (128, 4, 64)))
    sh["diff_subln_b"] = np.ascontiguousarray(inputs["diff_subln_w"][0].reshape(128, 1))
    wd = inputs["mla_w_down"][0]
    pD = np.zeros((3, D, 256), np.float32)
    pD[0] = wd[:, 0:256]
    pD[1] = wd[:, 256:512]
    pD[2][:, 64:96] = wd[:, 512:544]
    pD[2][:, 128 + 64:128 + 96] = wd[:, 512:544][:, perm32]
    sh["mla_pD"] = pD
    wuq = inputs["mla_w_uq"][0]
    pUQ = np.zeros((16, 256, 256), np.float32)
    for hd in range(16):
        blk = wuq[:, hd * 96:(hd + 1) * 96]
        pUQ[hd][:, 0:96] = blk
        pUQ[hd][:, 128:128 + 64] = blk[:, 0:64]
        pUQ[hd][:, 128 + 64:128 + 96] = blk[:, 64:96][:, perm32]
    sh["mla_pUQ"] = pUQ
    sh["mla_pUKV"] = np.ascontiguousarray(inputs["mla_w_ukv"][0].reshape(256, 16, 128).transpose(1, 0, 2))
    sh["mla_w_out"] = np.ascontiguousarray(inputs["mla_w_out"][0])
    nw = np.stack([inputs["mla_q_norm_w"][0].reshape(2, 128).T, inputs["mla_kv_norm_w"][0].reshape(2, 128).T], axis=1)
    sh["mla_nw"] = np.ascontiguousarray(nw)
    w_in = inputs["hgrn_w_in"][0]
    cq, cf, cb, cv, cg = (w_in[:, i * 1024:(i + 1) * 1024] for i in range(5))
    sh["hgrn_pA"] = np.stack([np.concatenate([cq[:, i * 128:(i + 1) * 128], cf[:, i * 128:(i + 1) * 128]], axis=1) for i in range(8)])
    sh["hgrn_pB"] = np.stack([np.concatenate([cb[:, i * 128:(i + 1) * 128], cg[:, i * 128:(i + 1) * 128]], axis=1) for i in range(8)])
    sh["hgrn_pV"] = np.stack([cv[:, i * 128:(i + 1) * 128] for i in range(8)])
    sh["hgrn_w_out"] = np.ascontiguousarray(inputs["hgrn_w_out"][0])
    sh["hgrn_lbT"] = np.ascontiguousarray(inputs["hgrn_lower_bounds"].reshape(4, 8, 128).transpose(2, 0, 1))
    sh["hgrn_nw"] = np.ascontiguousarray(inputs["hgrn_norm_w"][0].reshape(128, 1))
    rm = np.ones((128, 512), np.float32)
    rm[:, ::64] = 0.0
    sh["rmask"] = rm
    ss_, tt_ = np.arange(64)[:, None], np.arange(64)[None, :]
    tm = np.zeros((128, 2, 64), np.float32)
    tm[0:64, 0, :] = (ss_ <= tt_)
    tm[0:64, 1, :] = (ss_ >= tt_)
    sh["trimask"] = tm
    sh["gqa_sinks_b"] = np.ascontiguousarray(np.broadcast_to(inputs["gqa_sinks"][0][None, :], (128, 16)))
    sh["ada_w"] = np.ascontiguousarray(inputs["ada_w"])
    sh["ada_bT"] = np.ascontiguousarray(inputs["ada_b"].reshape(NL, 72, 128).transpose(2, 0, 1))
    sh["norm_wT"] = np.ascontiguousarray(inputs["norm_w"].reshape(NL * 3, KC, 128).transpose(2, 0, 1))
    sh["fnorm_wT"] = np.ascontiguousarray(inputs["final_norm_w"].reshape(KC, 128).T)
    for nm, src in (("ffn_wg_l", "ffn_w_gate"), ("ffn_wu_l", "ffn_w_up")):
        w = inputs[src].reshape(NL, 2, KC, 128, 11, 256)
        sh[nm] = np.ascontiguousarray(w.transpose(0, 1, 4, 3, 2, 5)).reshape(NL, 2, 11, 128, KC * 256)
    sh["ffn_w_down"] = np.ascontiguousarray(inputs["ffn_w_down"])
    return sh


def kernel(**inputs):
    inputs = {k: np.asarray(v) for k, v in inputs.items()}
    b = Builder()
    nc = b.build()
    sh = host_shared(inputs)
    in_maps = []
    for core in range(NCORES):
        m = dict(sh)
        m.update(host_inputs(inputs, core))
        in_maps.append({k: v for k, v in m.items() if k in b.dram})
    res = run_bass_kernel_spmd(nc, in_maps, core_ids=list(range(NCORES)))
    out = np.empty((16, S, D), np.float32)
    for core in range(NCORES):
        o = res.results[core]["outT"]
        for i in range(2):
            out[2 * core + i] = o[i].T
    return out
```

```python
import math
import contextlib
import numpy as np
import concourse.bass as bass
import concourse.mybir as mybir
from concourse.bass_utils import run_bass_kernel_spmd

F32 = mybir.dt.float32
BF16 = mybir.dt.bfloat16
AF = mybir.ActivationFunctionType
ALU = mybir.AluOpType
AX = mybir.AxisListType

ENGS = ("pe", "act", "dve", "pool", "sp")

D = 1024
KC = 8
S = 2048
CT = 256
T = CT + S
DFF = 2816
NL = 4
EPS = 1e-6
NCORES = 8
TT = [(0, 256), (256, 512), (768, 512), (1280, 512), (1792, 512)]


class Ev:
    __slots__ = ("eng", "need_inc", "semval", "dma_sem", "dma_val", "idx")

    def __init__(self, eng):
        self.eng = eng
        self.idx = -1
        self.need_inc = False
        self.semval = None
        self.dma_sem = None
        self.dma_val = None


class Rec:
    __slots__ = ("fn", "deps", "ev", "is_dma")

    def __init__(self, fn, deps, ev, is_dma=False):
        self.fn = fn
        self.deps = deps
        self.ev = ev
        self.is_dma = is_dma


class Prog:
    def __init__(self, nc, n_dma_sems=40):
        self.nc = nc
        self.streams = {e: [] for e in ENGS}
        self.last_w = {}
        self.readers = {}
        self.last_ev = {e: None for e in ENGS}
        self.n_dma_sems = n_dma_sems
        self.dma_cnt = 0
        self.dma_last = [None] * n_dma_sems
        self.dma_vals = [0] * n_dma_sems
        self.n_ops = 0

    def _collect(self, eng, R, W, is_dma):
        best = {}

        def add(e):
            if e is None:
                return
            k = ("d", e.dma_sem) if e.dma_sem is not None else ("e", e.eng)
            o = best.get(k)
            if o is None or (e.dma_val if e.dma_sem is not None else e.idx) > (o.dma_val if o.dma_sem is not None else o.idx):
                best[k] = e

        for k in R:
            add(self.last_w.get(k))
        for k in W:
            w = self.last_w.get(k)
            if w is not None and (w.eng != eng or w.dma_sem is not None or is_dma):
                add(w)
            for r in self.readers.get(k, ()):
                if r.eng != eng or r.dma_sem is not None or is_dma:
                    add(r)
        return list(best.values())

    def _update(self, ev, R, W):
        for k in R:
            self.readers.setdefault(k, []).append(ev)
        for k in W:
            self.last_w[k] = ev
            self.readers[k] = []

    def op(self, eng, fn, R=(), W=()):
        ev = Ev(eng)
        deps = self._collect(eng, R, W, False)
        for d in deps:
            if d.dma_sem is None:
                d.need_inc = True
        ev.idx = len(self.streams[eng])
        self.streams[eng].append(Rec(fn, deps, ev))
        self._update(ev, R, W)
        self.last_ev[eng] = ev
        self.n_ops += 1
        return ev

    def dma(self, eng, out, in_, R=(), W=(), **kw):
        ev = Ev(eng)
        slot = self.dma_cnt % self.n_dma_sems
        self.dma_cnt += 1
        deps = self._collect(eng, R, W, True)
        prev = self.dma_last[slot]
        if prev is not None:
            deps.append(prev)
        for d in deps:
            if d.dma_sem is None:
                d.need_inc = True
        self.dma_vals[slot] += 16
        ev.dma_sem = slot
        ev.dma_val = self.dma_vals[slot]
        self.dma_last[slot] = ev
        ev.idx = len(self.streams[eng])
        self.streams[eng].append(
            Rec(lambda e, o=out, i=in_, k=kw: e.dma_start(out=o, in_=i, **k), deps, ev, True))
        self._update(ev, R, W)
        self.n_ops += 1
        return ev

    def barrier(self, wait_dma=False):
        lasts = [self.last_ev[e] for e in ENGS if self.last_ev[e] is not None]
        for l in lasts:
            l.need_inc = True
        dmas = [d for d in self.dma_last if d is not None] if wait_dma else []
        for e in ENGS:
            deps = [l for l in lasts if l.eng != e] + dmas
            if deps:
                self.streams[e].append(Rec(None, deps, None))
        if wait_dma:
            self.last_w = {}
            self.readers = {}
        else:
            self.last_w = {k: v for k, v in self.last_w.items() if v.dma_sem is not None}
            rd = {k: [r for r in v if r.dma_sem is not None] for k, v in self.readers.items()}
            self.readers = {k: v for k, v in rd.items() if v}

    def wait_all_dma(self, eng="sp"):
        deps = [d for d in self.dma_last if d is not None]
        self.streams[eng].append(Rec(None, deps, None))

    def emit(self):
        nc = self.nc
        with contextlib.ExitStack() as st:
            sems = {e: st.enter_context(nc.semaphore("s_" + e)) for e in ENGS}
            dsems = [st.enter_context(nc.semaphore("d%d" % i)) for i in range(self.n_dma_sems)]
            for e in ENGS:
                c = 0
                for r in self.streams[e]:
                    if r.ev is not None and r.ev.dma_sem is None and r.ev.need_inc:
                        c += 1
                        r.ev.semval = c
            block = st.enter_context(nc.Block())

            def run(engname, engobj):
                waited = {}
                for r in self.streams[engname]:
                    for d in r.deps:
                        if d.dma_sem is not None:
                            key, sem, val = ("d", d.dma_sem), dsems[d.dma_sem], d.dma_val
                        else:
                            key, sem, val = ("e", d.eng), sems[d.eng], d.semval
                        if waited.get(key, 0) >= val:
                            continue
                        waited[key] = val
                        engobj.wait_ge(sem, val)
                    if r.fn is None:
                        continue
                    ins = r.fn(engobj)
                    if r.is_dma:
                        ins.then_inc(dsems[r.ev.dma_sem], 16)
                    elif r.ev.need_inc:
                        ins.then_inc(sems[engname], 1)

            @block.tensor
            def _(eng):
                run("pe", eng)

            @block.scalar
            def _(eng):
                run("act", eng)

            @block.vector
            def _(eng):
                run("dve", eng)

            @block.gpsimd
            def _(eng):
                run("pool", eng)

            @block.sync
            def _(eng):
                run("sp", eng)


class Mem:
    def __init__(self, nc, total_bytes):
        self.n = total_bytes // 4
        self.t = nc.alloc_sbuf_tensor("sbuf_all", [128, self.n], F32)
        self.off = 0
        self.hi = 0

    def alloc(self, shape, dt):
        n = 1
        for s in shape:
            n *= s
        nb = n * (4 if dt == F32 else 2)
        nf = ((nb + 31) // 32) * 8
        assert self.off + nf <= self.n, ("SBUF overflow", self.off * 4, nf * 4, self.n * 4)
        a = self.t[:, self.off:self.off + nf]
        self.off += nf
        self.hi = max(self.hi, self.off)
        if dt != F32:
            a = a.bitcast(dt)
        a = a[:, 0:n]
        if len(shape) == 2:
            a = a.rearrange("p (a b) -> p a b", a=shape[0])
        elif len(shape) == 3:
            a = a.rearrange("p (a b c) -> p a b c", a=shape[0], b=shape[1])
        return a

    def mark(self):
        return self.off

    def reset(self, m):
        self.off = m


class Rot:
    def __init__(self, mem, name, n, shape, dt):
        self.bufs = [mem.alloc(shape, dt) for _ in range(n)]
        self.name = name
        self.i = 0

    def next(self):
        j = self.i % len(self.bufs)
        self.i += 1
        return self.bufs[j], (self.name, j)


class Builder:
    def __init__(self, debug=None, layers=(0, 1, 2, 3), n_seq=2, stop_after=None):
        self.debug = debug or []
        self.layers = list(layers)
        self.n_seq = n_seq
        self.stop_after = stop_after
        self.cut = 99
        nc = self.nc = bass.Bass("TRN2", target_bir_lowering=False)
        self.P = Prog(nc)
        self.dram = {}
        self.dbg_out = {}

    def din(self, name, shape, dt=F32):
        t = self.nc.dram_tensor(name, list(shape), dt, kind="ExternalInput").ap()
        self.dram[name] = t
        return t

    def dout(self, name, shape, dt=F32):
        t = self.nc.dram_tensor(name, list(shape), dt, kind="ExternalOutput").ap()
        self.dram[name] = t
        return t

    def ps_s(self):
        i = self._ps_s % 4
        self._ps_s += 1
        return self.psb[i], ("ps", i)

    def ps_s2(self):
        if self._ps_s % 2 == 1:
            self._ps_s += 1
        i = self._ps_s % 4
        self._ps_s += 2
        return self.psd[i // 2], (("ps", i), ("ps", i + 1))

    def ps_a(self):
        i = 4 + self._ps_a % 4
        self._ps_a += 1
        return self.psb[i], ("ps", i)

    def wload(self, view, shape):
        a, b = shape
        assert a * b <= 2048, (a, b)
        s = self._wslot % len(self.wring)
        self._wslot += 1
        dst = self.wring[s][:, 0:a * b].rearrange("p (a b) -> p a b", a=a)
        key = ("w", s)
        self.P.dma("pool", dst, view, W=[key])
        return dst, key

    def mmk(self, out, pairs, R, pk):
        n = len(pairs)
        for i, (lt, rh) in enumerate(pairs):
            self.P.op("pe", lambda e, out=out, lt=lt, rh=rh, i=i, n=n: e.matmul(out, lt, rh, start=(i == 0), stop=(i == n - 1)),
                      R=R, W=[pk])

    def dump(self, name, ap, n, R):
        if name not in self.debug:
            return
        o = self.dout("dbg_" + name, [128, n])
        self.P.dma("sp", o, ap, R=R)

    def build(self):
        nc, P = self.nc, self.P
        hT = self.din("hT", [2, D, T])
        cT = self.din("cT", [128, KC, 3])
        ada_w = self.din("ada_w", [NL, D, 9 * D])
        ada_bT = self.din("ada_bT", [128, NL, 72])
        norm_wT = self.din("norm_wT", [128, NL * 3, KC])
        fnorm_wT = self.din("fnorm_wT", [128, KC])
        ffn_wg = self.din("ffn_wg_l", [NL, 2, 11, 128, KC * 256])
        ffn_wu = self.din("ffn_wu_l", [NL, 2, 11, 128, KC * 256])
        ffn_wd = self.din("ffn_w_down", [NL, 2, DFF, D])
        outT = self.dout("outT", [2, D, S])
        self.declare_mixer_inputs()

        mem = self.mem = Mem(nc, 206 * 1024)
        self.psd = [nc.alloc_psum_tensor("psd%d" % i, [128, 1024], F32) for i in range(2)]
        self.psb = [self.psd[i // 2][:, (i % 2) * 512:(i % 2 + 1) * 512] for i in range(4)]
        self.psb += [nc.alloc_psum_tensor("ps%d" % i, [128, 512], F32) for i in range(4, 8)]
        self._ps_s = 0
        self._ps_a = 0
        self._wslot = 0
        hx = self.hx = mem.alloc([KC, T], F32)
        self.wring = [mem.alloc([2048], BF16) for _ in range(8)]
        self.mod = mem.alloc([NL, 72, 3], F32)
        self.Amod = mem.alloc([NL * 3, KC, 3], F32)
        self.Gmod = mem.alloc([NL * 3, KC, 3], F32)
        self.normw = mem.alloc([NL * 3, KC], F32)
        self.fnormw = mem.alloc([KC], F32)
        self.ones_bf = mem.alloc([128], BF16)
        self.ostage = Rot(mem, "ostage", 2, [512], F32)
        self.alloc_mixer_consts()
        self.arena0 = mem.mark()

        P.op("dve", lambda e: e.memset(self.ones_bf, 1.0), W=["ones"])
        P.dma("sp", self.normw, norm_wT, W=["normw"])
        P.dma("sp", self.fnormw, fnorm_wT, W=["fnormw"])
        self.load_mixer_consts()

        self.prologue_mod(cT, ada_w, ada_bT)

        for s in range(self.n_seq):
            P.dma("sp", hx, hT[s].rearrange("(c p) t -> p c t", p=128), W=[("hx", tt) for tt in range(5)])
            for l in self.layers:
                last = (l == NL - 1)
                self.ffn(s, l, 0, [0, 1, 2, 3, 4], ffn_wg, ffn_wu, ffn_wd)
                if self.stop_after == ("ffn0", l):
                    break
                self.mixer(s, l)
                if self.stop_after == ("mixer", l):
                    break
                self.ffn(s, l, 1, [1, 2, 3, 4] if last else [0, 1, 2, 3, 4], ffn_wg, ffn_wu, ffn_wd)
            if "hx" in self.debug and s == 0:
                o = self.dout("dbg_hx", [128, KC, T])
                P.dma("sp", o, hx, R=[("hx", tt) for tt in range(5)])
            self.final_out(s, outT)
        P.wait_all_dma("sp")
        P.emit()
        return nc

    def prologue_mod(self, cT, ada_w, ada_bT):
        P, mem = self.P, self.mem
        m0 = mem.mark()
        sc = mem.alloc([KC, 3], F32)
        bT = mem.alloc([NL, 72], F32)
        stage = Rot(mem, "adastage", 3, [KC, 512], BF16)
        scb = mem.alloc([KC, 3], BF16)
        P.dma("sp", sc, cT, W=["sc"])
        P.dma("sp", bT, ada_bT, W=["bT"])
        P.op("act", lambda e: e.activation(scb, sc, AF.Silu), R=["sc"], W=["sc"])
        for l in self.layers:
            ps, pk = self.ps_a()
            psv = ps[:, 0:216].rearrange("p (a b) -> p a b", b=3)
            for j2 in range(18):
                st, sk = stage.next()
                P.dma("pool", st, ada_w[l, :, j2 * 512:(j2 + 1) * 512].rearrange("(k p) n -> p k n", p=128), W=[sk])
                for c in range(4):
                    for kc in range(8):
                        P.op("pe", lambda e, st=st, c=c, kc=kc, j2=j2, psv=psv: e.matmul(
                            psv[:, j2 * 4 + c, :], st[:, kc, c * 128:(c + 1) * 128], scb[:, kc, :],
                            start=(kc == 0), stop=(kc == 7)), R=[sk, "sc"], W=[pk])
            P.op("dve", lambda e, l=l, psv=psv: e.tensor_tensor(
                self.mod[:, l], psv, bT[:, l].unsqueeze(2).broadcast_to([128, 72, 3]), ALU.add),
                R=[pk, "bT"], W=["mod"])
            for n in range(3):
                ln = l * 3 + n
                P.op("dve", lambda e, l=l, n=n, ln=ln: e.scalar_tensor_tensor(
                    self.Amod[:, ln], self.mod[:, l, (3 * n + 1) * 8:(3 * n + 2) * 8, :], 1.0,
                    self.normw[:, ln].unsqueeze(2).broadcast_to([128, KC, 3]), ALU.add, ALU.mult),
                    R=["mod", "normw"], W=["Amod"])
                P.op("dve", lambda e, l=l, n=n, ln=ln: e.tensor_scalar(
                    self.Gmod[:, ln], self.mod[:, l, (3 * n + 2) * 8:(3 * n + 3) * 8, :],
                    (1.0 if n == 1 else 0.5), None, ALU.mult), R=["mod"], W=["Gmod"])
        self.prologue_mixer()
        P.barrier(wait_dma=True)
        mem.reset(m0)

    def shift_ap(self, l, n, c, col):
        return self.mod[:, l, (3 * n) * 8 + c, col:col + 1]

    def prenorm(self, s, l, n, tiles, h, rots):
        P = self.P
        sqr, sdr, tmpr = rots
        hx = self.hx
        ln = l * 3 + n
        for tt in tiles:
            t0, N = TT[tt]
            col = 2 if tt == 0 else s
            sq, sqk = sqr.next()
            P.op("act", lambda e, sq=sq, t0=t0, N=N: e.activation(sq[:, :, 0:N], hx[:, :, t0:t0 + N], AF.Square),
                 R=[("hx", tt)], W=[sqk])
            ps, pk = self.ps_s()
            for kc in range(KC):
                P.op("pe", lambda e, ps=ps, sq=sq, kc=kc, N=N: e.matmul(
                    ps[:, 0:N], self.ones_bf, sq[:, kc, 0:N], start=(kc == 0), stop=(kc == KC - 1)),
                    R=[sqk, "ones"], W=[pk])
            sd, sdk = sdr.next()
            P.op("act", lambda e, sd=sd, ps=ps, N=N: e.activation(sd[:, 0:N], ps[:, 0:N], AF.Ln, bias=self.eps_ap, scale=1.0 / D),
                 R=[pk, "eps"], W=[sdk])
            P.op("act", lambda e, sd=sd, N=N: e.activation(sd[:, 0:N], sd[:, 0:N], AF.Exp, scale=-0.5), R=[sdk], W=[sdk])
            for c in range(KC):
                tmp, tk = tmpr.next()
                P.op("dve", lambda e, tmp=tmp, c=c, t0=t0, N=N, sd=sd: e.tensor_tensor(
                    tmp[:, 0:N], hx[:, c, t0:t0 + N], sd[:, 0:N], ALU.mult), R=[("hx", tt), sdk], W=[tk])
                P.op("act", lambda e, tmp=tmp, c=c, t0=t0, N=N, col=col: e.activation(
                    h[:, c, t0:t0 + N], tmp[:, 0:N], AF.Identity,
                    bias=self.shift_ap(l, n, c, col), scale=self.Amod[:, ln, c, col:col + 1]),
                    R=[tk], W=[("h", tt)])

    def ffn(self, s, l, which, tiles, ffn_wg, ffn_wu, ffn_wd):
        P, mem = self.P, self.mem
        hx = self.hx
        n = 0 if which == 0 else 2
        ln = l * 3 + n
        P.barrier()
        m0 = mem.mark()
        h = mem.alloc([KC, T], BF16)
        rots = (Rot(mem, "sq", 2, [KC, 512], BF16), Rot(mem, "sd", 2, [512], F32), Rot(mem, "ntmp", 3, [512], F32))
        sgr = Rot(mem, "sg", 3, [512], BF16)
        actr = Rot(mem, "act", 3, [2, 512], BF16)
        groups = [(f0, 2) for f0 in range(0, 22, 2)]

        def down(tt, act, actk, wd, wdk, nf):
            t0, N = TT[tt]
            col = 2 if tt == 0 else s
            for dc in range(KC):
                pd, pdk = self.ps_a()
                self.mmk(pd[:, 0:N], [(wd[:, fc, dc * 128:(dc + 1) * 128], act[:, fc, 0:N]) for fc in range(nf)], [wdk, actk], pdk)
                P.op("dve", lambda e, pd=pd, dc=dc: e.scalar_tensor_tensor(
                    hx[:, dc, t0:t0 + N], pd[:, 0:N], self.Gmod[:, ln, dc, col:col + 1], hx[:, dc, t0:t0 + N],
                    ALU.mult, ALU.add), R=[pdk, ("hx", tt)], W=[("hx", tt)])

        pend_d = None
        for gi, (f0, nf) in enumerate(groups):
            wg, wgk = self.wload(ffn_wg[l, which, gi].rearrange("p (k n) -> p k n", k=KC), (KC, 256))
            wu, wuk = self.wload(ffn_wu[l, which, gi].rearrange("p (k n) -> p k n", k=KC), (KC, 256))
            wd, wdk = self.wload(ffn_wd[l, which, f0 * 128:(f0 + nf) * 128, :].rearrange("(k p) n -> p k n", p=128), (nf, D))
            for tt in tiles:
                t0, N = TT[tt]
                col = 2 if tt == 0 else s
                if gi == 0:
                    self.prenorm(s, l, n, [tt], h, rots)
                act, actk = actr.next()
                for fc in range(nf):
                    pg, pgk = self.ps_s()
                    for kc in range(KC):
                        P.op("pe", lambda e, pg=pg, wg=wg, fc=fc, kc=kc, t0=t0, N=N: e.matmul(
                            pg[:, 0:N], wg[:, kc, fc * 128:(fc + 1) * 128], h[:, kc, t0:t0 + N],
                            start=(kc == 0), stop=(kc == KC - 1)), R=[wgk, ("h", tt)], W=[pgk])
                    pu, puk = self.ps_s()
                    for kc in range(KC):
                        P.op("pe", lambda e, pu=pu, wu=wu, fc=fc, kc=kc, t0=t0, N=N: e.matmul(
                            pu[:, 0:N], wu[:, kc, fc * 128:(fc + 1) * 128], h[:, kc, t0:t0 + N],
                            start=(kc == 0), stop=(kc == KC - 1)), R=[wuk, ("h", tt)], W=[puk])
                    sg, sgk = sgr.next()
                    P.op("act", lambda e, sg=sg, pg=pg, N=N: e.activation(sg[:, 0:N], pg[:, 0:N], AF.Silu), R=[pgk], W=[sgk])
                    P.op("dve", lambda e, act=act, fc=fc, pu=pu, sg=sg, N=N: e.tensor_tensor(
                        act[:, fc, 0:N], pu[:, 0:N], sg[:, 0:N], ALU.mult), R=[puk, sgk], W=[actk])
                if pend_d is not None:
                    down(*pend_d)
                pend_d = (tt, act, actk, wd, wdk, nf)
        down(*pend_d)
        P.barrier()
        mem.reset(m0)

    def final_out(self, s, outT):
        P, mem = self.P, self.mem
        hx = self.hx
        P.barrier()
        m0 = mem.mark()
        sqr = Rot(mem, "sq", 2, [KC, 512], BF16)
        sdr = Rot(mem, "sd", 2, [512], F32)
        for tt in [1, 2, 3, 4]:
            t0, N = TT[tt]
            sq, sqk = sqr.next()
            P.op("act", lambda e, sq=sq, t0=t0, N=N: e.activation(sq[:, :, 0:N], hx[:, :, t0:t0 + N], AF.Square),
                 R=[("hx", tt)], W=[sqk])
            ps, pk = self.ps_s()
            for kc in range(KC):
                P.op("pe", lambda e, ps=ps, sq=sq, kc=kc, N=N: e.matmul(
                    ps[:, 0:N], self.ones_bf, sq[:, kc, 0:N], start=(kc == 0), stop=(kc == KC - 1)),
                    R=[sqk, "ones"], W=[pk])
            sd, sdk = sdr.next()
            P.op("act", lambda e, sd=sd, ps=ps, N=N: e.activation(sd[:, 0:N], ps[:, 0:N], AF.Ln, bias=self.eps_ap, scale=1.0 / D),
                 R=[pk, "eps"], W=[sdk])
            P.op("act", lambda e, sd=sd, N=N: e.activation(sd[:, 0:N], sd[:, 0:N], AF.Exp, scale=-0.5), R=[sdk], W=[sdk])
            for c in range(KC):
                o, ok = self.ostage.next()
                P.op("dve", lambda e, o=o, c=c, t0=t0, N=N, sd=sd: e.scalar_tensor_tensor(
                    o[:, 0:N], hx[:, c, t0:t0 + N], self.fnormw[:, c:c + 1], sd[:, 0:N], ALU.mult, ALU.mult),
                    R=[("hx", tt), sdk, "fnormw"], W=[ok])
                P.dma("sp", outT[s, c * 128:(c + 1) * 128, t0 - CT:t0 - CT + N], o[:, 0:N], R=[ok])
        P.barrier()
        mem.reset(m0)

    def declare_mixer_inputs(self):
        self.din("rope64", [128, 2, S])
        self.din("rope32", [128, 2, S])
        self.din("ident", [128, 128])
        self.din("gmask", [128, 2, 512])
        self.din("gqa_pA", [4, 2, D, 256])
        self.din("gqa_pB", [4, D, 256])
        self.din("gqa_pV", [4, D, 64])
        self.din("gqa_w_out", [D, D])
        self.din("gqa_sinks_b", [128, 16])
        self.din("diff_pQ", [8, D, 256])
        self.din("diff_pK", [8, D, 256])
        self.din("diff_pV", [8, D, 128])
        self.din("diff_w_out", [D, D])
        self.din("diff_lam_b", [128, 4, 64])
        self.din("diff_subln_b", [128, 1])
        self.din("mla_pD", [3, D, 256])
        self.din("mla_pUQ", [16, 256, 256])
        self.din("mla_pUKV", [16, 256, 128])
        self.din("mla_w_out", [D, D])
        self.din("mla_nw", [128, 2, 2])
        self.din("hgrn_pA", [8, D, 256])
        self.din("hgrn_pB", [8, D, 256])
        self.din("hgrn_pV", [8, D, 128])
        self.din("hgrn_w_out", [D, D])
        self.din("hgrn_lbT", [128, 4, 8])
        self.din("hgrn_nw", [128, 1])
        self.din("rmask", [128, 512])
        self.din("trimask", [128, 2, 64])

    def alloc_mixer_consts(self):
        mem = self.mem
        self.eps_ap = mem.alloc([1], F32)
        self.rope = mem.alloc([2, S], BF16)
        self.ident = mem.alloc([128], BF16)
        self.gmask = mem.alloc([2, 512], BF16)
        self.sinkexp = mem.alloc([16], F32)
        self.neglam = mem.alloc([1], F32)
        self.sublnw = mem.alloc([1], F32)
        self.mla_nw = mem.alloc([2, 2], F32)
        self.lb = mem.alloc([8], F32)
        self.oml = mem.alloc([8], F32)
        self.noml = mem.alloc([8], F32)
        self.one_ap = mem.alloc([1], F32)
        self.hnw = mem.alloc([1], F32)
        self.rmask = mem.alloc([512], F32)
        self.trimask = mem.alloc([2, 64], BF16)

    def load_mixer_consts(self):
        P = self.P
        P.op("dve", lambda e: e.memset(self.eps_ap, EPS), W=["eps"])
        P.op("dve", lambda e: e.memset(self.one_ap, 1.0), W=["one"])
        P.dma("pool", self.ident, self.dram["ident"], W=["ident"])
        P.dma("pool", self.gmask, self.dram["gmask"], W=["gmask"])
        P.dma("sp", self.sinkexp, self.dram["gqa_sinks_b"], W=["sinkexp"])
        P.dma("sp", self.mla_nw, self.dram["mla_nw"], W=["mla_nw"])
        P.dma("sp", self.hnw, self.dram["hgrn_nw"], W=["hnw"])
        P.dma("sp", self.rmask, self.dram["rmask"], W=["rmask"])
        P.dma("pool", self.trimask, self.dram["trimask"], W=["trimask"])
        P.op("act", lambda e: e.activation(self.sinkexp, self.sinkexp, AF.Exp), R=["sinkexp"], W=["sinkexp"])

    def prologue_mixer(self):
        P, mem = self.P, self.mem
        d = self.dram
        lam_init = 0.8 - 0.6 * math.exp(-0.3 * 1)
        lp = mem.alloc([4, 64], F32)
        pr = mem.alloc([2, 64], F32)
        ss = mem.alloc([2], F32)
        P.dma("sp", lp, d["diff_lam_b"], W=["lp"])
        P.dma("sp", self.sublnw, d["diff_subln_b"], W=["sublnw"])
        P.op("dve", lambda e: e.tensor_tensor(pr[:, 0, :], lp[:, 0, :], lp[:, 1, :], ALU.mult), R=["lp"], W=["pr"])
        P.op("dve", lambda e: e.tensor_tensor(pr[:, 1, :], lp[:, 2, :], lp[:, 3, :], ALU.mult), R=["lp"], W=["pr"])
        P.op("dve", lambda e: e.reduce_sum(ss[:, 0:1], pr[:, 0, :], axis=AX.X), R=["pr"], W=["ss"])
        P.op("dve", lambda e: e.reduce_sum(ss[:, 1:2], pr[:, 1, :], axis=AX.X), R=["pr"], W=["ss"])
        P.op("act", lambda e: e.activation(ss, ss, AF.Exp), R=["ss"], W=["ss"])
        P.op("dve", lambda e: e.scalar_tensor_tensor(self.neglam, ss[:, 1:2], -lam_init, ss[:, 0:1], ALU.add, ALU.subtract),
             R=["ss"], W=["neglam"])
        P.op("dve", lambda e: e.tensor_scalar(self.sublnw, self.sublnw, 1.0 - lam_init, None, ALU.mult), R=["sublnw"], W=["sublnw"])
        le = mem.alloc([4, 8], F32)
        den = mem.alloc([8], F32)
        P.dma("sp", le, d["hgrn_lbT"], W=["le"])
        P.op("act", lambda e: e.activation(le, le, AF.Exp), R=["le"], W=["le"])
        P.op("dve", lambda e: e.tensor_tensor(self.lb, le[:, 1, :], le[:, 2, :], ALU.add), R=["le"], W=["lb"])
        P.op("dve", lambda e: e.tensor_tensor(den, le[:, 0, :], le[:, 3, :], ALU.add), R=["le"], W=["den"])
        P.op("dve", lambda e: e.tensor_tensor(den, den, self.lb, ALU.add), R=["den", "lb"], W=["den"])
        P.op("dve", lambda e: e.reciprocal(den, den), R=["den"], W=["den"])
        P.op("dve", lambda e: e.tensor_tensor(self.lb, self.lb, den, ALU.mult), R=["lb", "den"], W=["lb"])
        P.op("dve", lambda e: e.tensor_scalar(self.oml, self.lb, -1.0, 1.0, ALU.mult, ALU.add), R=["lb"], W=["oml"])
        P.op("dve", lambda e: e.tensor_scalar(self.noml, self.lb, 1.0, -1.0, ALU.mult, ALU.add), R=["lb"], W=["noml"])

    def mm(self, out, lt, rh, start, stop, R, W):
        self.P.op("pe", lambda e: e.matmul(out, lt, rh, start=start, stop=stop), R=R, W=W)

    def mixer_begin(self, s, l, tiles):
        P, mem = self.P, self.mem
        P.barrier()
        m0 = mem.mark()
        h = mem.alloc([KC, T], BF16)
        rots = (Rot(mem, "sq", 1, [KC, 512], BF16), Rot(mem, "sd", 2, [512], F32), Rot(mem, "ntmp", 3, [512], F32))
        self.prenorm(s, l, 1, tiles, h, rots)
        P.barrier()
        mem.reset(m0)
        h = mem.alloc([KC, T], BF16)
        return m0, h

    def mixer_end(self, m0):
        self.P.barrier()
        self.mem.reset(m0)

    def rope_evac(self, psA, pkA, psB, pkB, dst, N, tokx, W, rows=slice(0, 128)):
        P = self.P
        t1, t2 = self.t1, self.t2
        cos = self.rope[rows, 0, tokx:tokx + N]
        sin = self.rope[rows, 1, tokx:tokx + N]
        P.op("dve", lambda e: e.tensor_tensor(t1[rows, 0:N], psA[rows, 0:N], cos, ALU.mult), R=[pkA, "rope"], W=["t1"])
        P.op("dve", lambda e: e.tensor_tensor(t2[rows, 0:N], psB[rows, 0:N], sin, ALU.mult), R=[pkB, "rope"], W=["t2"])
        P.op("dve", lambda e: e.tensor_tensor(dst, t1[rows, 0:N], t2[rows, 0:N], ALU.add), R=["t1", "t2"], W=W)

    def outproj_tile(self, s, l, tt, wO, kO, oT, ok, nk, pool="a"):
        P = self.P
        hx = self.hx
        t0, N = TT[tt]
        col = 2 if tt == 0 else s
        ln = l * 3 + 1
        for dc in range(KC):
            pd, pdk = self.ps_s() if pool == "s" else self.ps_a()
            self.mmk(pd[:, 0:N], [(wO[:, kc, dc * 128:(dc + 1) * 128], oT[:, kc, 0:N]) for kc in range(nk)], [kO, ok], pdk)
            P.op("dve", lambda e, pd=pd, dc=dc: e.scalar_tensor_tensor(
                hx[:, dc, t0:t0 + N], pd[:, 0:N], self.Gmod[:, ln, dc, col:col + 1], hx[:, dc, t0:t0 + N],
                ALU.mult, ALU.add), R=[pdk, ("hx", tt)], W=[("hx", tt)])

    @staticmethod
    def tile_of_block(tb):
        return 0 if tb < 2 else 1 + (tb - 2) // 4

    def mixer(self, s, l):
        kind = l % 4
        if kind == 0:
            self.gqa(s, l)
        elif kind == 1:
            self.diffattn(s, l)
        elif kind == 2:
            self.hgrn(s, l)
        else:
            self.mla(s, l)

    def gqa(self, s, l):
        P, mem = self.P, self.mem
        tiles = [0, 1, 2, 3, 4]
        m0, h = self.mixer_begin(s, l, tiles)
        d = self.dram
        P.dma("pool", self.rope, d["rope64"], W=["rope"])
        kTs = [mem.alloc([T], BF16), mem.alloc([T], BF16)]
        vaug = mem.alloc([18, 2, 128], BF16)
        qr = Rot(mem, "qT", 2, [2, 512], BF16)
        ptr = Rot(mem, "PT", 2, [5, 512], BF16)
        self.t1 = mem.alloc([512], F32)
        self.t2 = mem.alloc([512], F32)
        otr = Rot(mem, "oT", 2, [2, 512], BF16)
        dtr = Rot(mem, "dt", 2, [128], F32)
        P.op("dve", lambda e: e.memset(vaug, 1.0), W=["vaug"])
        P.op("dve", lambda e: e.memset(kTs[0], 0.0), W=[("kT", tt) for tt in tiles])
        P.op("dve", lambda e: e.memset(kTs[1], 0.0), W=[("kT", tt) for tt in tiles])
        for j in range(4):
            wA1, k1 = self.wload(d["gqa_pA"][j, 0].rearrange("(k p) n -> p k n", p=128), (KC, 256))
            wA2, k2 = self.wload(d["gqa_pA"][j, 1].rearrange("(k p) n -> p k n", p=128), (KC, 256))
            wB, kB = self.wload(d["gqa_pB"][j].rearrange("(k p) n -> p k n", p=128), (KC, 256))
            wV, kV = self.wload(d["gqa_pV"][j].rearrange("(k p) n -> p k n", p=128), (KC, 64))
            wO, kO = self.wload(d["gqa_w_out"][j * 256:(j + 1) * 256, :].rearrange("(k p) n -> p k n", p=128), (2, D))
            for tt in tiles:
                t0, N = TT[tt]
                psA, pkA = self.ps_s()
                self.mmk(psA[:, 0:N], [(wB[:, kc, 0:128], h[:, kc, t0:t0 + N]) for kc in range(KC)], [kB, ("h", tt)], pkA)
                rws = (slice(0, 64), slice(64, 128))
                if tt == 0:
                    for i in range(2):
                        P.op("act", lambda e, psA=psA, t0=t0, N=N, i=i: e.activation(kTs[i][rws[i], t0:t0 + N], psA[rws[i], 0:N], AF.Copy),
                             R=[pkA], W=[("kT", tt)])
                else:
                    psB, pkB = self.ps_s()
                    self.mmk(psB[:, 0:N], [(wB[:, kc, 128:256], h[:, kc, t0:t0 + N]) for kc in range(KC)], [kB, ("h", tt)], pkB)
                    for i in range(2):
                        self.rope_evac(psA, pkA, psB, pkB, kTs[i][rws[i], t0:t0 + N], N, t0 - CT, [("kT", tt)], rows=rws[i])
            if self.cut <= 1:
                break
            for g4 in range(0, 18, 4):
                nb = min(4, 18 - g4)
                ps, pk = self.ps_s()
                for bi in range(nb):
                    tb = g4 + bi
                    self.mmk(ps[:, bi * 64:(bi + 1) * 64], [(h[:, kc, tb * 128:(tb + 1) * 128], wV[:, kc, :]) for kc in range(KC)],
                             [kV, ("h", self.tile_of_block(tb))], pk)
                src = ps[:, 0:nb * 64].rearrange("p (a b) -> p a b", b=64)
                P.op("act", lambda e, src=src, g4=g4, nb=nb: e.activation(vaug[:, g4:g4 + nb, 0, 0:64], src, AF.Copy), R=[pk], W=["vaug"])
                P.op("dve", lambda e, src=src, g4=g4, nb=nb: e.tensor_copy(vaug[:, g4:g4 + nb, 1, 64:128], src), R=[pk], W=["vaug"])
            if self.cut <= 2:
                break
            pend_o = None

            def make_q(tt, wA1=wA1, wA2=wA2, k1=k1, k2=k2):
                t0, N = TT[tt]
                qT, qk = qr.next()
                for qc in range(2):
                    psA, pkA = self.ps_s()
                    self.mmk(psA[:, 0:N], [(wA1[:, kc, qc * 128:(qc + 1) * 128], h[:, kc, t0:t0 + N]) for kc in range(KC)], [k1, ("h", tt)], pkA)
                    if tt == 0:
                        P.op("act", lambda e, psA=psA, qc=qc, N=N, qT=qT: e.activation(qT[:, qc, 0:N], psA[:, 0:N], AF.Copy), R=[pkA], W=[qk])
                    else:
                        psB, pkB = self.ps_s()
                        self.mmk(psB[:, 0:N], [(wA2[:, kc, qc * 128:(qc + 1) * 128], h[:, kc, t0:t0 + N]) for kc in range(KC)], [k2, ("h", tt)], pkB)
                        self.rope_evac(psA, pkA, psB, pkB, qT[:, qc, 0:N], N, t0 - CT, [qk])
                return qT, qk

            qnext = make_q(tiles[0])
            for ti, tt in enumerate(tiles):
                t0, N = TT[tt]
                nqb = N // 128
                qT, qk = qnext
                if ti + 1 < len(tiles):
                    qnext = make_q(tiles[ti + 1])
                oT, ok = otr.next()

                def issue_scores(qi, tt=tt, t0=t0, qT=qT, qk=qk):
                    tbq = (t0 + qi * 128) // 128
                    if tt == 0:
                        kbs = [(0, None), (1, None)]
                    else:
                        kbs = []
                        if tbq > 2:
                            kbs.append((tbq - 1, 0))
                        kbs.append((tbq, None))
                        if tbq < 17:
                            kbs.append((tbq + 1, 1))
                        kbs += [(0, None), (1, None)]
                    PT, ptk = ptr.next()
                    for kbi, (kb, m) in enumerate(kbs):
                        ps, pk = self.ps_s()
                        first = True
                        if m is not None:
                            self.mm(ps[:, 0:512], self.ident, self.gmask[:, m, :], True, False, [], [pk])
                            first = False
                        for hh in range(4):
                            qc = hh // 2
                            self.mm(ps[:, hh * 128:(hh + 1) * 128], kTs[hh % 2][:, kb * 128:(kb + 1) * 128],
                                    qT[:, qc, qi * 128:(qi + 1) * 128], first, hh == 3,
                                    [("kT", self.tile_of_block(kb)), qk], [pk])
                            first = False
                        P.op("act", lambda e, PT=PT, kbi=kbi, ps=ps: e.activation(PT[:, kbi, :], ps[:, 0:512], AF.Exp, scale=0.125),
                             R=[pk], W=[ptk])
                    return PT, ptk, kbs

                def consume(qi, PT, ptk, kbs, oT=oT, ok=ok, j=j):
                    po_, pok = self.ps_a()
                    for hh in range(4):
                        var = hh % 2
                        self.mmk(po_[:, hh * 128:(hh + 1) * 128],
                                 [(vaug[:, kb, var, :], PT[:, kbi, hh * 128:(hh + 1) * 128]) for kbi, (kb, m) in enumerate(kbs)],
                                 [ptk, "vaug"], pok)
                    for hh in range(4):
                        head = 4 * j + hh
                        qc = hh // 2
                        orow, drow = (slice(0, 64), slice(64, 128)) if hh % 2 == 0 else (slice(64, 128), slice(0, 64))
                        cols = slice(hh * 128, (hh + 1) * 128)
                        dt, dk = dtr.next()
                        P.op("act", lambda e, dt=dt, orow=orow, drow=drow, cols=cols, head=head, po_=po_: e.activation(
                            dt[orow, :], po_[drow, cols], AF.Identity, bias=self.sinkexp[drow, head:head + 1]), R=[pok], W=[dk])
                        P.op("dve", lambda e, dt=dt, orow=orow: e.reciprocal(dt[orow, :], dt[orow, :]), R=[dk], W=[dk])
                        P.op("dve", lambda e, dt=dt, orow=orow, cols=cols, qc=qc, qi=qi, oT=oT, po_=po_: e.tensor_tensor(
                            oT[orow, qc, qi * 128:(qi + 1) * 128], po_[orow, cols], dt[orow, :], ALU.mult), R=[pok, dk], W=[ok])

                nxt = issue_scores(0)
                for qi in range(nqb):
                    cur = nxt
                    if qi + 1 < nqb:
                        nxt = issue_scores(qi + 1)
                    consume(qi, *cur)
                if pend_o is not None:
                    self.outproj_tile(*pend_o)
                pend_o = (s, l, tt, wO, kO, oT, ok, 2, "a")
            self.outproj_tile(*pend_o)
        self.mixer_end(m0)

    def diffattn(self, s, l):
        P, mem = self.P, self.mem
        d = self.dram
        tiles = [0, 1, 2, 3, 4]
        m0, h = self.mixer_begin(s, l, tiles)
        P.dma("pool", self.rope, d["rope64"], W=["rope"])
        kTs = [mem.alloc([T], BF16), mem.alloc([T], BF16)]
        V = mem.alloc([18, 128], BF16)
        qr = Rot(mem, "qT", 2, [512], BF16)
        ptr = Rot(mem, "PT", 3, [1024], BF16)
        self.t1 = mem.alloc([512], F32)
        self.t2 = mem.alloc([512], F32)
        rr = Rot(mem, "rj", 2, [512], F32)
        tr = Rot(mem, "tj", 2, [512], F32)
        combr = Rot(mem, "comb", 2, [512], F32)
        sqb = mem.alloc([512], BF16)
        sd = mem.alloc([512], F32)
        otr = Rot(mem, "oT", 2, [1, 512], BF16)

        def finish(tt, comb, ck, wO, kO):
            t0, N = TT[tt]
            P.op("act", lambda e: e.activation(sqb[:, 0:N], comb[:, 0:N], AF.Square), R=[ck], W=["sqb"])
            ps, pk = self.ps_s()
            self.mm(ps[:, 0:N], self.ones_bf, sqb[:, 0:N], True, True, ["sqb"], [pk])
            P.op("act", lambda e: e.activation(sd[:, 0:N], ps[:, 0:N], AF.Ln, bias=self.eps_ap, scale=1.0 / 128), R=[pk], W=["sd"])
            P.op("act", lambda e: e.activation(sd[:, 0:N], sd[:, 0:N], AF.Exp, scale=-0.5), R=["sd"], W=["sd"])
            P.op("dve", lambda e: e.tensor_tensor(comb[:, 0:N], comb[:, 0:N], sd[:, 0:N], ALU.mult), R=[ck, "sd"], W=[ck])
            oT, ok = otr.next()
            P.op("act", lambda e: e.activation(oT[:, 0, 0:N], comb[:, 0:N], AF.Identity, scale=self.sublnw[:, 0:1]), R=[ck], W=[ok])
            self.outproj_tile(s, l, tt, wO, kO, oT, ok, 1, pool="s")

        P.op("dve", lambda e: e.memset(kTs[0], 0.0), W=[("kT", tt) for tt in tiles])
        P.op("dve", lambda e: e.memset(kTs[1], 0.0), W=[("kT", tt) for tt in tiles])
        rws = (slice(0, 64), slice(64, 128))
        for hd in range(8):
            wQ, kQ = self.wload(d["diff_pQ"][hd].rearrange("(k p) n -> p k n", p=128), (KC, 256))
            wK, kK = self.wload(d["diff_pK"][hd].rearrange("(k p) n -> p k n", p=128), (KC, 256))
            wV, kV = self.wload(d["diff_pV"][hd].rearrange("(k p) n -> p k n", p=128), (KC, 128))
            wO, kO = self.wload(d["diff_w_out"][hd * 128:(hd + 1) * 128, :].rearrange("(k p) n -> p k n", p=128), (1, D))
            for tt in tiles:
                t0, N = TT[tt]
                psA, pkA = self.ps_s()
                self.mmk(psA[:, 0:N], [(wK[:, kc, 0:128], h[:, kc, t0:t0 + N]) for kc in range(KC)], [kK, ("h", tt)], pkA)
                if tt == 0:
                    for i in range(2):
                        P.op("act", lambda e, psA=psA, t0=t0, N=N, i=i: e.activation(kTs[i][rws[i], t0:t0 + N], psA[rws[i], 0:N], AF.Copy),
                             R=[pkA], W=[("kT", tt)])
                else:
                    psB, pkB = self.ps_s()
                    self.mmk(psB[:, 0:N], [(wK[:, kc, 128:256], h[:, kc, t0:t0 + N]) for kc in range(KC)], [kK, ("h", tt)], pkB)
                    for i in range(2):
                        self.rope_evac(psA, pkA, psB, pkB, kTs[i][rws[i], t0:t0 + N], N, t0 - CT, [("kT", tt)], rows=rws[i])
            for g4 in range(0, 18, 4):
                nb = min(4, 18 - g4)
                ps, pk = self.ps_s()
                for bi in range(nb):
                    tb = g4 + bi
                    self.mmk(ps[:, bi * 128:(bi + 1) * 128], [(h[:, kc, tb * 128:(tb + 1) * 128], wV[:, kc, :]) for kc in range(KC)],
                             [kV, ("h", self.tile_of_block(tb))], pk)
                src = ps[:, 0:nb * 128].rearrange("p (a b) -> p a b", b=128)
                P.op("act", lambda e, src=src, g4=g4, nb=nb: e.activation(V[:, g4:g4 + nb, :], src, AF.Copy), R=[pk], W=["V"])
            pend_o = None

            def make_q(tt, wQ=wQ, kQ=kQ):
                t0, N = TT[tt]
                qT, qk = qr.next()
                psA, pkA = self.ps_s()
                self.mmk(psA[:, 0:N], [(wQ[:, kc, 0:128], h[:, kc, t0:t0 + N]) for kc in range(KC)], [kQ, ("h", tt)], pkA)
                if tt == 0:
                    P.op("act", lambda e, psA=psA, N=N, qT=qT: e.activation(qT[:, 0:N], psA[:, 0:N], AF.Copy), R=[pkA], W=[qk])
                else:
                    psB, pkB = self.ps_s()
                    self.mmk(psB[:, 0:N], [(wQ[:, kc, 128:256], h[:, kc, t0:t0 + N]) for kc in range(KC)], [kQ, ("h", tt)], pkB)
                    self.rope_evac(psA, pkA, psB, pkB, qT[:, 0:N], N, t0 - CT, [qk])
                return qT, qk

            qnext = make_q(tiles[0])
            for ti, tt in enumerate(tiles):
                t0, N = TT[tt]
                qT, qk = qnext
                if ti + 1 < len(tiles):
                    qnext = make_q(tiles[ti + 1])
                kbs = [0, 1] if tt == 0 else list(range(18))
                tjs = []
                for j in range(2):
                    O, Ok = self.ps_a()
                    Dn, Dk = self.ps_a()
                    def issue_pair(pr, j=j, qT=qT, qk=qk, N=N):
                        dps, dk = self.ps_s2()
                        for hf, kb in enumerate(pr):
                            self.mm(dps[:, hf * 512:hf * 512 + N], kTs[j][:, kb * 128:(kb + 1) * 128], qT[:, 0:N], True, True,
                                    [("kT", self.tile_of_block(kb)), qk], [dk[hf]])
                        PT, ptk = ptr.next()
                        P.op("act", lambda e, PT=PT, dps=dps, N=N: e.activation(
                            PT.rearrange("p (a b) -> p a b", a=2)[:, :, 0:N], dps.rearrange("p (a b) -> p a b", a=2)[:, :, 0:N],
                            AF.Exp, scale=0.125), R=[dk[0], dk[1]], W=[ptk])
                        return PT, ptk
                    pairs = [kbs[i:i + 2] for i in range(0, len(kbs), 2)]
                    pend = [issue_pair(pairs[0])]
                    nkb = len(kbs)
                    for pi, pr in enumerate(pairs):
                        PT, ptk = pend[pi]
                        if pi + 1 < len(pairs):
                            pend.append(issue_pair(pairs[pi + 1]))
                        for hf, kb in enumerate(pr):
                            kbi = 2 * pi + hf
                            self.mm(O[:, 0:N], V[:, kb, :], PT[:, hf * 512:hf * 512 + N], kbi == 0, kbi == nkb - 1, [ptk, "V"], [Ok])
                            self.mm(Dn[:, 0:N], self.ones_bf, PT[:, hf * 512:hf * 512 + N], kbi == 0, kbi == nkb - 1, [ptk], [Dk])
                    rj, rk = rr.next()
                    tj, tk = tr.next()
                    P.op("dve", lambda e, rj=rj, Dn=Dn, N=N: e.reciprocal(rj[:, 0:N], Dn[:, 0:N]), R=[Dk], W=[rk])
                    P.op("dve", lambda e, tj=tj, O=O, rj=rj, N=N: e.tensor_tensor(tj[:, 0:N], O[:, 0:N], rj[:, 0:N], ALU.mult), R=[Ok, rk], W=[tk])
                    tjs.append((tj, tk))
                comb, ck = combr.next()
                P.op("dve", lambda e, tjs=tjs, N=N, comb=comb: e.scalar_tensor_tensor(
                    comb[:, 0:N], tjs[1][0][:, 0:N], self.neglam[:, 0:1], tjs[0][0][:, 0:N], ALU.mult, ALU.add),
                    R=[tjs[0][1], tjs[1][1]], W=[ck])
                if pend_o is not None:
                    finish(*pend_o)
                pend_o = (tt, comb, ck, wO, kO)
            finish(*pend_o)
        self.mixer_end(m0)

    def hgrn(self, s, l):
        P, mem = self.P, self.mem
        d = self.dram
        tiles = [0, 1, 2, 3, 4]
        m0, h = self.mixer_begin(s, l, tiles)
        Vtok = mem.alloc([36, 128], BF16)
        oacc = mem.alloc([T], F32)
        m1 = mem.mark()
        r64 = slice(0, 64)
        for hd in range(8):
            wA, kA = self.wload(d["hgrn_pA"][hd].rearrange("(k p) n -> p k n", p=128), (KC, 256))
            wB, kB = self.wload(d["hgrn_pB"][hd].rearrange("(k p) n -> p k n", p=128), (KC, 256))
            wV, kV = self.wload(d["hgrn_pV"][hd].rearrange("(k p) n -> p k n", p=128), (KC, 128))
            wO, kO = self.wload(d["hgrn_w_out"][hd * 128:(hd + 1) * 128, :].rearrange("(k p) n -> p k n", p=128), (1, D))
            for g4 in range(0, 36, 4):
                ps, pk = self.ps_s()
                for bi in range(4):
                    ch = g4 + bi
                    self.mmk(ps[r64, bi * 128:(bi + 1) * 128], [(h[:, kc, ch * 64:(ch + 1) * 64], wV[:, kc, :]) for kc in range(KC)],
                             [kV, ("h", self.tile_of_block(ch // 2))], pk)
                src = ps[r64, 0:512].rearrange("p (a b) -> p a b", b=128)
                P.op("act", lambda e, src=src, g4=g4: e.activation(Vtok[r64, g4:g4 + 4, :], src, AF.Copy), R=[pk], W=["Vtok"])
            qs_all = self.rope.rearrange("p a b -> p (a b)")[:, 0:T]
            for tt in tiles:
                t0, N = TT[tt]
                ps, pk = self.ps_s()
                self.mmk(ps[:, 0:N], [(wA[:, kc, 0:128], h[:, kc, t0:t0 + N]) for kc in range(KC)], [kA, ("h", tt)], pk)
                P.op("act", lambda e, ps=ps, t0=t0, N=N: e.activation(qs_all[:, t0:t0 + N], ps[:, 0:N], AF.Silu), R=[pk], W=["rope"])
            HT = [(i * 256, 256) for i in range(9)]
            P.op("dve", lambda e: e.memset(oacc, 0.0), W=[("oacc", i) for i in range(9)])

            def chain(dr):
                NN = 256
                F_ = mem.alloc([NN], F32)
                LF = mem.alloc([NN], F32)
                Pg = mem.alloc([NN], F32)
                D1, D4 = F_, LF
                E3 = mem.alloc([NN], F32)
                tot = mem.alloc([4], F32)
                kk = mem.alloc([NN], BF16)
                E1 = mem.alloc([NN], BF16)
                E2 = mem.alloc([NN], BF16)
                E4 = E1
                QG = mem.alloc([NN], BF16)
                KG = mem.alloc([NN], BF16)
                QE = mem.alloc([NN], BF16)
                KE = mem.alloc([NN], BF16)
                AT4 = mem.alloc([4, 64], BF16)
                KE4 = mem.alloc([4, 128], BF16)
                srot = Rot(mem, "S%d" % dr, 3, [128], F32)
                sbr = Rot(mem, "Sbf%d" % dr, 3, [128], BF16)
                yield
                K_ = lambda nm: (nm, dr)
                mi, li = (32, 63) if dr == 0 else (31, 0)
                order = list(range(9)) if dr == 0 else [0] + list(range(8, 0, -1))
                Sc, Sck = srot.next()
                P.op("dve", lambda e, Sc=Sc: e.memset(Sc, 0.0), W=[Sck])
                sb, sbk = sbr.next()
                P.op("dve", lambda e, sb=sb: e.memset(sb, 0.0), W=[sbk])
                v3 = lambda a: a[:, 0:NN].rearrange("p (c t) -> p c t", t=64)
                nch = 4
                for ti in order:
                    t0, N = HT[ti]
                    hk = ("h", 0 if ti == 0 else 1 + (ti - 1) // 2)
                    qs = qs_all[:, t0:t0 + NN]
                    ps, pk = self.ps_s()
                    wf, kf, c0 = (wA, kA, 128) if dr == 0 else (wB, kB, 0)
                    self.mmk(ps[:, 0:N], [(wf[:, kc, c0:c0 + 128], h[:, kc, t0:t0 + N]) for kc in range(KC)], [kf, hk], pk)
                    P.op("act", lambda e, ps=ps: e.activation(F_, ps[:, 0:NN], AF.Exp, scale=-1.0), R=[pk], W=[K_("F")])
                    yield
                    P.op("act", lambda e: e.activation(Pg, F_, AF.Ln, bias=self.one_ap), R=[K_("F")], W=[K_("Pg")])
                    yield
                    P.op("act", lambda e, hd=hd: e.activation(LF, F_, AF.Ln, bias=self.one_ap, scale=self.lb[:, hd:hd + 1]), R=[K_("F")], W=[K_("LF")])
                    yield
                    P.op("act", lambda e: e.activation(F_, Pg, AF.Exp, scale=-1.0), R=[K_("Pg")], W=[K_("F")])
                    P.op("dve", lambda e: e.tensor_tensor(LF, LF, Pg, ALU.subtract), R=[K_("LF"), K_("Pg")], W=[K_("LF")])
                    yield
                    P.op("dve", lambda e, hd=hd: e.tensor_scalar(kk, F_, self.noml[:, hd:hd + 1], self.oml[:, hd:hd + 1], ALU.mult, ALU.add),
                         R=[K_("F")], W=[K_("kk")])
                    yield
                    P.op("dve", lambda e: e.tensor_tensor_scan(Pg, self.rmask[:, 0:NN], LF, 0.0, ALU.mult, ALU.add), R=[K_("LF")], W=[K_("Pg")])
                    yield
                    if dr == 1:
                        P.op("dve", lambda e: e.tensor_copy(tot[:, 0:nch], v3(Pg)[:, :, 63]), R=[K_("Pg")], W=[K_("tot")])
                        P.op("dve", lambda e: e.tensor_tensor(Pg, LF, Pg, ALU.subtract), R=[K_("LF"), K_("Pg")], W=[K_("Pg")])
                        yield
                        P.op("dve", lambda e: e.tensor_tensor(v3(Pg), v3(Pg), tot[:, 0:nch].unsqueeze(2).broadcast_to([128, nch, 64]), ALU.add),
                             R=[K_("Pg"), K_("tot")], W=[K_("Pg")])
                        yield
                    P.op("dve", lambda e: e.tensor_tensor(v3(D1), v3(Pg), v3(Pg)[:, :, mi:mi + 1].broadcast_to([128, nch, 64]), ALU.subtract),
                         R=[K_("Pg")], W=[K_("F")])
                    P.op("act", lambda e: e.activation(E3, Pg, AF.Exp), R=[K_("Pg")], W=[K_("E3")])
                    yield
                    P.op("dve", lambda e: e.tensor_tensor(v3(D4), v3(Pg)[:, :, li:li + 1].broadcast_to([128, nch, 64]), v3(Pg), ALU.subtract),
                         R=[K_("Pg")], W=[K_("LF")])
                    P.op("act", lambda e: e.activation(E1, D1, AF.Exp), R=[K_("F")], W=[K_("E1")])
                    yield
                    P.op("dve", lambda e, qs=qs: e.tensor_tensor(QE, qs, E3, ALU.mult), R=["rope", K_("E3")], W=[K_("QE")])
                    P.op("act", lambda e: e.activation(E2, D1, AF.Exp, scale=-1.0), R=[K_("F")], W=[K_("E2")])
                    yield
                    P.op("dve", lambda e, qs=qs: e.tensor_tensor(QG, qs, E1, ALU.mult), R=["rope", K_("E1")], W=[K_("QG")])
                    P.op("act", lambda e: e.activation(E4, D4, AF.Exp), R=[K_("LF")], W=[K_("E1")])
                    yield
                    P.op("dve", lambda e: e.tensor_tensor(KG, kk, E2, ALU.mult), R=[K_("kk"), K_("E2")], W=[K_("KG")])
                    yield
                    P.op("dve", lambda e: e.tensor_tensor(KE, kk, E4, ALU.mult), R=[K_("kk"), K_("E1")], W=[K_("KE")])
                    yield
                    corder = list(range(nch)) if dr == 0 else list(range(nch - 1, -1, -1))
                    psA, pkA = self.ps_s()
                    for ci in range(nch):
                        cs = slice(ci * 64, (ci + 1) * 64)
                        self.mm(psA[r64, cs], KG[:, cs], QG[:, cs], True, True, [K_("KG"), K_("QG")], [pkA])
                    pst, pstk = self.ps_s()
                    pstb = pst[:, 0:256].bitcast(BF16)
                    for ci in range(nch):
                        cs = slice(ci * 64, (ci + 1) * 64)
                        P.op("pe", lambda e, pstb=pstb, cs=cs, ci=ci: e.transpose(pstb[r64, ci * 128:(ci + 1) * 128], KE[:, cs], self.ident),
                             R=[K_("KE")], W=[pstk])
                    yield
                    P.op("dve", lambda e, psA=psA: e.tensor_tensor(
                        AT4[r64, :, :], psA[r64, 0:256].rearrange("p (c t) -> p c t", t=64),
                        self.trimask[r64, dr, :].unsqueeze(1).broadcast_to([64, nch, 64]), ALU.mult), R=[pkA], W=[K_("AT")])
                    P.op("act", lambda e, pstb=pstb: e.activation(KE4[r64, :, :], pstb[r64, 0:512].rearrange("p (c t) -> p c t", t=128), AF.Copy),
                         R=[pstk], W=[K_("KEt")])
                    yield
                    pS, pSk = self.ps_a()
                    for ci in range(nch):
                        ch = t0 // 64 + ci
                        self.mm(pS[:, ci * 128:(ci + 1) * 128], KE4[r64, ci, :], Vtok[r64, ch, :], True, True, [K_("KEt"), "Vtok"], [pSk])
                    yield
                    O, Ok = self.ps_a()
                    for ci in corder:
                        ch = t0 // 64 + ci
                        cs = slice(ci * 64, (ci + 1) * 64)
                        self.mm(O[:, cs], Vtok[r64, ch, :], AT4[r64, ci, :], True, False, ["Vtok", K_("AT")], [Ok])
                        self.mm(O[:, cs], sb, QE[:, cs], False, True, [sbk, K_("QE")], [Ok])
                        di = ci * 64 + li
                        Sn, Snk = srot.next()
                        P.op("dve", lambda e, pS=pS, ci=ci, di=di, Sn=Sn, Sc=Sc: e.scalar_tensor_tensor(
                            Sn, Sc, E3[:, di:di + 1], pS[:, ci * 128:(ci + 1) * 128], ALU.mult, ALU.add),
                            R=[Sck, K_("E3"), pSk], W=[Snk])
                        Sc, Sck = Sn, Snk
                        sb, sbk = sbr.next()
                        P.op("act", lambda e, sb=sb, Sc=Sc: e.activation(sb, Sc, AF.Copy), R=[Sck], W=[sbk])
                        yield
                    P.op("dve", lambda e, O=O, t0=t0: e.tensor_tensor(oacc[:, t0:t0 + NN], O[:, 0:NN], oacc[:, t0:t0 + NN], ALU.add),
                         R=[Ok, ("oacc", ti)], W=[("oacc", ti)])
                    yield

            gens = [chain(0), chain(1)]
            alive = list(gens)
            while alive:
                for g in list(alive):
                    try:
                        next(g)
                    except StopIteration:
                        alive.remove(g)
            P.barrier()
            mem.reset(m1)
            gs_all = mem.alloc([T], BF16)
            sqr = Rot(mem, "sqb", 2, [512], BF16)
            sdr = Rot(mem, "sd", 2, [512], F32)
            tmr = Rot(mem, "tmp", 2, [512], F32)
            otr = Rot(mem, "oT", 2, [1, 512], BF16)
            for tt in tiles:
                t0, N = TT[tt]
                ps, pk = self.ps_s()
                self.mmk(ps[:, 0:N], [(wB[:, kc, 128:256], h[:, kc, t0:t0 + N]) for kc in range(KC)], [kB, ("h", tt)], pk)
                P.op("act", lambda e, ps=ps, t0=t0, N=N: e.activation(gs_all[:, t0:t0 + N], ps[:, 0:N], AF.Silu), R=[pk], W=[("gs", tt)])
            pend_o = None
            for tt in tiles:
                t0, N = TT[tt]
                sqb, sqk = sqr.next()
                sd, sdk = sdr.next()
                tmp, tmk = tmr.next()
                P.op("act", lambda e, sqb=sqb, t0=t0, N=N: e.activation(sqb[:, 0:N], oacc[:, t0:t0 + N], AF.Square), R=[], W=[sqk])
                ps2, pk2 = self.ps_s()
                self.mm(ps2[:, 0:N], self.ones_bf, sqb[:, 0:N], True, True, [sqk], [pk2])
                P.op("act", lambda e, sd=sd, ps2=ps2, N=N: e.activation(sd[:, 0:N], ps2[:, 0:N], AF.Ln, bias=self.eps_ap, scale=1.0 / 128), R=[pk2], W=[sdk])
                P.op("act", lambda e, sd=sd, N=N: e.activation(sd[:, 0:N], sd[:, 0:N], AF.Exp, scale=-0.5), R=[sdk], W=[sdk])
                P.op("dve", lambda e, tmp=tmp, sd=sd, t0=t0, N=N: e.tensor_tensor(tmp[:, 0:N], oacc[:, t0:t0 + N], sd[:, 0:N], ALU.mult), R=[sdk], W=[tmk])
                oT, ok = otr.next()
                P.op("dve", lambda e, oT=oT, tmp=tmp, t0=t0, N=N: e.scalar_tensor_tensor(
                    oT[:, 0, 0:N], tmp[:, 0:N], self.hnw[:, 0:1], gs_all[:, t0:t0 + N], ALU.mult, ALU.mult),
                    R=[tmk, ("gs", tt)], W=[ok])
                if pend_o is not None:
                    self.outproj_tile(*pend_o)
                pend_o = (s, l, tt, wO, kO, oT, ok, 1)
            self.outproj_tile(*pend_o)
            P.barrier()
            mem.reset(m1)
        self.mixer_end(m0)

    def mla(self, s, l):
        P, mem = self.P, self.mem
        d = self.dram
        tiles = [0, 1, 2, 3, 4]
        xt = [1, 2, 3, 4]
        P.barrier()
        m0 = mem.mark()
        cqT = mem.alloc([2, T], BF16)
        ckvT = mem.alloc([2, T], BF16)
        krT = mem.alloc([T], BF16)
        m1 = mem.mark()
        h = mem.alloc([KC, T], BF16)
        rots = (Rot(mem, "sq", 1, [KC, 512], BF16), Rot(mem, "sd", 2, [512], F32), Rot(mem, "ntmp", 3, [512], F32))
        self.prenorm(s, l, 1, tiles, h, rots)
        P.barrier()
        mem.reset(m1)
        h = mem.alloc([KC, T], BF16)
        P.dma("pool", self.rope, d["rope32"], W=["rope"])
        self.t1 = mem.alloc([512], F32)
        self.t2 = mem.alloc([512], F32)
        sq2 = mem.alloc([2, 512], BF16)
        sdr = Rot(mem, "sd", 2, [512], F32)
        wD1, kD1 = self.wload(d["mla_pD"][0].rearrange("(k p) n -> p k n", p=128), (KC, 256))
        wD2, kD2 = self.wload(d["mla_pD"][1].rearrange("(k p) n -> p k n", p=128), (KC, 256))
        wD3, kD3 = self.wload(d["mla_pD"][2].rearrange("(k p) n -> p k n", p=128), (KC, 256))
        r96 = slice(64, 96)
        for tt in tiles:
            t0, N = TT[tt]
            for (wD, kD, dstT, nwi) in ((wD1, kD1, cqT, 0), (wD2, kD2, ckvT, 1)):
                pcs = []
                for c in range(2):
                    ps, pk = self.ps_s()
                    self.mmk(ps[:, 0:N], [(wD[:, kc, c * 128:(c + 1) * 128], h[:, kc, t0:t0 + N]) for kc in range(KC)], [kD, ("h", tt)], pk)
                    P.op("act", lambda e, ps=ps, c=c, N=N: e.activation(sq2[:, c, 0:N], ps[:, 0:N], AF.Square), R=[pk], W=["sq2"])
                    pcs.append((ps, pk))
                pss, pssk = self.ps_s()
                self.mmk(pss[:, 0:N], [(self.ones_bf, sq2[:, c, 0:N]) for c in range(2)], ["sq2"], pssk)
                sd, sdk = sdr.next()
                P.op("act", lambda e, sd=sd, pss=pss, N=N: e.activation(sd[:, 0:N], pss[:, 0:N], AF.Ln, bias=self.eps_ap, scale=1.0 / 256), R=[pssk], W=[sdk])
                P.op("act", lambda e, sd=sd, N=N: e.activation(sd[:, 0:N], sd[:, 0:N], AF.Exp, scale=-0.5), R=[sdk], W=[sdk])
                for c in range(2):
                    ps, pk = pcs[c]
                    P.op("dve", lambda e, ps=ps, c=c, sd=sd, dstT=dstT, nwi=nwi, t0=t0, N=N: e.scalar_tensor_tensor(
                        dstT[:, c, t0:t0 + N], ps[:, 0:N], self.mla_nw[:, nwi, c:c + 1], sd[:, 0:N], ALU.mult, ALU.mult),
                        R=[pk, sdk], W=[("lora", nwi, tt)])
            psA, pkA = self.ps_s()
            self.mmk(psA[:, 0:N], [(wD3[:, kc, 0:128], h[:, kc, t0:t0 + N]) for kc in range(KC)], [kD3, ("h", tt)], pkA)
            if tt == 0:
                P.op("act", lambda e, psA=psA, t0=t0, N=N: e.activation(krT[r96, t0:t0 + N], psA[r96, 0:N], AF.Copy), R=[pkA], W=[("krT", tt)])
            else:
                psB, pkB = self.ps_s()
                self.mmk(psB[:, 0:N], [(wD3[:, kc, 128:256], h[:, kc, t0:t0 + N]) for kc in range(KC)], [kD3, ("h", tt)], pkB)
                self.rope_evac(psA, pkA, psB, pkB, krT[r96, t0:t0 + N], N, t0 - CT, [("krT", tt)], rows=r96)
        P.barrier()
        mem.reset(m1)
        self.t1 = mem.alloc([512], F32)
        self.t2 = mem.alloc([512], F32)
        kh = [mem.alloc([T], BF16), mem.alloc([T], BF16)]
        vaug = [mem.alloc([18, 128], BF16), mem.alloc([18, 128], BF16)]
        qr = Rot(mem, "qT", 2, [512], BF16)
        ptr = Rot(mem, "PT", 3, [1024], BF16)
        dtr = Rot(mem, "dt", 2, [512], F32)
        otr = Rot(mem, "oT", 2, [1, 512], BF16)
        scale = 96.0 ** -0.5
        for i in range(2):
            P.op("dve", lambda e, i=i: e.memset(kh[i], 0.0), W=[("kh", i)])
            P.op("dve", lambda e, i=i: e.memset(vaug[i], 1.0), W=[("vaug", i)])
        for q in qr.bufs:
            P.op("dve", lambda e, q=q: e.memset(q, 0.0), W=[])
        P.barrier()
        for pr in range(8):
            wO, kO = self.wload(d["mla_w_out"][pr * 128:(pr + 1) * 128, :].rearrange("(k p) n -> p k n", p=128), (1, D))
            wq_, wkv_ = [], []
            for hh in range(2):
                hd = 2 * pr + hh
                wq_.append(self.wload(d["mla_pUQ"][hd].rearrange("(k p) n -> p k n", p=128), (2, 256)))
                wkv_.append(self.wload(d["mla_pUKV"][hd].rearrange("(k p) n -> p k n", p=128), (2, 128)))
            for hh in range(2):
                wkv, kkv = wkv_[hh]
                vcols = slice(0, 64) if hh == 0 else slice(64, 128)
                for tt in tiles:
                    t0, N = TT[tt]
                    ps, pk = self.ps_s()
                    self.mmk(ps[0:64, 0:N], [(wkv[:, kc, 0:64], ckvT[:, kc, t0:t0 + N]) for kc in range(2)], [kkv, ("lora", 1, tt)], pk)
                    P.op("act", lambda e, ps=ps, hh=hh, t0=t0, N=N: e.activation(kh[hh][0:64, t0:t0 + N], ps[0:64, 0:N], AF.Copy), R=[pk], W=[("kh", hh)])
                P.op("act", lambda e, hh=hh: e.activation(kh[hh][r96, :], krT[r96, :], AF.Copy), R=[("krT", tt) for tt in tiles], W=[("kh", hh)])
                for g8 in range(0, 18, 8):
                    nb = min(8, 18 - g8)
                    ps, pk = self.ps_s()
                    for bi in range(nb):
                        tb = g8 + bi
                        self.mmk(ps[:, bi * 64:(bi + 1) * 64], [(ckvT[:, kc, tb * 128:(tb + 1) * 128], wkv[:, kc, 64:128]) for kc in range(2)],
                                 [kkv, ("lora", 1, self.tile_of_block(tb))], pk)
                    src = ps[:, 0:nb * 64].rearrange("p (a b) -> p a b", b=64)
                    P.op("act", lambda e, src=src, g8=g8, nb=nb, hh=hh, vcols=vcols: e.activation(vaug[hh][:, g8:g8 + nb, vcols], src, AF.Copy),
                         R=[pk], W=[("vaug", hh)])
            pend_o = None

            def make_q(tt, hh, wq_=wq_):
                t0, N = TT[tt]
                wq, kq = wq_[hh]
                qT, qk = qr.next()
                psA, pkA = self.ps_s()
                self.mmk(psA[0:96, 0:N], [(wq[:, kc, 0:96], cqT[:, kc, t0:t0 + N]) for kc in range(2)], [kq, ("lora", 0, tt)], pkA)
                psB, pkB = self.ps_s()
                self.mmk(psB[0:96, 0:N], [(wq[:, kc, 128:224], cqT[:, kc, t0:t0 + N]) for kc in range(2)], [kq, ("lora", 0, tt)], pkB)
                P.op("act", lambda e, qT=qT, psA=psA, N=N: e.activation(qT[0:64, 0:N], psA[0:64, 0:N], AF.Copy), R=[pkA], W=[qk])
                self.rope_evac(psA, pkA, psB, pkB, qT[r96, 0:N], N, t0 - CT, [qk], rows=r96)
                return qT, qk

            items = [(tt, hh) for tt in xt for hh in range(2)]
            qnext = make_q(*items[0])
            for tt in xt:
                t0, N = TT[tt]
                oT, ok = otr.next()
                for hh in range(2):
                    orow, drow = (slice(0, 64), slice(64, 128)) if hh == 0 else (slice(64, 128), slice(0, 64))
                    qT, qk = qnext
                    ii = items.index((tt, hh))
                    if ii + 1 < len(items):
                        qnext = make_q(*items[ii + 1])
                    O, Ok = self.ps_a()
                    def issue_pair(kb0, hh=hh, qT=qT, qk=qk, N=N):
                        dps, dk2 = self.ps_s2()
                        for hf in range(2):
                            kb = kb0 + hf
                            self.mm(dps[:, hf * 512:hf * 512 + N], kh[hh][:, kb * 128:(kb + 1) * 128], qT[:, 0:N], True, True, [("kh", hh), qk], [dk2[hf]])
                        PT, ptk = ptr.next()
                        P.op("act", lambda e, PT=PT, dps=dps, N=N: e.activation(PT[:, 0:1024], dps[:, 0:1024], AF.Exp, scale=scale),
                             R=[dk2[0], dk2[1]], W=[ptk])
                        return PT, ptk
                    pend = [issue_pair(0)]
                    for pi in range(9):
                        PT, ptk = pend[pi]
                        if pi + 1 < 9:
                            pend.append(issue_pair(2 * (pi + 1)))
                        for hf in range(2):
                            kb = 2 * pi + hf
                            self.mm(O[:, 0:N], vaug[hh][:, kb, :], PT[:, hf * 512:hf * 512 + N], kb == 0, kb == 17, [ptk, ("vaug", hh)], [Ok])
                    dt, dk = dtr.next()
                    P.op("act", lambda e, dt=dt, O=O, orow=orow, drow=drow, N=N: e.activation(dt[orow, 0:N], O[drow, 0:N], AF.Copy), R=[Ok], W=[dk])
                    P.op("dve", lambda e, dt=dt, orow=orow, N=N: e.reciprocal(dt[orow, 0:N], dt[orow, 0:N]), R=[dk], W=[dk])
                    P.op("dve", lambda e, dt=dt, O=O, orow=orow, oT=oT, N=N: e.tensor_tensor(oT[orow, 0, 0:N], O[orow, 0:N], dt[orow, 0:N], ALU.mult),
                         R=[Ok, dk], W=[ok])
                if pend_o is not None:
                    self.outproj_tile(*pend_o)
                pend_o = (s, l, tt, wO, kO, oT, ok, 1)
            self.outproj_tile(*pend_o)
        self.mixer_end(m0)


def host_inputs(inputs, core):
    b0 = 2 * core
    x = inputs["x"]
    ctx = inputs["ctx"]
    hT = np.empty((2, D, T), np.float32)
    for i in range(2):
        hT[i, :, :CT] = ctx[b0 + i].T
        hT[i, :, CT:] = x[b0 + i].T
    cvec = np.stack([inputs["c"][b0], inputs["c"][b0 + 1], inputs["c_ctx"]], axis=1)
    cT = np.ascontiguousarray(cvec.reshape(KC, 128, 3).transpose(1, 0, 2))
    m = {"hT": hT, "cT": cT}
    return m


def rope_tables(rot_dim):
    rows = S // 64
    row = np.repeat(np.arange(rows, dtype=np.float32), 64)
    col = np.tile(np.arange(64, dtype=np.float32), rows)
    axis_dim = rot_dim // 2
    inv_freq = (np.float32(10000.0) ** (-np.arange(0, axis_dim, 2, dtype=np.float32) / np.float32(axis_dim))).astype(np.float32)
    ang_r = row[:, None] * inv_freq[None, :]
    ang_c = col[:, None] * inv_freq[None, :]
    ang = np.concatenate([ang_r, ang_r, ang_c, ang_c], axis=-1).astype(np.float32)
    f = rot_dim // 4
    sgn = np.concatenate([-np.ones(f), np.ones(f), -np.ones(f), np.ones(f)]).astype(np.float32)
    perm = np.concatenate([np.arange(f, 2 * f), np.arange(0, f), np.arange(3 * f, 4 * f), np.arange(2 * f, 3 * f)])
    return np.cos(ang).astype(np.float32), (np.sin(ang) * sgn[None, :]).astype(np.float32), perm


def host_consts():
    c = {}
    cos, sins, perm64 = rope_tables(64)
    r = np.zeros((128, 2, S), np.float32)
    r[:, 0, :] = np.tile(cos.T, (2, 1))
    r[:, 1, :] = np.tile(sins.T, (2, 1))
    c["rope64"] = r
    cos, sins, perm32 = rope_tables(32)
    r = np.zeros((128, 2, S), np.float32)
    r[64:96, 0, :] = cos.T
    r[64:96, 1, :] = sins.T
    c["rope32"] = r
    c["ident"] = np.eye(128, dtype=np.float32)
    NEG = -30000.0
    kk = np.arange(128)[:, None]
    qq = np.arange(128)[None, :]
    gm = np.zeros((128, 2, 512), np.float32)
    gm[:, 0, :] = np.tile(np.where(qq <= kk, 0.0, NEG), (1, 4))
    gm[:, 1, :] = np.tile(np.where(kk <= qq, 0.0, NEG), (1, 4))
    c["gmask"] = gm
    return c, perm64, perm32


def host_shared(inputs):
    sh, perm64, perm32 = host_consts()
    w_in = inputs["gqa_w_in"][0]
    hp = (np.arange(16)[:, None] * 64 + perm64[None, :]).reshape(-1)
    wq, wqp = w_in[:, :1024], w_in[:, :1024][:, hp]
    pA = np.empty((4, 2, D, 256), np.float32)
    pB = np.empty((4, D, 256), np.float32)
    pV = np.empty((4, D, 64), np.float32)
    for j in range(4):
        pA[j, 0] = wq[:, j * 256:(j + 1) * 256]
        pA[j, 1] = wqp[:, j * 256:(j + 1) * 256]
        kj = w_in[:, 1024 + j * 64:1024 + (j + 1) * 64]
        kjp = kj[:, perm64]
        pB[j] = np.concatenate([kj, kj, kjp, kjp], axis=1)
        pV[j] = w_in[:, 1280 + j * 64:1280 + (j + 1) * 64]
    sh["gqa_pA"], sh["gqa_pB"], sh["gqa_pV"] = pA, pB, pV
    sh["gqa_w_out"] = np.ascontiguousarray(inputs["gqa_w_out"][0])
    w_in = inputs["diff_w_in"][0]
    sp = (np.arange(16)[:, None] * 64 + perm64[None, :]).reshape(-1)
    wq, wk, wv = w_in[:, :1024], w_in[:, 1024:2048], w_in[:, 2048:]
    wqp, wkp = wq[:, sp], wk[:, sp]
    sh["diff_pQ"] = np.stack([np.concatenate([wq[:, i * 128:(i + 1) * 128], wqp[:, i * 128:(i + 1) * 128]], axis=1) for i in range(8)])
    sh["diff_pK"] = np.stack([np.concatenate([wk[:, i * 128:(i + 1) * 128], wkp[:, i * 128:(i + 1) * 128]], axis=1) for i in range(8)])
    sh["diff_pV"] = np.stack([wv[:, i * 128:(i + 1) * 128] for i in range(8)])
    sh["diff_w_out"] = np.ascontiguousarray(inputs["diff_w_out"][0])
    sh["diff_lam_b"] = np.ascontiguousarray(np.broadcast_to(inputs["diff_lambda"][0][None], (128, 4, 64)))
    sh["diff_subln_b"] = np.ascontiguousarray(inputs["diff_subln_w"][0].reshape(128, 1))
    wd = inputs["mla_w_down"][0]
    pD = np.zeros((3, D, 256), np.float32)
    pD[0] = wd[:, 0:256]
    pD[1] = wd[:, 256:512]
    pD[2][:, 64:96] = wd[:, 512:544]
    pD[2][:, 128 + 64:128 + 96] = wd[:, 512:544][:, perm32]
    sh["mla_pD"] = pD
    wuq = inputs["mla_w_uq"][0]
    pUQ = np.zeros((16, 256, 256), np.float32)
    for hd in range(16):
        blk = wuq[:, hd * 96:(hd + 1) * 96]
        pUQ[hd][:, 0:96] = blk
        pUQ[hd][:, 128:128 + 64] = blk[:, 0:64]
        pUQ[hd][:, 128 + 64:128 + 96] = blk[:, 64:96][:, perm32]
    sh["mla_pUQ"] = pUQ
    sh["mla_pUKV"] = np.ascontiguousarray(inputs["mla_w_ukv"][0].reshape(256, 16, 128).transpose(1, 0, 2))
    sh["mla_w_out"] = np.ascontiguousarray(inputs["mla_w_out"][0])
    nw = np.stack([inputs["mla_q_norm_w"][0].reshape(2, 128).T, inputs["mla_kv_norm_w"][0].reshape(2, 128).T], axis=1)
    sh["mla_nw"] = np.ascontiguousarray(nw)
    w_in = inputs["hgrn_w_in"][0]
    cq, cf, cb, cv, cg = (w_in[:, i * 1024:(i + 1) * 1024] for i in range(5))
    sh["hgrn_pA"] = np.stack([np.concatenate([cq[:, i * 128:(i + 1) * 128], cf[:, i * 128:(i + 1) * 128]], axis=1) for i in range(8)])
    sh["hgrn_pB"] = np.stack([np.concatenate([cb[:, i * 128:(i + 1) * 128], cg[:, i * 128:(i + 1) * 128]], axis=1) for i in range(8)])
    sh["hgrn_pV"] = np.stack([cv[:, i * 128:(i + 1) * 128] for i in range(8)])
    sh["hgrn_w_out"] = np.ascontiguousarray(inputs["hgrn_w_out"][0])
    sh["hgrn_lbT"] = np.ascontiguousarray(inputs["hgrn_lower_bounds"].reshape(4, 8, 128).transpose(2, 0, 1))
    sh["hgrn_nw"] = np.ascontiguousarray(inputs["hgrn_norm_w"][0].reshape(128, 1))
    rm = np.ones((128, 512), np.float32)
    rm[:, ::64] = 0.0
    sh["rmask"] = rm
    ss_, tt_ = np.arange(64)[:, None], np.arange(64)[None, :]
    tm = np.zeros((128, 2, 64), np.float32)
    tm[0:64, 0, :] = (ss_ <= tt_)
    tm[0:64, 1, :] = (ss_ >= tt_)
    sh["trimask"] = tm
    sh["gqa_sinks_b"] = np.ascontiguousarray(np.broadcast_to(inputs["gqa_sinks"][0][None, :], (128, 16)))
    sh["ada_w"] = np.ascontiguousarray(inputs["ada_w"])
    sh["ada_bT"] = np.ascontiguousarray(inputs["ada_b"].reshape(NL, 72, 128).transpose(2, 0, 1))
    sh["norm_wT"] = np.ascontiguousarray(inputs["norm_w"].reshape(NL * 3, KC, 128).transpose(2, 0, 1))
    sh["fnorm_wT"] = np.ascontiguousarray(inputs["final_norm_w"].reshape(KC, 128).T)
    for nm, src in (("ffn_wg_l", "ffn_w_gate"), ("ffn_wu_l", "ffn_w_up")):
        w = inputs[src].reshape(NL, 2, KC, 128, 11, 256)
        sh[nm] = np.ascontiguousarray(w.transpose(0, 1, 4, 3, 2, 5)).reshape(NL, 2, 11, 128, KC * 256)
    sh["ffn_w_down"] = np.ascontiguousarray(inputs["ffn_w_down"])
    return sh


def kernel(**inputs):
    inputs = {k: np.asarray(v) for k, v in inputs.items()}
    b = Builder()
    nc = b.build()
    sh = host_shared(inputs)
    in_maps = []
    for core in range(NCORES):
        m = dict(sh)
        m.update(host_inputs(inputs, core))
        in_maps.append({k: v for k, v in m.items() if k in b.dram})
    res = run_bass_kernel_spmd(nc, in_maps, core_ids=list(range(NCORES)))
    out = np.empty((16, S, D), np.float32)
    for core in range(NCORES):
        o = res.results[core]["outT"]
        for i in range(2):
            out[2 * core + i] = o[i].T
    return out
```
